# Optimizing a Trainium2 kernel written in Bass

```python
import math
import jax, jax.numpy as jnp
from jax import lax
import numpy as np

D_MODEL = 1024
BATCH = 4
SEQ = 8192
DEPTH = 2

CHUNK = 64
Q_BLOCK = 128
SB_HEADS = 8
SB_HEAD_DIM = 64
SB_WIDTH = SB_HEADS * SB_HEAD_DIM
SSM_WIDTH = D_MODEL // 2
SSM_GROUP = 16
SSM_GROUPS = SSM_WIDTH // SSM_GROUP
SSM_STATE = 64
DT_MIN = 1e-3
DT_MAX = 1e-1
FFN_HIDDEN = ((8 * D_MODEL // 3 + 255) // 256) * 256
IN_SPLITS = (SB_WIDTH, 2 * SB_WIDTH, 3 * SB_WIDTH, 3 * SB_WIDTH + SSM_WIDTH,
             3 * SB_WIDTH + SSM_WIDTH + D_MODEL)
IN_COLS = 3 * SB_WIDTH + SSM_WIDTH + 2 * D_MODEL
N_MOD = 6
DEEPNORM_ALPHA = (2 * DEPTH) ** 0.25
DEEPNORM_BETA = (8 * DEPTH) ** -0.25
LN_EPS = 1e-5

kernel_name = "hybrid_sb_s5_deepnorm_adaln"


def _normalize(x):
    xf = x.astype(jnp.float32)
    mu = jnp.mean(xf, axis=-1, keepdims=True)
    var = jnp.mean(jnp.square(xf - mu), axis=-1, keepdims=True)
    return ((xf - mu) * lax.rsqrt(var + LN_EPS)).astype(x.dtype)


def _layer_norm(x, g, b):
    return _normalize(x) * g + b


def stick_breaking_attention(q, k, v):
    b, s, h, dh = q.shape
    nb = s // Q_BLOCK
    f32 = jnp.float32
    qb = q.astype(f32).reshape(b, nb, Q_BLOCK, h, dh).transpose(1, 0, 3, 2, 4)
    kt = k.astype(f32).transpose(0, 2, 1, 3)
    vt = v.astype(f32).transpose(0, 2, 1, 3)
    key_pos = jnp.arange(s, dtype=jnp.int32)
    scale = 1.0 / math.sqrt(dh)

    def one_block(args):
        q_blk, blk = args
        q_pos = blk * Q_BLOCK + jnp.arange(Q_BLOCK, dtype=jnp.int32)
        z = jnp.einsum('bhqd,bhkd->bhqk', q_blk, kt) * scale
        causal = key_pos[None, :] < q_pos[:, None]
        log_beta = jax.nn.log_sigmoid(z)
        log_one_minus = jnp.where(causal, log_beta - z, 0.0)
        after = lax.cumsum(log_one_minus, axis=3, reverse=True) - log_one_minus
        w = jnp.where(causal, jnp.exp(log_beta + after), 0.0)
        return jnp.einsum('bhqk,bhkd->bhqd', w, vt)

    out = lax.map(one_block, (qb, jnp.arange(nb, dtype=jnp.int32)))
    return out.transpose(1, 0, 3, 2, 4).reshape(b, s, h * dh).astype(v.dtype)


def s5_branch(u, a_re, a_im, log_dt, b_re, b_im, c_re, c_im, d_skip, w_glu, b_glu):
    f32 = jnp.float32
    c64 = jnp.complex64
    bsz, s, _ = u.shape
    n_chunks = s // CHUNK
    uf = u.astype(f32)
    lam = lax.complex(a_re.astype(f32), a_im.astype(f32))
    dt = jnp.exp(log_dt.astype(f32))[:, None]
    lam_dt = lam * dt
    lam_bar = jnp.exp(lam_dt)
    b_mat = lax.complex(b_re.astype(f32), b_im.astype(f32))
    b_bar = ((lam_bar - 1.0) / lam)[..., None] * b_mat
    c_mat = lax.complex(c_re.astype(f32), c_im.astype(f32))
    steps = jnp.arange(1, CHUNK + 1, dtype=f32)
    powers = jnp.exp(lam_dt[None] * steps[:, None, None].astype(c64))
    a_seq = jnp.broadcast_to(lam_bar[None, None], (CHUNK, bsz, SSM_GROUPS, SSM_STATE))

    u_chunks = uf.reshape(bsz, n_chunks, CHUNK, SSM_GROUPS, SSM_GROUP).transpose(1, 2, 0, 3, 4)

    def combine(left, right):
        a_l, b_l = left
        a_r, b_r = right
        return a_r * a_l, a_r * b_l + b_r

    def step(state, u_c):
        bu = jnp.einsum('gpc,lbgc->lbgp', b_bar, u_c.astype(c64))
        _, h_loc = lax.associative_scan(combine, (a_seq, bu), axis=0)
        h = h_loc + powers[:, None] * state[None]
        y = jnp.einsum('gcp,lbgp->lbgc', c_mat, h).real
        return h[-1], y

    state0 = jnp.zeros((bsz, SSM_GROUPS, SSM_STATE), c64)
    _, ys = lax.scan(step, state0, u_chunks)
    y = ys.transpose(2, 0, 1, 3, 4).reshape(bsz, s, SSM_WIDTH)
    y = y + d_skip.astype(f32) * uf
    y = jax.nn.gelu(y)
    y = y * jax.nn.sigmoid(y @ w_glu.astype(f32) + b_glu.astype(f32))
    return y.astype(u.dtype)


def token_mixer(h, w_in, w_sb_up, a_re, a_im, log_dt, b_re, b_im, c_re, c_im,
                d_skip, w_glu, b_glu, w_ssm_up, w_out):
    bsz, s, _ = h.shape
    proj = h @ w_in
    q, k, v, u, g_sb, g_ssm = jnp.split(proj, IN_SPLITS, axis=-1)
    shp = (bsz, s, SB_HEADS, SB_HEAD_DIM)
    y_sb = stick_breaking_attention(q.reshape(shp), k.reshape(shp), v.reshape(shp)) @ w_sb_up
    y_ssm = s5_branch(u, a_re, a_im, log_dt, b_re, b_im, c_re, c_im,
                      d_skip, w_glu, b_glu) @ w_ssm_up
    merged = jax.nn.sigmoid(g_sb) * y_sb + jax.nn.sigmoid(g_ssm) * y_ssm
    return merged @ w_out


def swiglu_ffn(h, w_ffn_in, w_ffn_out):
    gate, up = jnp.split(h @ w_ffn_in, 2, axis=-1)
    return (jax.nn.silu(gate) * up) @ w_ffn_out


def setup_inputs(seed: int = 0) -> dict:
    key = jax.random.key(seed)
    ks = jax.random.split(key, 32)
    f32 = jnp.float32

    def nrm(k, shape, scale):
        return jax.random.normal(k, shape, f32) * scale

    G, P, Cg = SSM_GROUPS, SSM_STATE, SSM_GROUP
    n = jnp.arange(P, dtype=f32)
    return {
        "x": nrm(ks[0], (BATCH, SEQ, D_MODEL), 1.0),
        "c": nrm(ks[1], (BATCH, D_MODEL), 1.0),
        "w_ada": nrm(ks[2], (DEPTH, D_MODEL, N_MOD * D_MODEL), 0.5 * D_MODEL ** -0.5),
        "b_ada": nrm(ks[3], (DEPTH, N_MOD * D_MODEL), 0.02),
        "w_in": nrm(ks[4], (DEPTH, D_MODEL, IN_COLS), D_MODEL ** -0.5),
        "w_sb_up": nrm(ks[5], (DEPTH, SB_WIDTH, D_MODEL), SB_WIDTH ** -0.5),
        "ssm_a_re": -0.5 + nrm(ks[6], (DEPTH, G, P), 0.01),
        "ssm_a_im": math.pi * n + nrm(ks[7], (DEPTH, G, P), 0.01),
        "ssm_log_dt": jax.random.uniform(ks[8], (DEPTH, G), f32,
                                         math.log(DT_MIN), math.log(DT_MAX)),
        "ssm_b_re": nrm(ks[9], (DEPTH, G, P, Cg), (2 * Cg) ** -0.5),
        "ssm_b_im": nrm(ks[10], (DEPTH, G, P, Cg), (2 * Cg) ** -0.5),
        "ssm_c_re": nrm(ks[11], (DEPTH, G, Cg, P), P ** -0.5),
        "ssm_c_im": nrm(ks[12], (DEPTH, G, Cg, P), P ** -0.5),
        "ssm_d": 1.0 + nrm(ks[13], (DEPTH, SSM_WIDTH), 0.1),
        "w_glu": nrm(ks[14], (DEPTH, SSM_WIDTH, SSM_WIDTH), SSM_WIDTH ** -0.5),
        "b_glu": nrm(ks[15], (DEPTH, SSM_WIDTH), 0.02),
        "w_ssm_up": nrm(ks[16], (DEPTH, SSM_WIDTH, D_MODEL), SSM_WIDTH ** -0.5),
        "w_out": nrm(ks[17], (DEPTH, D_MODEL, D_MODEL), D_MODEL ** -0.5 * DEEPNORM_BETA),
        "ln1_g": 1.0 + nrm(ks[18], (DEPTH, D_MODEL), 0.02),
        "ln1_b": nrm(ks[19], (DEPTH, D_MODEL), 0.02),
        "w_ffn_in": nrm(ks[20], (DEPTH, D_MODEL, 2 * FFN_HIDDEN), D_MODEL ** -0.5),
        "w_ffn_out": nrm(ks[21], (DEPTH, FFN_HIDDEN, D_MODEL), FFN_HIDDEN ** -0.5 * DEEPNORM_BETA),
        "ln2_g": 1.0 + nrm(ks[22], (DEPTH, D_MODEL), 0.02),
        "ln2_b": nrm(ks[23], (DEPTH, D_MODEL), 0.02),
    }


def reference(x, c, w_ada, b_ada, w_in, w_sb_up, ssm_a_re, ssm_a_im, ssm_log_dt,
              ssm_b_re, ssm_b_im, ssm_c_re, ssm_c_im, ssm_d, w_glu, b_glu,
              w_ssm_up, w_out, ln1_g, ln1_b, w_ffn_in, w_ffn_out, ln2_g, ln2_b):
    c_act = jax.nn.silu(c)
    for l in range(DEPTH):
        mod = c_act @ w_ada[l] + b_ada[l]
        sh_m, sc_m, g_m, sh_f, sc_f, g_f = [m[:, None, :] for m in jnp.split(mod, N_MOD, axis=-1)]
        h = _normalize(x) * (1.0 + sc_m) + sh_m
        y = token_mixer(h, w_in[l], w_sb_up[l], ssm_a_re[l], ssm_a_im[l], ssm_log_dt[l],
                        ssm_b_re[l], ssm_b_im[l], ssm_c_re[l], ssm_c_im[l], ssm_d[l],
                        w_glu[l], b_glu[l], w_ssm_up[l], w_out[l])
        x = _layer_norm(DEEPNORM_ALPHA * x + (1.0 + g_m) * y, ln1_g[l], ln1_b[l])
        h = _normalize(x) * (1.0 + sc_f) + sh_f
        y = swiglu_ffn(h, w_ffn_in[l], w_ffn_out[l])
        x = _layer_norm(DEEPNORM_ALPHA * x + (1.0 + g_f) * y, ln2_g[l], ln2_b[l])
    return x
```

```python
import numpy as np
import concourse.bass as bass
import concourse.mybir as mybir

F32 = mybir.dt.float32
BF16 = mybir.dt.bfloat16
AF = mybir.ActivationFunctionType
ALU = mybir.AluOpType
AX = mybir.AxisListType


class Buf:
    __slots__ = ("name", "w", "r")

    def __init__(self, name):
        self.name = name
        self.w = None
        self.r = {}


class Sched:
    ENG = ("pe", "act", "dve", "pool", "sp")

    def __init__(self, nc, stack):
        self.nc = nc
        self.stack = stack
        self.ops = {e: [] for e in self.ENG}
        self.cnt = {e: 0 for e in self.ENG}
        self.sem = {e: stack.enter_context(nc.semaphore("s_" + e)) for e in self.ENG}
        self.seen = {e: {} for e in self.ENG}
        self.dsems = {}
        self.keymap = {}
        self.nbuf = 0

    def buf(self, name=None):
        self.nbuf += 1
        return Buf(name or ("b%d" % self.nbuf))

    def _waits(self, eng, reads, writes, is_dma=False, dkey=None):
        deps = []
        for b in reads:
            if b.w is not None:
                deps.append((b.w, "raw"))
        for b in writes:
            if b.w is not None:
                deps.append((b.w, "waw"))
            for ev in b.r.values():
                deps.append((ev, "war"))
        waits = {}
        for (ev, kind) in deps:
            sem, val, src = ev
            if not is_dma and src == eng:
                if eng == "pe":
                    continue
                if kind in ("war", "waw"):
                    continue
            if is_dma and kind == "waw" and src == ("dma", dkey):
                continue
            key = id(sem)
            if self.seen[eng].get(key, 0) >= val:
                continue
            if key not in waits or waits[key][1] < val:
                waits[key] = (sem, val)
        for key, (sem, val) in waits.items():
            self.seen[eng][key] = val
        return list(waits.values())

    def _update(self, ev, reads, writes):
        for b in writes:
            b.w = ev
            b.r = {}
        for b in reads:
            if b in writes:
                continue
            k = id(ev[0])
            old = b.r.get(k)
            if old is None or old[1] < ev[1]:
                b.r[k] = ev

    def op(self, eng, emit, reads=(), writes=()):
        waits = self._waits(eng, reads, writes)
        self.cnt[eng] += 1
        ev = (self.sem[eng], self.cnt[eng], eng)
        self.ops[eng].append((waits, emit, (self.sem[eng], 1)))
        self._update(ev, reads, writes)
        return ev

    def new_phase(self):
        self.keymap = {}

    def dsem(self, key):
        if key not in self.dsems:
            self.dsems[key] = [self.stack.enter_context(self.nc.semaphore("d_%d" % len(self.dsems))), 0]
        return self.dsems[key]

    def dma(self, q, out_ap, in_ap, reads=(), writes=(), key=None, **kw):
        assert key is not None
        key = self.keymap.setdefault(key, "k%d" % len(self.keymap))
        waits = self._waits(q, reads, writes, is_dma=True, dkey=key)
        ds = self.dsem(key)
        ds[1] += 16
        ev = (ds[0], ds[1], ("dma", key))
        if callable(in_ap):
            sem_ = ds[0]

            def fn(e):
                rk = self.rank(e)
                with e.If(rk == 0):
                    e.dma_start(out=out_ap, in_=in_ap(0), **kw).then_inc(sem_, 16)
                with e.Else():
                    e.dma_start(out=out_ap, in_=in_ap(1), **kw).then_inc(sem_, 16)
                return None
            self.ops[q].append((waits, fn, (ds[0], 16)))
        else:
            self.ops[q].append((waits, (lambda e: e.dma_start(out=out_ap, in_=in_ap, **kw)), (ds[0], 16)))
        self._update(ev, reads, writes)
        return ev

    def coll(self, kind, in_ap, out_ap, groups, reads=(), writes=(), key=None):
        key = "cc_" + key
        waits = self._waits("pool", reads, writes, is_dma=True, dkey=key)
        ds = self.dsem(key)
        ds[1] += 1
        ev = (ds[0], ds[1], ("dma", key))
        self.ops["pool"].append((waits, (lambda e: e.collective_compute(kind, ALU.bypass, replica_groups=groups,
                                                                         ins=[in_ap.opt()], outs=[out_ap.opt()])), (ds[0], 1)))
        self._update(ev, reads, writes)
        return ev

    def wait_events(self, eng, evs):
        waits = {}
        for ev in evs:
            sem, val, src = ev
            key = id(sem)
            if self.seen[eng].get(key, 0) >= val:
                continue
            if key not in waits or waits[key][1] < val:
                waits[key] = (sem, val)
        for key, (sem, val) in waits.items():
            self.seen[eng][key] = val
        self.ops[eng].append((list(waits.values()), None, None))

    def rank(self, e):
        k = id(e)
        if k not in self._rank_cache:
            self._rank_cache[k] = e.partition_id() % 2
        return self._rank_cache[k]

    def _emit_eng(self, name, e):
        self._rank_cache = {}
        for waits, fn, inc in self.ops[name]:
            for sem, val in waits:
                e.wait_ge(sem, val)
            if fn is None:
                continue
            ins = fn(e)
            if ins is not None:
                ins.then_inc(inc[0], inc[1])

    def emit(self):
        with self.nc.Block() as block:
            @block.tensor
            def _(e):
                self._emit_eng("pe", e)

            @block.scalar
            def _(e):
                self._emit_eng("act", e)

            @block.vector
            def _(e):
                self._emit_eng("dve", e)

            @block.gpsimd
            def _(e):
                self._emit_eng("pool", e)

            @block.sync
            def _(e):
                self._emit_eng("sp", e)
        self.ops = {e: [] for e in self.ENG}


from contextlib import ExitStack
from concourse.bass_utils import run_bass_kernel_spmd
import ml_dtypes

NPBF16 = ml_dtypes.bfloat16
NCORES = 8
S_LEN = 8192
TOK = 4096
D = 1024
LN_EPS = 1e-5
ALPHA = (2 * 2) ** 0.25


class Rot:
    def __init__(self, items):
        self.items = items
        self.i = 0

    def next(self):
        it = self.items[self.i % len(self.items)]
        self.i += 1
        return it


class Prog:
    def __init__(self):
        self.nc = bass.Bass("TRN2", target_bir_lowering=False)
        self.gst = ExitStack()
        self.S = Sched(self.nc, self.gst)
        self.nphase = 0
        self.ext = {}

    def din(self, name, shape, dt=F32):
        if name not in self.ext:
            self.ext[name] = self.nc.dram_tensor(name, list(shape), dt, kind="ExternalInput").ap()
        return self.ext[name]

    def dout(self, name, shape, dt=F32):
        return self.nc.dram_tensor(name, list(shape), dt, kind="ExternalOutput").ap()

    def scratch(self, name, shape, dt=F32):
        t = self.nc.dram_tensor(name, list(shape), dt, kind="Internal").ap()
        return t, self.S.buf(name)


class Ctx:
    def __init__(self, prog=None):
        self.prog = prog
        if prog is None:
            self.nc = bass.Bass("TRN2", target_bir_lowering=False)
            self.st = ExitStack()
            self.S = Sched(self.nc, self.st)
            self.tag = ""
        else:
            self.nc = prog.nc
            self.st = ExitStack()
            self.S = prog.S
            self.S.new_phase()
            prog.nphase += 1
            self.tag = "f%d_" % prog.nphase
        self.outs = []
        self.n = 0

    def din(self, name, shape, dt=F32):
        return self.nc.dram_tensor(name, list(shape), dt, kind="ExternalInput").ap()

    def dout(self, name, shape, dt=F32):
        return self.nc.dram_tensor(name, list(shape), dt, kind="ExternalOutput").ap()

    def sb(self, shape, dt=F32, name=None):
        self.n += 1
        name = "sb_" + self.tag + (name or ("t%d" % self.n))
        return self.st.enter_context(self.nc.sbuf_tensor(name, list(shape), dt))

    def ps(self, shape, dt=F32, name=None):
        self.n += 1
        name = "ps_" + self.tag + (name or ("p%d" % self.n))
        return self.st.enter_context(self.nc.psum_tensor(name, list(shape), dt))

    def sbb(self, shape, dt=F32, name=None):
        t = self.sb(shape, dt, name)
        return t, self.S.buf(name)

    def psb(self, shape, dt=F32, name=None):
        t = self.ps(shape, dt, name)
        return t, self.S.buf(name)

    def store(self, q, out_ap, in_ap, reads, key, writes=()):
        ev = self.S.dma(q, out_ap, in_ap, reads=reads, writes=list(writes), key=key)
        self.outs.append(ev)
        return ev

    def finish(self):
        self.S.wait_events("sp", self.outs)
        self.S.emit()
        self.st.close()
        return self.nc


def layer_norm_group(C, xt, xb, tmp, n=4):
    S = C.S
    st, stb, mv, mvb, rs, rsb = tmp
    for i in range(n):
        S.op("dve", (lambda e, i=i: e.bn_stats(out=st[:, i, 0, :], in_=xt[:, i, 0:512])), reads=[xb[i]], writes=[stb])
        S.op("dve", (lambda e, i=i: e.bn_stats(out=st[:, i, 1, :], in_=xt[:, i, 512:1024])), reads=[xb[i]], writes=[stb])
        S.op("dve", (lambda e, i=i: e.bn_aggr(out=mv[:, i, :], in_=st[:, i, :, :].rearrange("p a b -> p (a b)"))), reads=[stb], writes=[mvb])
    S.op("dve", lambda e: e.tensor_scalar(out=rs[:, 0:n], in0=mv[:, 0:n, 1], scalar1=LN_EPS, scalar2=None, op0=ALU.add),
         reads=[mvb], writes=[rsb])
    S.op("act", lambda e: e.activation(out=rs[:, 0:n], in_=rs[:, 0:n], func=AF.Sqrt), reads=[rsb], writes=[rsb])
    S.op("dve", lambda e: e.reciprocal(out=rs[:, 0:n], in_=rs[:, 0:n]), reads=[rsb], writes=[rsb])
    for i in range(n):
        S.op("dve", (lambda e, i=i: e.tensor_scalar(out=xt[:, i, :], in0=xt[:, i, :], scalar1=mv[:, i, 0:1], scalar2=rs[:, i:i + 1],
                                                    op0=ALU.subtract, op1=ALU.mult)), reads=[xb[i], mvb, rsb], writes=[xb[i]])


def ln_tmp(C, n=4):
    st, stb = C.sbb([128, n, 2, 6])
    mv, mvb = C.sbb([128, n, 2])
    rs, rsb = C.sbb([128, n])
    return (st, stb, mv, mvb, rs, rsb)


def load_weight_bf16(C, w_dram, rows, cols, stage_rot, conv_eng="pool", name="w"):
    S = C.S
    kt = rows // 128
    wt = C.sb([128, kt, cols], BF16, name=name)
    bufs = [S.buf("%s_%d" % (name, k)) for k in range(kt)]
    CW = 2048
    for k in range(kt):
        for c0 in range(0, cols, CW):
            cw = min(CW, cols - c0)
            S.dma("pool", wt[:, k, c0:c0 + cw], w_dram[k * 128:(k + 1) * 128, c0:c0 + cw], writes=[bufs[k]], key=name)
    return wt, bufs


def make_stage_rot(C, n=2, width=2048, name="wstg"):
    items = []
    for i in range(n):
        t, b = C.sbb([128, width], F32, name="%s%d" % (name, i))
        items.append((t, b, "%s%d" % (name, i)))
    return Rot(items)


def load_cact(C, cT_d):
    S = C.S
    cT, cTb = C.sbb([128, 8], name="c_T")
    cA, cAb = C.sbb([128, 8], name="c_A")
    S.dma("sp", cT[:], cT_d, writes=[cTb], key="c_T")
    S.op("act", lambda e: e.activation(out=cA[:], in_=cT[:], func=AF.Silu), reads=[cTb], writes=[cAb])
    return cA, cAb


def compute_mod_T2(C, cA, cAb, w_d, bT_d, ncols, pm, pmb, stg_rot):
    S = C.S
    nj = ncols // 128
    bT, bTb = C.sbb([128, nj], name="b_T")
    modsb, modb = C.sbb([128, nj], name="mod_sb")
    S.dma("sp", bT[:], bT_d, writes=[bTb], key="b_T")
    for j in range(nj):
        t, b, key = stg_rot.next()
        S.dma("sp", t[:], w_d[:, j * 128:(j + 1) * 128].rearrange("(k p) c -> p k c", p=128), writes=[b], key=key)
        for kc in range(8):
            S.op("pe", (lambda e, t=t, j=j, kc=kc: e.matmul(out=pm[:, j:j + 1], lhsT=t[:, kc, :], rhs=cA[:, kc:kc + 1],
                                                            start=(kc == 0), stop=(kc == 7))), reads=[b, cAb], writes=[pmb])
    S.op("dve", lambda e: e.tensor_tensor(out=modsb[:], in0=pm[:, 0:nj], in1=bT[:], op=ALU.add), reads=[pmb, bTb], writes=[modb])
    return modsb, modb


def compute_mod_bc(C, cA, cAb, w_d, b_bc_d, stg_rot, psrot, name):
    S = C.S
    G, Gb = C.sbb([128, 1024], name=name)
    bb, bbb = C.sbb([128, 1024], name=name + "_b")
    ones, onesb = C.sbb([128, 128], name=name + "_ones")
    CA, CAb = C.sbb([128, 8, 128], name=name + "_CA")
    S.dma("sp", bb[:], b_bc_d, writes=[bbb], key=name + "_b")
    S.op("pool", lambda e: e.memset(ones[:], 1.0), writes=[onesb])
    for kc in range(8):
        S.op("dve", (lambda e, kc=kc: e.tensor_scalar(out=CA[:, kc, :], in0=ones[:], scalar1=cA[:, kc:kc + 1], scalar2=None, op0=ALU.mult)),
             reads=[onesb, cAb], writes=[CAb])
    pss = [psrot.next() for _ in range(2)]
    for j in range(8):
        t, b, key = stg_rot.next()
        S.dma("sp", t[:], w_d[:, j * 128:(j + 1) * 128].rearrange("(k p) c -> p k c", p=128), writes=[b], key=key)
        ps, psb = pss[j // 4]
        for kc in range(8):
            S.op("pe", (lambda e, t=t, j=j, kc=kc, ps=ps: e.matmul(out=ps[:, (j % 4) * 128:(j % 4 + 1) * 128], lhsT=CA[:, kc, :], rhs=t[:, kc, :],
                                                                   start=(kc == 0), stop=(kc == 7))), reads=[b, CAb], writes=[psb])
    for hf in range(2):
        ps, psb = pss[hf]
        S.op("dve", (lambda e, ps=ps, hf=hf: e.scalar_tensor_tensor(out=G[:, hf * 512:(hf + 1) * 512], in0=ps[:], scalar=1.0,
                                                                    in1=bb[:, hf * 512:(hf + 1) * 512], op0=ALU.add, op1=ALU.add)),
             reads=[psb, bbb], writes=[Gb])
    return G, Gb


def mod_stage_rot(C):
    items = []
    for i in range(2):
        t, b = C.sbb([128, 8, 128], F32, name="mstg%d" % i)
        items.append((t, b, "mstg%d" % i))
    return Rot(items)


def ln_affine_store(C, xt, xb, n, lng, lngb, lnb, lnbb, lntmp, out_rows, keyp):
    S = C.S
    layer_norm_group(C, xt, xb, lntmp, n)
    for i in range(n):
        S.op("pool", (lambda e, i=i: e.tensor_tensor(out=xt[:, i, :], in0=xt[:, i, :], in1=lng[:], op=ALU.mult)), reads=[xb[i], lngb], writes=[xb[i]])
        S.op("pool", (lambda e, i=i: e.tensor_tensor(out=xt[:, i, :], in0=xt[:, i, :], in1=lnb[:], op=ALU.add)), reads=[xb[i], lnbb], writes=[xb[i]])
        C.store("sp", out_rows(i), xt[:, i, :], [xb[i]], "%s%d" % (keyp, i))


def consts_att():
    p = np.arange(128)[:, None]
    rr = np.arange(128)[None, :]
    minv = (rr <= 127 - p).astype(np.float32)
    identb = np.eye(128, dtype=np.float32).astype(NPBF16)
    return minv, identb


def consts_ssm():
    I2 = np.concatenate([np.eye(64), np.eye(64)], axis=0).astype(np.float32)
    top = (np.arange(128) < 64).astype(np.float32)
    bot = 1.0 - top
    cst = np.stack([top, bot, -top, -bot, top - bot, bot - top, np.full(128, -np.pi, np.float32), np.zeros(128, np.float32)], axis=1)
    return I2, f32c(cst), np.eye(128, dtype=np.float32)


def run_prog(nc, in_maps):
    res = run_bass_kernel_spmd(nc, in_maps, core_ids=list(range(NCORES)))
    return res.results


def f32c(a):
    return np.ascontiguousarray(a, dtype=np.float32)


def bc128(v):
    return f32c(np.broadcast_to(np.asarray(v)[None, :], (128, v.shape[0])))


PAIRS = [[0, 1], [2, 3], [4, 5], [6, 7]]
I32 = mybir.dt.int32
TWO_PI = 6.283185307179586
NGRP = 16


_SCHED = []


def rank_of(e):
    return _SCHED[-1].rank(e)


def gather(C, S, src, srcb, dsts, dstb, key, rows, block=False):
    S.wait_events("pool", C.outs)
    R = src.shape[0]
    assert R % rows == 0 and len(dsts) == R // rows
    for k in range(R // rows):
        ev = S.coll("AllGather", src[k * rows:(k + 1) * rows, :], dsts[k], PAIRS, reads=[srcb], writes=[dstb], key=key)
        if block:
            C.outs.append(ev)


def phase_A(P, l, x_src, x_srcb, sc, E):
    C = Ctx(P)
    S = C.S
    idt, idb = C.sbb([128, 128], name="idt")
    jt, jb = C.sbb([128, 128], name="jt")
    S.dma("sp", idt[:], E["ident"], writes=[idb], key="idt")
    S.dma("sp", jt[:], E["J"], writes=[jb], key="jt")
    pm, pmb = C.psb([128, 512], name="pm")
    cA, cAb = load_cact(C, E["cT"])
    mrot = mod_stage_rot(C)
    modsb, modb = compute_mod_T2(C, cA, cAb, E["w_ada"][l][:, 0:2048], E["b_adaT"][l], 2048, pm, pmb, mrot)
    S.op("dve", lambda e: e.tensor_scalar(out=modsb[:, 8:16], in0=modsb[:, 8:16], scalar1=1.0, scalar2=None, op0=ALU.add),
         reads=[modb], writes=[modb])
    srot = make_stage_rot(C, 2, 2048)
    wbf, wb = load_weight_bf16(C, E["w_in"][l], D, 4096, srot, "pool", "w_in")

    xrot = Rot([(C.sb([128, 4, D], name="xt%d" % i), [S.buf() for _ in range(4)], "xt%d" % i) for i in range(2)])
    hrot = Rot([(C.sb([128, 8, 512], BF16, name="hT%d" % i), [S.buf() for _ in range(8)]) for i in range(2)])
    hrrot = Rot([(C.sb([128, 8, 512], BF16, name="hR%d" % i), [S.buf() for _ in range(8)]) for i in range(2)])
    lntmps = Rot([ln_tmp(C) for _ in range(2)])
    ptrot = Rot([C.psb([128, 512], name="ptr%d" % i) for i in range(3)])
    pprot = Rot([C.psb([128, 512], name="pp%d" % i) for i in range(4)])
    sbf = Rot([C.sbb([128, 512], BF16, name="sbf%d" % i) + ("sbf%d" % i,) for i in range(4)])
    sf32 = Rot([C.sbb([128, 512], F32, name="sf%d" % i) + ("sf%d" % i,) for i in range(2)])
    qT_s, qT_sb = sc["qT_s"]
    kTr_s, kTr_sb = sc["kTr_s"]
    vr_s, vr_sb = sc["vr_s"]
    ut_s, ut_sb = sc["utok_s"]
    sg_s, sg_sb = sc["sgT_s"]
    if "utok_slabs" not in sc:
        sc["utok_slabs"] = [S.buf() for _ in range(8)]
    ut_slabs = sc["utok_slabs"]

    def group(tg):
        xt, xb, xkey = xrot.next()
        hT, hb = hrot.next()
        hR, hRb = hrrot.next()
        S.dma("sp", xt[:], x_src[tg * 512:(tg + 1) * 512, :].rearrange("(i p) d -> p i d", p=128), reads=[x_srcb], writes=xb, key=xkey)
        layer_norm_group(C, xt, xb, lntmps.next())
        for fc in range(8):
            pt, ptb = ptrot.next()
            for i in range(4):
                S.op("pe", (lambda e, pt=pt, i=i, fc=fc: e.matmul(out=pt[:, i * 128:(i + 1) * 128], lhsT=xt[:, i, fc * 128:(fc + 1) * 128],
                                                               rhs=idt[:], start=True, stop=True)), reads=[xb[i], idb], writes=[ptb])
            S.op("act", (lambda e, pt=pt, fc=fc: e.activation(out=hT[:, fc, :], in_=pt[:], func=AF.Identity,
                                                             scale=modsb[:, 8 + fc:9 + fc], bias=modsb[:, fc:fc + 1])),
                 reads=[ptb, modb], writes=[hb[fc]])
            pt2, pt2b = ptrot.next()
            for i in range(4):
                S.op("pe", (lambda e, pt2=pt2, i=i, fc=fc: e.matmul(out=pt2[:, (3 - i) * 128:(4 - i) * 128], lhsT=xt[:, i, fc * 128:(fc + 1) * 128],
                                                                 rhs=jt[:], start=True, stop=True)), reads=[xb[i], jb], writes=[pt2b])
            S.op("dve", (lambda e, pt2=pt2, fc=fc: e.tensor_scalar(out=hR[:, fc, :], in0=pt2[:], scalar1=modsb[:, 8 + fc:9 + fc],
                                                                  scalar2=modsb[:, fc:fc + 1], op0=ALU.mult, op1=ALU.add)),
                 reads=[pt2b, modb], writes=[hRb[fc]])
        tsl = slice(tg * 512, (tg + 1) * 512)
        rbase = TOK - (tg + 1) * 512
        rsl = slice(rbase, rbase + 512)
        for oc in list(range(0, 8)) + list(range(16, 32)):
            pp, ppb = pprot.next()
            src, srcb = (hR, hRb) if 4 <= oc < 8 else (hT, hb)
            for fc in range(8):
                S.op("pe", (lambda e, pp=pp, fc=fc, oc=oc, src=src: e.matmul(out=pp[:], lhsT=wbf[:, fc, oc * 128:(oc + 1) * 128],
                                                                        rhs=src[:, fc, :], start=(fc == 0), stop=(fc == 7))),
                     reads=[wb[fc], srcb[fc]], writes=[ppb])
            stg, stgb, key = sbf.next()
            if oc < 8:
                S.op("dve", (lambda e, stg=stg, pp=pp: e.tensor_copy(out=stg[:], in_=pp[:])), reads=[ppb], writes=[stgb])
                if oc < 4:
                    C.store("pool", qT_s[oc * 128:(oc + 1) * 128, tsl], stg[:], [stgb], key, writes=[qT_sb])
                else:
                    C.store("pool", kTr_s[(oc - 4) * 128:(oc - 3) * 128, rsl], stg[:], [stgb], key, writes=[kTr_sb])
            else:
                S.op("act", (lambda e, stg=stg, pp=pp: e.activation(out=stg[:], in_=pp[:], func=AF.Sigmoid)), reads=[ppb], writes=[stgb])
                C.store("pool", sg_s[(oc - 16) * 128:(oc - 15) * 128, tsl], stg[:], [stgb], key, writes=[sg_sb])
        for i in range(4):
            pp, ppb = pprot.next()
            for fc in range(8):
                S.op("pe", (lambda e, pp=pp, fc=fc, i=i: e.matmul(out=pp[:], lhsT=hR[:, fc, i * 128:(i + 1) * 128],
                                                               rhs=wbf[:, fc, 1024:1536], start=(fc == 0), stop=(fc == 7))),
                     reads=[wb[fc], hRb[fc]], writes=[ppb])
            stg, stgb, key = sbf.next()
            S.op("dve", (lambda e, stg=stg, pp=pp: e.tensor_copy(out=stg[:], in_=pp[:])), reads=[ppb], writes=[stgb])
            for hg in range(2):
                C.store("pool", vr_s[hg * TOK + rbase + i * 128:hg * TOK + rbase + (i + 1) * 128, :], stg[:, hg * 256:(hg + 1) * 256],
                        [stgb], key, writes=[vr_sb])
        for i in range(4):
            pp, ppb = pprot.next()
            for fc in range(8):
                S.op("pe", (lambda e, pp=pp, fc=fc, i=i: e.matmul(out=pp[:], lhsT=hT[:, fc, i * 128:(i + 1) * 128],
                                                               rhs=wbf[:, fc, 1536:2048], start=(fc == 0), stop=(fc == 7))),
                     reads=[wb[fc], hb[fc]], writes=[ppb])
            stg, stgb, key = sf32.next()
            S.op("dve", (lambda e, stg=stg, pp=pp: e.tensor_copy(out=stg[:], in_=pp[:])), reads=[ppb], writes=[stgb])
            r0 = (tg * 4 + i) * 128
            for hg in range(2):
                C.store("pool", ut_s[hg * TOK + r0:hg * TOK + r0 + 128, :], stg[:, hg * 256:(hg + 1) * 256], [stgb], key,
                        writes=[ut_slabs[hg * 4 + r0 // 1024]])

    ut_gaps, ut_gb = sc["utok_g"]
    for tg in range(TOK // 512):
        group(tg)
        if tg % 2 == 1:
            S.wait_events("pool", C.outs)
            j = tg // 2
            for kslab in (j, 4 + j):
                S.coll("AllGather", ut_s[kslab * 1024:(kslab + 1) * 1024, :], ut_gaps[kslab], PAIRS, reads=[ut_slabs[kslab]], writes=[ut_gb],
                       key="ag_utok")
    for nm, rows in (("qT", 128), ("kTr", 128), ("vr", 2048)):
        s_ap, s_b = sc[nm + "_s"]
        g_aps, g_b = sc[nm + "_g"]
        gather(C, S, s_ap, s_b, g_aps, g_b, "ag_" + nm, rows)
    C.finish()


def phase_Batt(P, sc, E):
    C = Ctx(P)
    S = C.S
    CH = 2048
    qT_g, qT_gb = sc["qT_g"]
    kTr_g, kTr_gb = sc["kTr_g"]
    vr_g, vr_gb = sc["vr_g"]
    oT_s, oT_sb = sc["oT_s"]
    qs = [C.sbb([128, S_LEN], BF16, name="q%d" % i) for i in range(2)]
    ks = [C.sbb([128, S_LEN], BF16, name="k%d" % i) for i in range(2)]
    vs, vsb = C.sbb([128, 64, 256], BF16, name="vs")
    mneg, mnegb = C.sbb([128, 128], BF16, name="mneg")
    idb_t, idbb = C.sbb([128, 128], BF16, name="identb")
    ones, onesb = C.sbb([128, CH], BF16, name="ones")
    zeros, zerosb = C.sbb([128, 3, 128], BF16, name="zeros")
    carry = C.sb([128, 4], F32, name="carry")
    carryb = [S.buf() for _ in range(4)]
    S.dma("sp", mneg[:], E["mneg"], writes=[mnegb], key="mneg")
    S.dma("sp", idb_t[:], E["identb"], writes=[idbb], key="identb")
    S.op("pool", lambda e: e.memset(ones[:], 1.0), writes=[onesb])
    S.op("pool", lambda e: e.memset(zeros[:], 0.0), writes=[zerosb])
    for i in range(2):
        for hf in range(2):
            sl = slice(hf * 4096, (hf + 1) * 4096)
            S.dma("sp", qs[i][0][:, sl], (lambda r, i=i, hf=hf: qT_g[r * 2 + i][hf * 128:(hf + 1) * 128, :]),
                  reads=[qT_gb], writes=[qs[i][1]], key="q%d" % i)
            S.dma("sp", ks[i][0][:, sl], (lambda r, i=i, hf=hf: kTr_g[r * 2 + i][(1 - hf) * 128:(2 - hf) * 128, :]),
                  reads=[kTr_gb], writes=[ks[i][1]], key="k%d" % i)
    for j in range(4):
        src_ = 1 if j < 2 else 0
        S.dma("sp", vs[:, j * 16:(j + 1) * 16, :],
              (lambda r, j=j, src_=src_: vr_g[r * 2 + j % 2][src_ * 2048:(src_ + 1) * 2048, :].rearrange("(blk p) c -> p blk c", p=128)),
              reads=[vr_gb], writes=[vsb], key="vs")

    NB = 3
    gs = [(C.sb([128, CH], F32, name="g%d" % i), [S.buf() for _ in range(4)]) for i in range(NB)]
    cbs = [C.sbb([128, CH + 1], F32, name="cb%d" % i) for i in range(NB)]
    As = [C.sbb([128, CH], BF16, name="A%d" % i) for i in range(2)]
    AT4s = [(C.sb([128, 16, 4, 128], BF16, name="AT4_%d" % i), [S.buf() for _ in range(2)]) for i in range(2)]
    osts = [C.sbb([64, 512], BF16, name="ost%d" % i) + ("ost%d" % i,) for i in range(2)]
    zrot = Rot([C.psb([128, 512], F32, name="z%d" % i) for i in range(4)])
    pTs = [C.psb([128, 1024], BF16, name="pT%d" % i) for i in range(2)]
    pos = [C.psb([64, 512], F32, name="po%d" % i) for i in range(2)]

    units = []
    gcs = []
    for h in range(4):
        for G in range(16):
            N = 512 * (G + 1)
            r0 = S_LEN - N
            offs = list(range(0, N, CH))
            for ci, off in enumerate(offs):
                n = min(CH, N - off)
                gc = dict(h=h, G=G, r0=r0, off=off, n=n, first=(ci == 0), last=(ci == len(offs) - 1), idx=len(gcs))
                gcs.append(gc)
                for k in range(4):
                    lo = 128 * (3 - k) if ci == 0 else 0
                    units.append(dict(gc=gc, k=k, lo=lo, h=h, G=G, r0=r0, off=off, n=n, first=(ci == 0), last=(ci == len(offs) - 1)))

    def s1(i, u):
        h, G, k, r0, off, n, lo = u["h"], u["G"], u["k"], u["r0"], u["off"], u["n"], u["lo"]
        qt = 4 * G + k
        g, gb = gs[i % NB]
        qtile, qb = qs[h // 2]
        ktile, kb = ks[h // 2]
        p0 = (h % 2) * 64
        first_bank = True
        for s in range(lo, n, 512):
            w = min(512, n - s)
            zb, zbb = zrot.next()
            diag = u["first"] and first_bank
            S.op("pe", (lambda e, zb=zb, w=w, s=s, diag=diag: e.matmul(out=zb[:, 0:w], lhsT=qtile[p0:p0 + 64, qt * 128:(qt + 1) * 128],
                                                                   rhs=ktile[p0:p0 + 64, r0 + off + s:r0 + off + s + w],
                                                                   start=True, stop=(not diag))),
                 reads=[qb, kb], writes=[zbb])
            if diag:
                S.op("pe", (lambda e, zb=zb: e.matmul(out=zb[:, 0:128], lhsT=idb_t[:], rhs=mneg[:], start=False, stop=True)),
                     reads=[idbb, mnegb], writes=[zbb])
            S.op("act", (lambda e, zb=zb, w=w, s=s: e.activation(out=g[:, s:s + w], in_=zb[:, 0:w], func=AF.Sigmoid, scale=-0.125)),
                 reads=[zbb], writes=[gb[(s - lo) // 512]])
            first_bank = False

    def s2(i, u):
        n, lo, k = u["n"], u["lo"], u["k"]
        g, gb = gs[i % NB]
        cb, cbb = cbs[i % NB]
        if u["first"]:
            S.op("dve", lambda e: e.memset(cb[:, lo:lo + 1], 1.0), writes=[cbb])
            init = 1.0
            rd = []
        else:
            S.op("dve", lambda e: e.tensor_copy(out=cb[:, 0:1], in_=carry[:, k:k + 1]), reads=[carryb[k]], writes=[cbb])
            init = carry[:, k:k + 1]
            rd = [carryb[k]]
        ng = (n - lo + 511) // 512
        S.op("dve", lambda e: e.tensor_tensor_scan(out=cb[:, lo + 1:n + 1], data0=g[:, lo:n], data1=ones[:, lo:n], initial=init,
                                                   op0=ALU.mult, op1=ALU.mult),
             reads=gb[0:ng] + [onesb, cbb] + rd, writes=[cbb])
        if not u["last"]:
            S.op("dve", lambda e: e.tensor_copy(out=carry[:, k:k + 1], in_=cb[:, n:n + 1]), reads=[cbb], writes=[carryb[k]])

    def s3(i, u):
        n, lo = u["n"], u["lo"]
        cb, cbb = cbs[i % NB]
        A, Ab = As[i % 2]
        S.op("pool", lambda e: e.tensor_tensor(out=A[:, lo:n], in0=cb[:, lo:n], in1=cb[:, lo + 1:n + 1], op=ALU.subtract),
             reads=[cbb], writes=[Ab])

    def s4(i, u):
        n, lo, k = u["n"], u["lo"], u["k"]
        A, Ab = As[i % 2]
        AT4, ATb = AT4s[u["gc"]["idx"] % 2]
        nblk = n // 128
        blo = lo // 128
        if blo > 0:
            S.op("act", lambda e: e.copy(out=AT4[:, 0:blo, k, :], in_=zeros[:, 0:blo, :]), reads=[zerosb], writes=[ATb[0]])
        for b0 in range(0, nblk, 8):
            bs = max(b0, blo)
            be = min(b0 + 8, nblk)
            if bs >= be:
                continue
            pT, pTb = pTs[(b0 // 8) % 2]
            for blk in range(bs, be):
                j = blk - b0
                S.op("pe", (lambda e, pT=pT, j=j, blk=blk: e.transpose(out=pT[:, j * 128:(j + 1) * 128], in_=A[:, blk * 128:(blk + 1) * 128],
                                                                   identity=idb_t[:])),
                     reads=[Ab, idbb], writes=[pTb])
            S.op("act", (lambda e, pT=pT, b0=b0, bs=bs, be=be: e.copy(out=AT4[:, bs:be, k, :],
                                                                   in_=pT[:, (bs - b0) * 128:(be - b0) * 128].rearrange("p (b q) -> p b q", q=128))),
                 reads=[pTb], writes=[ATb[b0 // 8]])

    def s5(gc, part):
        h, G, r0, off, n = gc["h"], gc["G"], gc["r0"], gc["off"], gc["n"]
        AT4, ATb = AT4s[gc["idx"] % 2]
        gidx = h * 16 + G
        po, pob = pos[gidx % 2]
        nblk = n // 128
        kb0 = (r0 + off) // 128
        for blk in range(part * 4, min(part * 4 + 4, nblk)):
            S.op("pe", (lambda e, blk=blk: e.matmul(out=po[:, :], lhsT=vs[:, kb0 + blk, h * 64:(h + 1) * 64],
                                                  rhs=AT4[:, blk, :, :].rearrange("p a b -> p (a b)"),
                                                  start=(gc["first"] and blk == 0), stop=(gc["last"] and blk == nblk - 1))),
                 reads=[vsb, ATb[blk // 8]], writes=[pob])
        if gc["last"] and part * 4 <= nblk - 1 < part * 4 + 4:
            ost, ostb, okey = osts[gidx % 2]
            S.op("act", lambda e: e.copy(out=ost[:], in_=po[:, :]), reads=[pob], writes=[ostb])
            half = G // 8
            c0_ = (G % 8) * 512
            C.store("sp", oT_s[half * 256 + h * 64:half * 256 + (h + 1) * 64, c0_:c0_ + 512], ost[:], [ostb], okey, writes=[oT_sb])

    nun = len(units)
    pending = {}
    for it in range(nun + 9):
        for d, st in enumerate((s1, s2, s3, s4)):
            i = it - d
            if 0 <= i < nun:
                st(i, units[i])
        i5 = it - 4
        if 0 <= i5 < nun and units[i5]["k"] == 3:
            for part in range(4):
                pending.setdefault(it + part, []).append((units[i5]["gc"], part))
        for gc_, part in pending.pop(it, []):
            s5(gc_, part)
    assert not pending
    gather(C, S, oT_s, oT_sb, sc["oT_g"][0], sc["oT_g"][1], "ag_oT", 128)
    C.finish()


def phase_Bssm(P, l, sc, E):
    C = Ctx(P)
    S = C.S
    G = NGRP
    ut_g, ut_gb = sc["utok_g"]
    yt_s, yt_sb = sc["ytok_s"]

    def ld(name, src, shape):
        t, b = C.sbb(shape, name=name)
        S.dma("sp", t[:], src, writes=[b], key=name)
        return t, b

    are, areb = ld("are", E["are"][l], [128, G])
    aim, aimb = ld("aim", E["aim"][l], [128, G])
    ldt, ldtb = ld("ldt", E["ldt"][l], [128, G])
    dTs, dTsb = ld("dTs", E["dTs"][l], [128, G])
    I2, I2b = ld("I2", E["I2"], [128, 64])
    cst, cstb = ld("cst", E["cst"], [128, 8])
    idt, idb = ld("idt", E["ident"], [128, 128])
    bA, bAb = ld("bA", E["bA"][l].rearrange("g p c -> p g c"), [128, G, 16])
    bB, bBb = ld("bB", E["bB"][l].rearrange("g p c -> p g c"), [128, G, 16])
    cTs, cTsb = ld("cTs", E["cTs"][l].rearrange("g p c -> p g c"), [128, G, 16])
    MT, MB, NMT, NMB, SGN, NSGN, NPI = [cst[:, i:i + 1] for i in range(7)]
    psrot = Rot([C.psb([128, 512], F32, name="ps%d" % i) for i in range(8)])

    U_all = C.sb([128, G, 1024], name="U_all")
    U_allb = [S.buf() for _ in range(4)]
    u3rot = Rot([C.sbb([128, 8, 256], F32, name="u3_%d" % i) + ("u3_%d" % i,) for i in range(2)])
    u3grot = Rot([C.sbb([128, 16, 8, 16], F32, name="u3g_%d" % i) for i in range(2)])

    def load_mb(mb):
        u3, u3b, key = u3rot.next()
        src_ = mb // 4
        S.dma("sp", u3[:], (lambda r: ut_g[r * 4 + mb % 4][src_ * 1024:(src_ + 1) * 1024, :].rearrange("(m j) c -> m j c", j=8)),
              reads=[ut_gb], writes=[u3b], key=key)
        u3g, u3gb = u3grot.next()
        S.op("pool", lambda e: e.tensor_copy(out=u3g[:].rearrange("p g j c -> p j g c"), in_=u3[:].rearrange("p j (g c) -> p j g c", c=16)),
             reads=[u3b], writes=[u3gb])

        def quad(gq):
            ps, psb = psrot.next()
            for q in range(4):
                g = gq * 4 + q
                S.op("pe", (lambda e, q=q, g=g: e.matmul(out=ps[:, q * 128:(q + 1) * 128], lhsT=u3g[:, g, :, :].rearrange("p j c -> p (j c)"),
                                                     rhs=idt[:], start=True, stop=True)), reads=[u3gb, idb], writes=[psb])
            dst = U_all[:, gq * 4:(gq + 1) * 4, mb * 128:(mb + 1) * 128]
            src = ps[:].rearrange("p (q m) -> p q m", q=4)
            if gq % 2 == 0:
                S.op("dve", lambda e: e.tensor_copy(out=dst, in_=src), reads=[psb], writes=[U_allb[gq]])
            else:
                S.op("act", lambda e: e.copy(out=dst, in_=src), reads=[psb], writes=[U_allb[gq]])

        for gq in range(4):
            quad(gq)

    for mb in range(8):
        load_mb(mb)

    def small(name):
        return C.sbb([128, G], name=name)

    def dve(fn, reads, writes):
        S.op("dve", fn, reads=reads, writes=writes)

    def tt(out, ob, a, ab, b, bb, op):
        dve(lambda e: e.tensor_tensor(out=out, in0=a, in1=b, op=op), [ab, bb], [ob])

    dt, dtb = small("dt")
    S.op("act", lambda e: e.activation(out=dt[:], in_=ldt[:], func=AF.Exp), reads=[ldtb], writes=[dtb])
    ar, arb = small("ar")
    th, thb = small("th")
    tt(ar[:], arb, are[:], areb, dt[:], dtb, ALU.mult)
    tt(th[:], thb, aim[:], aimb, dt[:], dtb, ALU.mult)
    rho, rhob = small("rho")
    S.op("act", lambda e: e.activation(out=rho[:], in_=ar[:], func=AF.Exp), reads=[arb], writes=[rhob])

    def sin_of(name, shift):
        a, ab = small(name + "_a")
        ki, kib = C.sbb([128, G], I32, name=name + "_ki")
        kf, kfb = small(name + "_kf")
        r, rb = small(name + "_r")
        m, mb_ = small(name + "_m")
        out, outb = small(name)
        dve(lambda e: e.tensor_scalar(out=a[:], in0=th[:], scalar1=shift, scalar2=None, op0=ALU.add), [thb], [ab])
        dve(lambda e: e.tensor_scalar(out=kf[:], in0=a[:], scalar1=1.0 / TWO_PI, scalar2=None, op0=ALU.mult), [ab], [kfb])
        dve(lambda e: e.tensor_copy(out=ki[:], in_=kf[:]), [kfb], [kib])
        dve(lambda e: e.tensor_copy(out=kf[:], in_=ki[:]), [kib], [kfb])
        dve(lambda e: e.scalar_tensor_tensor(out=r[:], in0=kf[:], scalar=-TWO_PI, in1=a[:], op0=ALU.mult, op1=ALU.add), [kfb, ab], [rb])
        dve(lambda e: e.tensor_scalar(out=m[:], in0=r[:], scalar1=-3.141592653589793, scalar2=1e30, op0=ALU.add, op1=ALU.mult), [rb], [mb_])
        dve(lambda e: e.tensor_scalar(out=m[:], in0=m[:], scalar1=0.0, scalar2=1.0, op0=ALU.max, op1=ALU.min), [mb_], [mb_])
        dve(lambda e: e.scalar_tensor_tensor(out=r[:], in0=m[:], scalar=-TWO_PI, in1=r[:], op0=ALU.mult, op1=ALU.add), [mb_, rb], [rb])
        dve(lambda e: e.tensor_scalar(out=m[:], in0=r[:], scalar1=3.141592653589793, scalar2=-1e30, op0=ALU.add, op1=ALU.mult), [rb], [mb_])
        dve(lambda e: e.tensor_scalar(out=m[:], in0=m[:], scalar1=0.0, scalar2=1.0, op0=ALU.max, op1=ALU.min), [mb_], [mb_])
        dve(lambda e: e.scalar_tensor_tensor(out=r[:], in0=m[:], scalar=TWO_PI, in1=r[:], op0=ALU.mult, op1=ALU.add), [mb_, rb], [rb])
        S.op("act", lambda e: e.activation(out=out[:], in_=r[:], func=AF.Sin), reads=[rb], writes=[outb])
        return out, outb

    sn, snb = sin_of("sn", 0.0)
    cs, csb = sin_of("cs", 1.5707963267948966)

    NE = 1 + 4 * 8
    PW, PWb = C.sbb([128, NE, 2, G], name="PW")

    def pidx(k, j):
        return 0 if j == 0 else 1 + k * 8 + (j - 1)

    S.op("pool", lambda e: e.memset(PW[:, 0, 0, :], 1.0), writes=[PWb])
    S.op("pool", lambda e: e.memset(PW[:, 0, 1, :], 0.0), reads=[PWb], writes=[PWb])
    tt(PW[:, 1, 0, :], PWb, rho[:], rhob, cs[:], csb, ALU.mult)
    tt(PW[:, 1, 1, :], PWb, rho[:], rhob, sn[:], snb, ALU.mult)
    t1, t1b = small("cm_t1")
    t2, t2b = small("cm_t2")

    def cmul(io, ia, ib):
        ar_, ai_ = PW[:, ia, 0, :], PW[:, ia, 1, :]
        br_, bi_ = PW[:, ib, 0, :], PW[:, ib, 1, :]
        tt(t1[:], t1b, ai_, PWb, bi_, PWb, ALU.mult)
        tt(t2[:], t2b, ar_, PWb, br_, PWb, ALU.mult)
        tt(PW[:, io, 0, :], PWb, t2[:], t2b, t1[:], t1b, ALU.subtract)
        tt(t1[:], t1b, ar_, PWb, bi_, PWb, ALU.mult)
        tt(t2[:], t2b, ai_, PWb, br_, PWb, ALU.mult)
        tt(PW[:, io, 1, :], PWb, t1[:], t1b, t2[:], t2b, ALU.add)

    for k in range(4):
        if k > 0:
            dve((lambda e, k=k: e.tensor_copy(out=PW[:, pidx(k, 1), :, :], in_=PW[:, pidx(k - 1, 8), :, :])), [PWb], [PWb])
        for j in range(2, 9):
            cmul(pidx(k, j), pidx(k, j - 1), pidx(k, 1))

    nr, nrb = small("nr")
    den, denb = small("den")
    fre, freb = small("fre")
    fim, fimb = small("fim")
    F2, F2b = small("F2")
    lr, li = PW[:, 1, 0, :], PW[:, 1, 1, :]
    dve(lambda e: e.tensor_scalar(out=nr[:], in0=lr, scalar1=-1.0, scalar2=None, op0=ALU.add), [PWb], [nrb])
    tt(den[:], denb, are[:], areb, are[:], areb, ALU.mult)
    tt(t1[:], t1b, aim[:], aimb, aim[:], aimb, ALU.mult)
    tt(den[:], denb, den[:], denb, t1[:], t1b, ALU.add)
    dve(lambda e: e.reciprocal(out=den[:], in_=den[:]), [denb], [denb])
    tt(t1[:], t1b, nr[:], nrb, are[:], areb, ALU.mult)
    tt(t2[:], t2b, li, PWb, aim[:], aimb, ALU.mult)
    tt(fre[:], freb, t1[:], t1b, t2[:], t2b, ALU.add)
    tt(fre[:], freb, fre[:], freb, den[:], denb, ALU.mult)
    tt(t1[:], t1b, li, PWb, are[:], areb, ALU.mult)
    tt(t2[:], t2b, nr[:], nrb, aim[:], aimb, ALU.mult)
    tt(fim[:], fimb, t1[:], t1b, t2[:], t2b, ALU.subtract)
    tt(fim[:], fimb, fim[:], fimb, den[:], denb, ALU.mult)
    dve(lambda e: e.tensor_scalar(out=F2[:], in0=fim[:], scalar1=NSGN, scalar2=None, op0=ALU.mult), [fimb, cstb], [F2b])

    VV, VVb = C.sbb([128, NE, 4, G], name="VV")

    def mkvv(idx):
        pr, pi_ = PW[:, idx, 0, :], PW[:, idx, 1, :]
        for (vi, (s_top, src_top), (s_bot, src_bot)) in ((0, (MT, pr), (NMB, pi_)), (1, (MT, pi_), (MB, pr)),
                                                         (2, (MT, pr), (MB, pi_)), (3, (NMT, pi_), (MB, pr))):
            dve((lambda e, s_bot=s_bot, src_bot=src_bot: e.tensor_scalar(out=t1[:], in0=src_bot, scalar1=s_bot, scalar2=None, op0=ALU.mult)),
                [PWb, cstb], [t1b])
            dve((lambda e, vi=vi, s_top=s_top, src_top=src_top: e.scalar_tensor_tensor(
                out=VV[:, idx, vi, :], in0=src_top, scalar=s_top, in1=t1[:], op0=ALU.mult, op1=ALU.add)),
                [PWb, cstb, t1b], [VVb])

    for idx in range(1, NE):
        mkvv(idx)

    class GSet:
        pass

    def mkset(si):
        gsx = GSet()
        gsx.P0 = [None] + [C.sbb([128, 128], name="P0_%d_%d" % (si, d)) for d in range(1, 9)]
        gsx.PT = {}
        for k in range(4):
            for j in range(1, 8):
                gsx.PT[(k, j)] = C.sbb([128, 128], name="PT_%d_%d_%d" % (si, k, j))
        gsx.tmpB = C.sbb([128, 16], name="tmpB%d" % si)
        gsx.Cm = C.sbb([128, 16], name="Cm%d" % si)
        gsx.Bw = C.sbb([128, 240], name="Bw%d" % si)
        gsx.Rw = C.sbb([128, 256], name="Rw%d" % si)
        gsx.K1T = C.sbb([128, 128], name="K1T%d" % si)
        gsx.T1T = C.sbb([128, 128], name="T1T%d" % si)
        gsx.E = C.sbb([128, 1024], name="E%d" % si)
        gsx.X = {1024: C.sbb([128, 1024], name="X1_%d" % si), 128: C.sbb([128, 128], name="X2_%d" % si),
                 16: C.sbb([128, 16], name="X3_%d" % si), 2: C.sbb([128, 2], name="X4_%d" % si)}
        gsx.F = {128: C.sbb([128, 128], name="F1_%d" % si), 16: C.sbb([128, 16], name="F2_%d" % si), 2: C.sbb([128, 2], name="F3_%d" % si)}
        gsx.Y = C.sbb([128, 1024], name="Y%d" % si)
        gsx.Erm = {128: C.sbb([128, 7, 128], name="Er1_%d" % si), 16: C.sbb([128, 7, 16], name="Er2_%d" % si), 2: C.sbb([128, 7, 2], name="Er3_%d" % si)}
        S.op("pool", lambda e: e.memset(gsx.Bw[0][:], 0.0), writes=[gsx.Bw[1]])
        S.op("pool", lambda e: e.memset(gsx.Rw[0][:], 0.0), writes=[gsx.Rw[1]])
        return gsx

    gsets = [mkset(0), mkset(1)]
    Y3q, Y3qb = C.sbb([128, 8, 8, 64], name="Y3q")

    def build_mat(dst, idx, g, v0, v1):
        t, b = dst
        S.op("dve", lambda e: e.tensor_scalar(out=t[:, 0:64], in0=I2[:], scalar1=VV[:, idx, v0, g:g + 1], scalar2=None, op0=ALU.mult),
             reads=[I2b, VVb], writes=[b])
        S.op("dve", lambda e: e.tensor_scalar(out=t[:, 64:128], in0=I2[:], scalar1=VV[:, idx, v1, g:g + 1], scalar2=None, op0=ALU.mult),
             reads=[I2b, VVb], writes=[b])

    def mm(out, lhsT, rhs, start, stop, reads, writes):
        S.op("pe", lambda e: e.matmul(out=out, lhsT=lhsT, rhs=rhs, start=start, stop=stop), reads=reads, writes=writes)

    def do_group(g):
        gs_ = gsets[g % 2]
        Ub = U_allb[g // 4]
        tB, tBb = gs_.tmpB
        Cm, Cmb = gs_.Cm
        Bw, Bwb = gs_.Bw
        Rw, Rwb = gs_.Rw
        dve(lambda e: e.tensor_scalar(out=tB[:], in0=bB[:, g, :], scalar1=F2[:, g:g + 1], scalar2=None, op0=ALU.mult), [bBb, F2b], [tBb])
        dve(lambda e: e.scalar_tensor_tensor(out=Bw[:, 112:128], in0=bA[:, g, :], scalar=fre[:, g:g + 1], in1=tB[:],
                                             op0=ALU.mult, op1=ALU.add), [bAb, freb, tBb], [Bwb])
        dve(lambda e: e.tensor_scalar(out=Cm[:], in0=cTs[:, g, :], scalar1=SGN, scalar2=None, op0=ALU.mult), [cTsb, cstb], [Cmb])
        for d in range(1, 9):
            build_mat(gs_.P0[d], pidx(0, d), g, 2, 3)
        for k in range(4):
            for j in range(1, 8):
                build_mat(gs_.PT[(k, j)], pidx(k, j), g, 0, 1)

        def PTm(k, j):
            return (idt, idb) if j == 0 else gs_.PT[(k, j)]

        pR, pRb = psrot.next()
        for d in range(9):
            Pd, Pdb = (idt, idb) if d == 0 else gs_.P0[d]
            mm(pR[:, d * 16:(d + 1) * 16], Pd[:], Cm[:], True, True, [Pdb, Cmb], [pRb])
        dve(lambda e: e.tensor_copy(out=Rw[:, 112:256], in_=pR[:, 0:144]), [pRb], [Rwb])
        pK, pKb = psrot.next()
        pT_, pT_b = psrot.next()
        for j in range(8):
            Pj, Pjb = PTm(0, 7 - j)
            mm(pK[:, 0:128], Bw[:, (7 - j) * 16:(7 - j) * 16 + 128], Pj[:], (j == 0), (j == 7), [Bwb, Pjb], [pKb])
        for j in range(8):
            mm(pT_[:, 0:128], Bw[:, (7 - j) * 16:(7 - j) * 16 + 128], Rw[:, (7 - j) * 16:(7 - j) * 16 + 128], (j == 0), (j == 7),
               [Bwb, Rwb], [pT_b])
        K1T, K1Tb = gs_.K1T
        T1T, T1Tb = gs_.T1T
        dve(lambda e: e.tensor_copy(out=K1T[:], in_=pK[:, 0:128]), [pKb], [K1Tb])
        S.op("act", lambda e: e.copy(out=T1T[:], in_=pT_[:, 0:128]), reads=[pT_b], writes=[T1Tb])
        E_, Eb = gs_.E

        def e_half(hf):
            pe_, pe_b = psrot.next()
            mm(pe_[:], K1T[:], U_all[:, g, hf * 512:(hf + 1) * 512], True, True, [K1Tb, Ub], [pe_b])
            if hf == 0:
                dve(lambda e: e.tensor_copy(out=E_[:, hf * 512:(hf + 1) * 512], in_=pe_[:]), [pe_b], [Eb])
            else:
                S.op("act", lambda e: e.copy(out=E_[:, hf * 512:(hf + 1) * 512], in_=pe_[:]), reads=[pe_b], writes=[Eb])

        e_half(0)
        e_half(1)

        def solve(k, Et, Etb, N):
            Xp, Xpb = gs_.X[N]
            if N == 2:
                S.op("pool", lambda e: e.memset(Xp[:, 0:1], 0.0), writes=[Xpb])
                dve(lambda e: e.tensor_copy(out=Xp[:, 1:2], in_=Et[:, 0:1]), [Etb, Xpb], [Xpb])
                return Xp, Xpb
            M = N // 8
            Ev = Et[:, 0:N].rearrange("p (m r) -> p r m", r=8)
            Xv = Xp[:, 0:N].rearrange("p (m r) -> p r m", r=8)
            Ft, Ftb = gs_.F[M]
            pf, pfb = psrot.next()
            for rp in range(8):
                Pj, Pjb = PTm(k, 7 - rp)
                mm(pf[:, 0:M], Pj[:], Ev[:, rp, :], (rp == 0), (rp == 7), [Pjb, Etb], [pfb])
            dve(lambda e: e.tensor_copy(out=Ft[:, 0:M], in_=pf[:, 0:M]), [pfb], [Ftb])
            Zp, Zpb = solve(k + 1, Ft, Ftb, M)
            dve(lambda e: e.tensor_copy(out=Xv[:, 0, :], in_=Zp[:, 0:M]), [Zpb], [Xpb])
            Erm, Ermb = gs_.Erm[M]
            p0, p0b = psrot.next()
            P1, P1b = PTm(k, 1)
            mm(p0[:, 0:M], idt[:], Ev[:, 0, :], True, False, [idb, Etb], [p0b])
            mm(p0[:, 0:M], P1[:], Zp[:, 0:M], False, True, [P1b, Zpb], [p0b])
            S.op("act", lambda e: e.copy(out=Erm[:, 0, :], in_=p0[:, 0:M]), reads=[p0b], writes=[Ermb])
            dve(lambda e: e.tensor_copy(out=Erm[:, 1:7, :], in_=Ev[:, 1:7, :]), [Etb], [Ermb])
            Ef = Erm[:].rearrange("p r m -> p (r m)")
            accA, accAb = psrot.next()
            two = 7 * M > 512
            if two:
                accB, accBb = psrot.next()
            for j in range(7):
                Pj, Pjb = PTm(k, j)
                lo_c, hi_c = j * M, 7 * M
                segs = [(lo_c, hi_c)] if not two else [sg for sg in ((lo_c, min(hi_c, 512)), (max(lo_c, 512), hi_c)) if sg[0] < sg[1]]
                for (a_, b_) in segs:
                    if a_ < 512:
                        mm(accA[:, a_:b_], Pj[:], Ef[:, a_ - lo_c:b_ - lo_c], (j == 0), (j == 6), [Pjb, Ermb], [accAb])
                    else:
                        mm(accB[:, a_ - 512:b_ - 512], Pj[:], Ef[:, a_ - lo_c:b_ - lo_c], (j == 0), (j == 6), [Pjb, Ermb], [accBb])
            nA = min(7, 512 // M)
            dve(lambda e: e.tensor_copy(out=Xv[:, 1:1 + nA, :], in_=accA[:, 0:nA * M].rearrange("p (r m) -> p r m", m=M)), [accAb, Xpb], [Xpb])
            if nA < 7:
                S.op("act", lambda e: e.copy(out=Xv[:, 1 + nA:8, :], in_=accB[:, 0:(7 - nA) * M].rearrange("p (r m) -> p r m", m=M)),
                     reads=[accBb, Xpb], writes=[Xpb])
            return Xp, Xpb

        X1, X1b = solve(1, E_, Eb, 1024)
        Y, Yb = gs_.Y

        def y_half(hf):
            py, pyb = psrot.next()
            sl = slice(hf * 512, (hf + 1) * 512)
            mm(py[:], T1T[:], U_all[:, g, sl], True, False, [T1Tb, Ub], [pyb])
            mm(py[:], Rw[:, 128:256], X1[:, sl], False, True, [Rwb, X1b], [pyb])
            dve(lambda e: e.scalar_tensor_tensor(out=Y[:, sl], in0=U_all[:, g, sl], scalar=dTs[:, g:g + 1], in1=py[:],
                                                 op0=ALU.mult, op1=ALU.add), [pyb, Ub, dTsb], [Yb])

        y_half(0)
        y_half(1)
        S.op("act", lambda e: e.activation(out=Y[:], in_=Y[:], func=AF.Gelu_apprx_tanh), reads=[Yb], writes=[Yb])
        gl = g % 4

        def back(mq):
            ps, psb = psrot.next()
            for q in range(4):
                mb = mq * 4 + q
                mm(ps[:, q * 128:(q + 1) * 128], Y[:, mb * 128:(mb + 1) * 128], idt[:], True, True, [Yb, idb], [psb])
            dst = Y3q[:, mq * 4:(mq + 1) * 4, :, gl * 16:(gl + 1) * 16]
            src = ps[:].rearrange("p (q i c) -> p q i c", q=4, i=8)
            if mq == 0:
                dve(lambda e: e.tensor_copy(out=dst, in_=src), [psb], [Y3qb])
            else:
                S.op("act", lambda e: e.copy(out=dst, in_=src), reads=[psb], writes=[Y3qb])

        back(0)
        back(1)
        if gl == 3:
            gq = g // 4
            dview = yt_s[:, gq * 64:(gq + 1) * 64].rearrange("(mb m i) c -> m mb i c", m=128, i=8)
            for mb_ in range(8):
                C.store("sp", dview[:, mb_, :, :], Y3q[:, mb_, :, :], [Y3qb], "y3q%d" % (mb_ % 2), writes=[yt_sb])

    for g_ in range(G):
        do_group(g_)
    gather(C, S, yt_s, yt_sb, sc["ytok_g"][0], sc["ytok_g"][1], "ag_y", 1024)
    C.finish()


def phase_C1(P, l, x_src, x_srcb, sc, E):
    C = Ctx(P)
    S = C.S
    oT_g, oT_gb = sc["oT_g"]
    yt_g, yt_gb = sc["ytok_g"]
    sg_s, sg_sb = sc["sgT_s"]
    x1_s, x1_sb = sc["x1_s"]
    psrot = Rot([C.psb([128, 512], F32, name="ps%d" % i) for i in range(8)])
    cA, cAb = load_cact(C, E["cT"])
    mrot = mod_stage_rot(C)
    G1, G1b = compute_mod_bc(C, cA, cAb, E["w_ada"][l][:, 2048:3072], E["b_g1_bc"][l], mrot, psrot, "G1")
    srot = make_stage_rot(C, 2, 1024)
    wglu, wglub = load_weight_bf16(C, E["w_glu"][l], 512, 512, srot, "pool", "wglu")
    wssm, wssmb = load_weight_bf16(C, E["w_ssm_up"][l], 512, D, srot, "pool", "wssm")
    wsb, wsbb = load_weight_bf16(C, E["w_sb_up"][l], 512, D, srot, "pool", "wsb")
    wout, woutb = load_weight_bf16(C, E["w_out"][l], D, D, srot, "pool", "wout")

    def ld(name, src, shape, dt=F32):
        t, b = C.sbb(shape, dt, name=name)
        S.dma("sp", t[:], src, writes=[b], key=name)
        return t, b

    bglu, bglub = ld("bglu", E["bgluT"][l], [128, 4])
    lng, lngb = ld("lng", E["ln1g_bc"][l], [128, D])
    lnb, lnbb = ld("lnb", E["ln1b_bc"][l], [128, D])
    idt, idb = ld("idt", E["ident"], [128, 128])

    at_t, at_b = C.sbb([128, 4, 512], BF16, name="at_t")
    yt, ytb = C.sbb([128, 4, 512], F32, name="yt_t")
    sg_t, sg_b = C.sbb([128, 16, 512], BF16, name="sg_t")
    xt = C.sb([128, 4, D], name="x_t")
    xb = [S.buf() for _ in range(4)]
    yg = C.sb([128, 4, 512], F32, name="yg")
    ygB = [S.buf() for _ in range(4)]
    ygb = C.sb([128, 4, 512], BF16, name="ygb")
    ygbB = [S.buf() for _ in range(4)]
    yglu = C.sb([128, 4, 512], BF16, name="yglu")
    ygluB = [S.buf() for _ in range(4)]
    mT = C.sb([128, 8, 512], BF16, name="mT")
    mTB = [S.buf() for _ in range(8)]
    tmps = Rot([C.sbb([128, 512], F32, name="tmp%d" % i) for i in range(4)])
    lntmp = ln_tmp(C)

    def group(tg):
        tsl = slice(tg * 512, (tg + 1) * 512)
        for src in range(2):
            for k2 in range(2):
                S.dma("sp", at_t[:, src * 2 + k2, :],
                      (lambda r, src=src, k2=k2: oT_g[r * 2 + k2][src * 128:(src + 1) * 128, tg * 512:(tg + 1) * 512]),
                      reads=[oT_gb], writes=[at_b], key="at_t")
        for src in range(2):
            S.dma("sp", yt[:, :, src * 256:(src + 1) * 256],
                  (lambda r, src=src: yt_g[r * 4 + tg // 2][src * 1024 + (tg % 2) * 512:src * 1024 + (tg % 2 + 1) * 512, :].rearrange("(i p) c -> p i c", p=128)),
                  reads=[yt_gb], writes=[ytb], key="yt_t")
        S.dma("sp", sg_t[:], sg_s[:, tsl].rearrange("(k p) t -> p k t", p=128), reads=[sg_sb], writes=[sg_b], key="sg_t")
        S.dma("sp", xt[:], x_src[tsl, :].rearrange("(i p) d -> p i d", p=128), reads=[x_srcb], writes=xb, key="x_t")
        for cc in range(4):
            ps, psb = psrot.next()
            for i in range(4):
                S.op("pe", (lambda e, ps=ps, i=i, cc=cc: e.matmul(out=ps[:, i * 128:(i + 1) * 128], lhsT=yt[:, i, cc * 128:(cc + 1) * 128],
                                                               rhs=idt[:], start=True, stop=True)), reads=[ytb, idb], writes=[psb])
            S.op("act", (lambda e, ps=ps, cc=cc: e.copy(out=yg[:, cc, :], in_=ps[:])), reads=[psb], writes=[ygB[cc]])
            S.op("pool", (lambda e, cc=cc: e.tensor_copy(out=ygb[:, cc, :], in_=yg[:, cc, :])), reads=[ygB[cc]], writes=[ygbB[cc]])
        for oc in range(4):
            ps, psb = psrot.next()
            for kc in range(4):
                S.op("pe", (lambda e, ps=ps, oc=oc, kc=kc: e.matmul(out=ps[:], lhsT=wglu[:, kc, oc * 128:(oc + 1) * 128], rhs=ygb[:, kc, :],
                                                                start=(kc == 0), stop=(kc == 3))), reads=[wglub[kc], ygbB[kc]], writes=[psb])
            tmp, tmpb = tmps.next()
            S.op("act", (lambda e, ps=ps, oc=oc, tmp=tmp: e.activation(out=tmp[:], in_=ps[:], func=AF.Sigmoid, bias=bglu[:, oc:oc + 1])),
                 reads=[psb, bglub], writes=[tmpb])
            S.op("dve", (lambda e, oc=oc, tmp=tmp: e.tensor_tensor(out=yglu[:, oc, :], in0=yg[:, oc, :], in1=tmp[:], op=ALU.mult)),
                 reads=[ygB[oc], tmpb], writes=[ygluB[oc]])
        for fo in range(8):
            ps1, ps1b = psrot.next()
            ps2, ps2b = psrot.next()
            for kc in range(4):
                S.op("pe", (lambda e, ps1=ps1, fo=fo, kc=kc: e.matmul(out=ps1[:], lhsT=wsb[:, kc, fo * 128:(fo + 1) * 128], rhs=at_t[:, kc, :],
                                                                  start=(kc == 0), stop=(kc == 3))), reads=[wsbb[kc], at_b], writes=[ps1b])
            for kc in range(4):
                S.op("pe", (lambda e, ps2=ps2, fo=fo, kc=kc: e.matmul(out=ps2[:], lhsT=wssm[:, kc, fo * 128:(fo + 1) * 128], rhs=yglu[:, kc, :],
                                                                  start=(kc == 0), stop=(kc == 3))), reads=[wssmb[kc], ygluB[kc]], writes=[ps2b])
            t1, t1b = tmps.next()
            t2, t2b = tmps.next()
            S.op("dve", (lambda e, ps1=ps1, fo=fo, t1=t1: e.tensor_tensor(out=t1[:], in0=ps1[:], in1=sg_t[:, fo, :], op=ALU.mult)),
                 reads=[ps1b, sg_b], writes=[t1b])
            S.op("dve", (lambda e, ps2=ps2, fo=fo, t2=t2: e.tensor_tensor(out=t2[:], in0=ps2[:], in1=sg_t[:, 8 + fo, :], op=ALU.mult)),
                 reads=[ps2b, sg_b], writes=[t2b])
            S.op("pool", (lambda e, fo=fo, t1=t1, t2=t2: e.tensor_tensor(out=mT[:, fo, :], in0=t1[:], in1=t2[:], op=ALU.add)),
                 reads=[t1b, t2b], writes=[mTB[fo]])
        for i in range(4):
            for hf in range(2):
                ps, psb = psrot.next()
                for fc in range(8):
                    S.op("pe", (lambda e, ps=ps, i=i, hf=hf, fc=fc: e.matmul(out=ps[:], lhsT=mT[:, fc, i * 128:(i + 1) * 128],
                                                                         rhs=wout[:, fc, hf * 512:(hf + 1) * 512],
                                                                         start=(fc == 0), stop=(fc == 7))), reads=[mTB[fc], woutb[fc]], writes=[psb])
                tmp, tmpb = tmps.next()
                S.op("dve", (lambda e, ps=ps, hf=hf, tmp=tmp: e.tensor_tensor(out=tmp[:], in0=ps[:], in1=G1[:, hf * 512:(hf + 1) * 512], op=ALU.mult)),
                     reads=[psb, G1b], writes=[tmpb])
                S.op("dve", (lambda e, i=i, hf=hf, tmp=tmp: e.scalar_tensor_tensor(out=xt[:, i, hf * 512:(hf + 1) * 512],
                                                                                in0=xt[:, i, hf * 512:(hf + 1) * 512], scalar=ALPHA, in1=tmp[:],
                                                                                op0=ALU.mult, op1=ALU.add)), reads=[xb[i], tmpb], writes=[xb[i]])
        layer_norm_group(C, xt, xb, lntmp, 4)
        for i in range(4):
            S.op("pool", (lambda e, i=i: e.tensor_tensor(out=xt[:, i, :], in0=xt[:, i, :], in1=lng[:], op=ALU.mult)), reads=[xb[i], lngb], writes=[xb[i]])
            S.op("pool", (lambda e, i=i: e.tensor_tensor(out=xt[:, i, :], in0=xt[:, i, :], in1=lnb[:], op=ALU.add)), reads=[xb[i], lnbb], writes=[xb[i]])
            r0 = (tg * 4 + i) * 128
            C.store("sp", x1_s[r0:r0 + 128, :], xt[:, i, :], [xb[i]], "x1o%d" % i, writes=[x1_sb])

    for tg in range(TOK // 512):
        group(tg)
    C.finish()


def phase_C2(P, l, sc, E, dst, dstb):
    C = Ctx(P)
    S = C.S
    FH = 2816
    x1_s, x1_sb = sc["x1_s"]
    psrot = Rot([C.psb([128, 512], F32, name="ps%d" % i) for i in range(7)])
    pm, pmb = C.psb([128, 512], F32, name="pm")
    cA, cAb = load_cact(C, E["cT"])
    mrot = mod_stage_rot(C)
    modsb, modb = compute_mod_T2(C, cA, cAb, E["w_ada"][l][:, 3072:5120], E["b_fT"][l], 2048, pm, pmb, mrot)
    S.op("dve", lambda e: e.tensor_scalar(out=modsb[:, 8:16], in0=modsb[:, 8:16], scalar1=1.0, scalar2=None, op0=ALU.add),
         reads=[modb], writes=[modb])
    G2, G2b = compute_mod_bc(C, cA, cAb, E["w_ada"][l][:, 5120:6144], E["b_g2_bc"][l], mrot, psrot, "G2")
    srot = make_stage_rot(C, 2, 1024)
    w1, w1b = load_weight_bf16(C, E["w_ffn_in"][l], D, 2 * FH, srot, "pool", "w1")
    w2, w2b = load_weight_bf16(C, E["w_ffn_out"][l], FH, D, srot, "pool", "w2")
    idt, idb = C.sbb([128, 128], name="idt")
    S.dma("sp", idt[:], E["ident"], writes=[idb], key="idt")
    lng, lngb = C.sbb([128, D], name="lng")
    lnb, lnbb = C.sbb([128, D], name="lnb")
    S.dma("sp", lng[:], E["ln2g_bc"][l], writes=[lngb], key="lng")
    S.dma("sp", lnb[:], E["ln2b_bc"][l], writes=[lnbb], key="lnb")

    NT = 2
    xa = C.sb([128, NT, D], name="xa")
    xab = [S.buf() for _ in range(NT)]
    xr = C.sb([128, NT, D], name="xr")
    xrb = [S.buf() for _ in range(NT)]
    hT = C.sb([128, 8, NT * 128], BF16, name="hT")
    hTB = [S.buf() for _ in range(8)]
    aT = C.sb([128, 22, NT * 128], BF16, name="aT")
    aTB = [S.buf() for _ in range(11)]
    tmps = Rot([C.sbb([128, 512], F32, name="tmp%d" % i) for i in range(2)])
    lntmp_a = ln_tmp(C, NT)
    lntmp_r = ln_tmp(C, NT)
    wr = [dstb] if dstb is not None else []
    W = NT * 128

    def tile(tt_):
        rows = slice(tt_ * W, (tt_ + 1) * W)
        S.dma("sp", xa[:], x1_s[rows, :].rearrange("(i p) d -> p i d", p=128), reads=[x1_sb], writes=xab, key="xa")
        S.dma("sp", xr[:], x1_s[rows, :].rearrange("(i p) d -> p i d", p=128), reads=[x1_sb], writes=xrb, key="xr")
        layer_norm_group(C, xa, xab, lntmp_a, NT)
        for g4 in range(4):
            pt, ptb = psrot.next()
            for q in range(2):
                fc = g4 * 2 + q
                for i in range(NT):
                    S.op("pe", (lambda e, pt=pt, q=q, fc=fc, i=i: e.matmul(out=pt[:, q * W + i * 128:q * W + (i + 1) * 128],
                                                                        lhsT=xa[:, i, fc * 128:(fc + 1) * 128],
                                                                        rhs=idt[:], start=True, stop=True)), reads=[xab[i], idb], writes=[ptb])
            for q in range(2):
                fc = g4 * 2 + q
                S.op("act", (lambda e, pt=pt, q=q, fc=fc: e.activation(out=hT[:, fc, :], in_=pt[:, q * W:(q + 1) * W], func=AF.Identity,
                                                                    scale=modsb[:, 8 + fc:9 + fc], bias=modsb[:, fc:fc + 1])),
                     reads=[ptb, modb], writes=[hTB[fc]])
        for jb in range(11):
            psg, psgb = psrot.next()
            psu, psub = psrot.next()
            for q in range(2):
                j = jb * 2 + q
                for fc in range(8):
                    S.op("pe", (lambda e, psg=psg, q=q, j=j, fc=fc: e.matmul(out=psg[:, q * W:(q + 1) * W], lhsT=w1[:, fc, j * 128:(j + 1) * 128],
                                                                         rhs=hT[:, fc, :], start=(fc == 0), stop=(fc == 7))),
                         reads=[w1b[fc], hTB[fc]], writes=[psgb])
                for fc in range(8):
                    S.op("pe", (lambda e, psu=psu, q=q, j=j, fc=fc: e.matmul(out=psu[:, q * W:(q + 1) * W],
                                                                         lhsT=w1[:, fc, FH + j * 128:FH + (j + 1) * 128],
                                                                         rhs=hT[:, fc, :], start=(fc == 0), stop=(fc == 7))),
                         reads=[w1b[fc], hTB[fc]], writes=[psub])
            tmp, tmpb = tmps.next()
            S.op("act", (lambda e, psg=psg, tmp=tmp: e.activation(out=tmp[:], in_=psg[:], func=AF.Silu)), reads=[psgb], writes=[tmpb])
            S.op("dve", (lambda e, psu=psu, tmp=tmp, jb=jb: e.tensor_tensor(
                out=aT[:, jb * 2:jb * 2 + 2, :].rearrange("p a b -> p (a b)"), in0=tmp[:], in1=psu[:], op=ALU.mult)),
                reads=[tmpb, psub], writes=[aTB[jb]])
        for i in range(NT):
            for hf in range(2):
                ps, psb = psrot.next()
                for kc in range(22):
                    S.op("pe", (lambda e, ps=ps, hf=hf, kc=kc, i=i: e.matmul(out=ps[:], lhsT=aT[:, kc, i * 128:(i + 1) * 128],
                                                                         rhs=w2[:, kc, hf * 512:(hf + 1) * 512],
                                                                         start=(kc == 0), stop=(kc == 21))), reads=[aTB[kc // 2], w2b[kc]], writes=[psb])
                tmp, tmpb = tmps.next()
                S.op("dve", (lambda e, ps=ps, hf=hf, tmp=tmp: e.tensor_tensor(out=tmp[:], in0=ps[:], in1=G2[:, hf * 512:(hf + 1) * 512], op=ALU.mult)),
                     reads=[psb, G2b], writes=[tmpb])
                S.op("dve", (lambda e, hf=hf, tmp=tmp, i=i: e.scalar_tensor_tensor(out=xr[:, i, hf * 512:(hf + 1) * 512],
                                                                                in0=xr[:, i, hf * 512:(hf + 1) * 512],
                                                                                scalar=ALPHA, in1=tmp[:], op0=ALU.mult, op1=ALU.add)),
                     reads=[xrb[i], tmpb], writes=[xrb[i]])
        layer_norm_group(C, xr, xrb, lntmp_r, NT)
        for i in range(NT):
            S.op("pool", (lambda e, i=i: e.tensor_tensor(out=xr[:, i, :], in0=xr[:, i, :], in1=lng[:], op=ALU.mult)), reads=[xrb[i], lngb], writes=[xrb[i]])
            S.op("pool", (lambda e, i=i: e.tensor_tensor(out=xr[:, i, :], in0=xr[:, i, :], in1=lnb[:], op=ALU.add)), reads=[xrb[i], lnbb], writes=[xrb[i]])
            r0 = tt_ * W + i * 128
            C.store("sp", dst[r0:r0 + 128, :], xr[:, i, :], [xrb[i]], "outo%d" % i, writes=wr)

    for tt_ in range(TOK // W):
        tile(tt_)
    C.finish()


EXT_SPECS = None


def build_fused():
    P = Prog()
    _SCHED.append(P.S)
    FH = 2816
    specs = dict(
        x=([TOK, D], F32), cT=([128, 8], F32), w_ada=([2, D, 6144], F32), b_adaT=([2, 128, 16], F32), b_g1_bc=([2, 128, 1024], F32),
        b_fT=([2, 128, 16], F32), b_g2_bc=([2, 128, 1024], F32), w_in=([2, D, 4096], F32),
        ident=([128, 128], F32), J=([128, 128], F32), identb=([128, 128], BF16), mneg=([128, 128], BF16),
        I2=([128, 64], F32), cst=([128, 8], F32),
        are=([2, 128, 16], F32), aim=([2, 128, 16], F32), ldt=([2, 128, 16], F32), dTs=([2, 128, 16], F32),
        bA=([2, 16, 128, 16], F32), bB=([2, 16, 128, 16], F32), cTs=([2, 16, 128, 16], F32),
        bgluT=([2, 128, 4], F32), w_glu=([2, 512, 512], F32), w_ssm_up=([2, 512, D], F32), w_sb_up=([2, 512, D], F32),
        w_out=([2, D, D], F32), ln1g_bc=([2, 128, D], F32), ln1b_bc=([2, 128, D], F32),
        w_ffn_in=([2, D, 2 * FH], F32), w_ffn_out=([2, FH, D], F32), ln2g_bc=([2, 128, D], F32), ln2b_bc=([2, 128, D], F32))
    E = {k: P.din(k, shp, dt) for k, (shp, dt) in specs.items()}
    out_ext = P.dout("out", [TOK, D])
    sc = {}
    for nm, shp, dt in (("qT_s", [512, TOK], BF16), ("kTr_s", [512, TOK], BF16), ("vr_s", [2 * TOK, 256], BF16), ("utok_s", [2 * TOK, 256], F32),
                        ("sgT_s", [2048, TOK], BF16), ("oT_s", [512, TOK], BF16), ("ytok_s", [S_LEN, 256], F32),
                        ("x1_s", [TOK, D], F32), ("xs_s", [TOK, D], F32)):
        sc[nm] = P.scratch(nm, shp, dt)
    for nm, n, shp, dt in (("qT_g", 4, [256, TOK], BF16), ("kTr_g", 4, [256, TOK], BF16), ("vr_g", 4, [4096, 256], BF16),
                           ("utok_g", 8, [2048, 256], F32), ("oT_g", 4, [256, TOK], BF16), ("ytok_g", 8, [2048, 256], F32)):
        aps = [P.scratch("%s_%d" % (nm, k), shp, dt)[0] for k in range(n)]
        sc[nm] = (aps, P.S.buf(nm))
    x_extb = P.S.buf("x_ext")
    for l in range(2):
        xsrc = (E["x"], x_extb) if l == 0 else sc["xs_s"]
        phase_A(P, l, xsrc[0], xsrc[1], sc, E)
        phase_Bssm(P, l, sc, E)
        phase_Batt(P, sc, E)
        phase_C1(P, l, xsrc[0], xsrc[1], sc, E)
        if l == 0:
            phase_C2(P, l, sc, E, sc["xs_s"][0], sc["xs_s"][1])
        else:
            phase_C2(P, l, sc, E, out_ext, None)
    return P.nc


_FUSED = {}


def kernel(x, c, w_ada, b_ada, w_in, w_sb_up, ssm_a_re, ssm_a_im, ssm_log_dt,
           ssm_b_re, ssm_b_im, ssm_c_re, ssm_c_im, ssm_d, w_glu, b_glu,
           w_ssm_up, w_out, ln1_g, ln1_b, w_ffn_in, w_ffn_out, ln2_g, ln2_b):
    A = lambda a: np.asarray(a)
    x = A(x); c = A(c); w_ada = A(w_ada); b_ada = A(b_ada)
    B_, S_, D_ = x.shape
    xf = x.reshape(B_ * S_, D_)
    if "nc" not in _FUSED:
        _FUSED["nc"] = build_fused()
    minv, identb = consts_att()
    I2, cst, ident = consts_ssm()
    J = np.ascontiguousarray(ident[::-1])
    L = w_ada.shape[0]
    common = dict(
        w_ada=f32c(w_ada), w_in=f32c(A(w_in)),
        b_adaT=f32c(np.stack([b_ada[l][0:2048].reshape(16, 128).T for l in range(L)])),
        b_g1_bc=f32c(np.stack([bc128(b_ada[l][2048:3072]) for l in range(L)])),
        b_fT=f32c(np.stack([b_ada[l][3072:5120].reshape(16, 128).T for l in range(L)])),
        b_g2_bc=f32c(np.stack([bc128(b_ada[l][5120:6144]) for l in range(L)])),
        ident=ident, J=J, identb=identb, mneg=np.ascontiguousarray((minv * -30000.0).astype(NPBF16)), I2=I2, cst=cst,
        bgluT=f32c(np.stack([A(b_glu)[l].reshape(4, 128).T for l in range(L)])),
        w_glu=f32c(A(w_glu)), w_ssm_up=f32c(A(w_ssm_up)), w_sb_up=f32c(A(w_sb_up)), w_out=f32c(A(w_out)),
        ln1g_bc=f32c(np.stack([bc128(A(ln1_g)[l]) for l in range(L)])), ln1b_bc=f32c(np.stack([bc128(A(ln1_b)[l]) for l in range(L)])),
        w_ffn_in=f32c(A(w_ffn_in)), w_ffn_out=f32c(A(w_ffn_out)),
        ln2g_bc=f32c(np.stack([bc128(A(ln2_g)[l]) for l in range(L)])), ln2b_bc=f32c(np.stack([bc128(A(ln2_b)[l]) for l in range(L)])))
    are_, aim_, ldt_ = A(ssm_a_re), A(ssm_a_im), A(ssm_log_dt)
    bre_, bim_, cre_, cim_, d_ = A(ssm_b_re), A(ssm_b_im), A(ssm_c_re), A(ssm_c_im), A(ssm_d)
    maps = []
    for core in range(NCORES):
        b, hg = core // 2, core % 2
        gsl = slice(hg * 16, (hg + 1) * 16)
        m = dict(common)
        m["x"] = f32c(xf[core * TOK:(core + 1) * TOK])
        m["cT"] = f32c(c[b].reshape(8, 128).T)
        m["are"] = f32c(np.stack([np.concatenate([are_[l][gsl].T, are_[l][gsl].T], axis=0) for l in range(L)]))
        m["aim"] = f32c(np.stack([np.concatenate([aim_[l][gsl].T, aim_[l][gsl].T], axis=0) for l in range(L)]))
        m["ldt"] = f32c(np.stack([np.broadcast_to(ldt_[l][gsl][None, :], (128, 16)) for l in range(L)]))
        m["dTs"] = f32c(np.stack([np.tile(d_[l][hg * 256:(hg + 1) * 256].reshape(16, 16).T, (8, 1)) for l in range(L)]))
        m["bA"] = f32c(np.stack([np.concatenate([bre_[l][gsl], bim_[l][gsl]], axis=1) for l in range(L)]))
        m["bB"] = f32c(np.stack([np.concatenate([bim_[l][gsl], bre_[l][gsl]], axis=1) for l in range(L)]))
        m["cTs"] = f32c(np.stack([np.concatenate([cre_[l][gsl].transpose(0, 2, 1), cim_[l][gsl].transpose(0, 2, 1)], axis=1) for l in range(L)]))
        maps.append(m)
    res = run_prog(_FUSED["nc"], maps)
    out = np.concatenate([np.asarray(r["out"]) for r in res], axis=0).astype(np.float32)
    return out.reshape(B_, S_, D_)
```

```python
import numpy as np
import concourse.bass as bass
import concourse.mybir as mybir

F32 = mybir.dt.float32
BF16 = mybir.dt.bfloat16
AF = mybir.ActivationFunctionType
ALU = mybir.AluOpType
AX = mybir.AxisListType


class Buf:
    __slots__ = ("name", "w", "r")

    def __init__(self, name):
        self.name = name
        self.w = None
        self.r = {}


class Sched:
    ENG = ("pe", "act", "dve", "pool", "sp")

    def __init__(self, nc, stack):
        self.nc = nc
        self.stack = stack
        self.ops = {e: [] for e in self.ENG}
        self.cnt = {e: 0 for e in self.ENG}
        self.sem = {e: stack.enter_context(nc.semaphore("s_" + e)) for e in self.ENG}
        self.seen = {e: {} for e in self.ENG}
        self.dsems = {}
        self.keymap = {}
        self.nbuf = 0

    def buf(self, name=None):
        self.nbuf += 1
        return Buf(name or ("b%d" % self.nbuf))

    def _waits(self, eng, reads, writes, is_dma=False, dkey=None):
        deps = []
        for b in reads:
            if b.w is not None:
                deps.append((b.w, "raw"))
        for b in writes:
            if b.w is not None:
                deps.append((b.w, "waw"))
            for ev in b.r.values():
                deps.append((ev, "war"))
        waits = {}
        for (ev, kind) in deps:
            sem, val, src = ev
            if not is_dma and src == eng:
                if eng == "pe":
                    continue
                if kind in ("war", "waw"):
                    continue
            if is_dma and kind == "waw" and src == ("dma", dkey):
                continue
            key = id(sem)
            if self.seen[eng].get(key, 0) >= val:
                continue
            if key not in waits or waits[key][1] < val:
                waits[key] = (sem, val)
        for key, (sem, val) in waits.items():
            self.seen[eng][key] = val
        return list(waits.values())

    def _update(self, ev, reads, writes):
        for b in writes:
            b.w = ev
            b.r = {}
        for b in reads:
            if b in writes:
                continue
            k = id(ev[0])
            old = b.r.get(k)
            if old is None or old[1] < ev[1]:
                b.r[k] = ev

    def op(self, eng, emit, reads=(), writes=()):
        waits = self._waits(eng, reads, writes)
        self.cnt[eng] += 1
        ev = (self.sem[eng], self.cnt[eng], eng)
        self.ops[eng].append((waits, emit, (self.sem[eng], 1)))
        self._update(ev, reads, writes)
        return ev

    def new_phase(self):
        self.keymap = {}

    def dsem(self, key):
        if key not in self.dsems:
            self.dsems[key] = [self.stack.enter_context(self.nc.semaphore("d_%d" % len(self.dsems))), 0]
        return self.dsems[key]

    def dma(self, q, out_ap, in_ap, reads=(), writes=(), key=None, **kw):
        assert key is not None
        key = self.keymap.setdefault(key, "k%d" % len(self.keymap))
        waits = self._waits(q, reads, writes, is_dma=True, dkey=key)
        ds = self.dsem(key)
        ds[1] += 16
        ev = (ds[0], ds[1], ("dma", key))
        if callable(in_ap):
            sem_ = ds[0]

            def fn(e):
                rk = self.rank(e)
                with e.If(rk == 0):
                    e.dma_start(out=out_ap, in_=in_ap(0), **kw).then_inc(sem_, 16)
                with e.Else():
                    e.dma_start(out=out_ap, in_=in_ap(1), **kw).then_inc(sem_, 16)
                return None
            self.ops[q].append((waits, fn, (ds[0], 16)))
        else:
            self.ops[q].append((waits, (lambda e: e.dma_start(out=out_ap, in_=in_ap, **kw)), (ds[0], 16)))
        self._update(ev, reads, writes)
        return ev

    def coll(self, kind, in_ap, out_ap, groups, reads=(), writes=(), key=None):
        key = "cc_" + key
        waits = self._waits("pool", reads, writes, is_dma=True, dkey=key)
        ds = self.dsem(key)
        ds[1] += 1
        ev = (ds[0], ds[1], ("dma", key))
        self.ops["pool"].append((waits, (lambda e: e.collective_compute(kind, ALU.bypass, replica_groups=groups,
                                                                         ins=[in_ap.opt()], outs=[out_ap.opt()])), (ds[0], 1)))
        self._update(ev, reads, writes)
        return ev

    def wait_events(self, eng, evs):
        waits = {}
        for ev in evs:
            sem, val, src = ev
            key = id(sem)
            if self.seen[eng].get(key, 0) >= val:
                continue
            if key not in waits or waits[key][1] < val:
                waits[key] = (sem, val)
        for key, (sem, val) in waits.items():
            self.seen[eng][key] = val
        self.ops[eng].append((list(waits.values()), None, None))

    def rank(self, e):
        k = id(e)
        if k not in self._rank_cache:
            self._rank_cache[k] = e.partition_id() % 2
        return self._rank_cache[k]

    def _emit_eng(self, name, e):
        self._rank_cache = {}
        for waits, fn, inc in self.ops[name]:
            for sem, val in waits:
                e.wait_ge(sem, val)
            if fn is None:
                continue
            ins = fn(e)
            if ins is not None:
                ins.then_inc(inc[0], inc[1])

    def emit(self):
        with self.nc.Block() as block:
            @block.tensor
            def _(e):
                self._emit_eng("pe", e)

            @block.scalar
            def _(e):
                self._emit_eng("act", e)

            @block.vector
            def _(e):
                self._emit_eng("dve", e)

            @block.gpsimd
            def _(e):
                self._emit_eng("pool", e)

            @block.sync
            def _(e):
                self._emit_eng("sp", e)
        self.ops = {e: [] for e in self.ENG}


from contextlib import ExitStack
from concourse.bass_utils import run_bass_kernel_spmd
import ml_dtypes

NPBF16 = ml_dtypes.bfloat16
NCORES = 8
S_LEN = 8192
TOK = 4096
D = 1024
LN_EPS = 1e-5
ALPHA = (2 * 2) ** 0.25


class Rot:
    def __init__(self, items):
        self.items = items
        self.i = 0

    def next(self):
        it = self.items[self.i % len(self.items)]
        self.i += 1
        return it


class Prog:
    def __init__(self):
        self.nc = bass.Bass("TRN2", target_bir_lowering=False)
        self.gst = ExitStack()
        self.S = Sched(self.nc, self.gst)
        self.nphase = 0
        self.ext = {}

    def din(self, name, shape, dt=F32):
        if name not in self.ext:
            self.ext[name] = self.nc.dram_tensor(name, list(shape), dt, kind="ExternalInput").ap()
        return self.ext[name]

    def dout(self, name, shape, dt=F32):
        return self.nc.dram_tensor(name, list(shape), dt, kind="ExternalOutput").ap()

    def scratch(self, name, shape, dt=F32):
        t = self.nc.dram_tensor(name, list(shape), dt, kind="Internal").ap()
        return t, self.S.buf(name)


class Ctx:
    def __init__(self, prog=None):
        self.prog = prog
        if prog is None:
            self.nc = bass.Bass("TRN2", target_bir_lowering=False)
            self.st = ExitStack()
            self.S = Sched(self.nc, self.st)
            self.tag = ""
        else:
            self.nc = prog.nc
            self.st = ExitStack()
            self.S = prog.S
            self.S.new_phase()
            prog.nphase += 1
            self.tag = "f%d_" % prog.nphase
        self.outs = []
        self.n = 0

    def din(self, name, shape, dt=F32):
        return self.nc.dram_tensor(name, list(shape), dt, kind="ExternalInput").ap()

    def dout(self, name, shape, dt=F32):
        return self.nc.dram_tensor(name, list(shape), dt, kind="ExternalOutput").ap()

    def sb(self, shape, dt=F32, name=None):
        self.n += 1
        name = "sb_" + self.tag + (name or ("t%d" % self.n))
        return self.st.enter_context(self.nc.sbuf_tensor(name, list(shape), dt))

    def ps(self, shape, dt=F32, name=None):
        self.n += 1
        name = "ps_" + self.tag + (name or ("p%d" % self.n))
        return self.st.enter_context(self.nc.psum_tensor(name, list(shape), dt))

    def sbb(self, shape, dt=F32, name=None):
        t = self.sb(shape, dt, name)
        return t, self.S.buf(name)

    def psb(self, shape, dt=F32, name=None):
        t = self.ps(shape, dt, name)
        return t, self.S.buf(name)

    def store(self, q, out_ap, in_ap, reads, key, writes=()):
        ev = self.S.dma(q, out_ap, in_ap, reads=reads, writes=list(writes), key=key)
        self.outs.append(ev)
        return ev

    def finish(self):
        self.S.wait_events("sp", self.outs)
        self.S.emit()
        self.st.close()
        return self.nc


def layer_norm_group(C, xt, xb, tmp, n=4):
    S = C.S
    st, stb, mv, mvb, rs, rsb = tmp
    for i in range(n):
        S.op("dve", (lambda e, i=i: e.bn_stats(out=st[:, i, 0, :], in_=xt[:, i, 0:512])), reads=[xb[i]], writes=[stb])
        S.op("dve", (lambda e, i=i: e.bn_stats(out=st[:, i, 1, :], in_=xt[:, i, 512:1024])), reads=[xb[i]], writes=[stb])
        S.op("dve", (lambda e, i=i: e.bn_aggr(out=mv[:, i, :], in_=st[:, i, :, :].rearrange("p a b -> p (a b)"))), reads=[stb], writes=[mvb])
    S.op("dve", lambda e: e.tensor_scalar(out=rs[:, 0:n], in0=mv[:, 0:n, 1], scalar1=LN_EPS, scalar2=None, op0=ALU.add),
         reads=[mvb], writes=[rsb])
    S.op("act", lambda e: e.activation(out=rs[:, 0:n], in_=rs[:, 0:n], func=AF.Sqrt), reads=[rsb], writes=[rsb])
    S.op("dve", lambda e: e.reciprocal(out=rs[:, 0:n], in_=rs[:, 0:n]), reads=[rsb], writes=[rsb])
    for i in range(n):
        S.op("dve", (lambda e, i=i: e.tensor_scalar(out=xt[:, i, :], in0=xt[:, i, :], scalar1=mv[:, i, 0:1], scalar2=rs[:, i:i + 1],
                                                    op0=ALU.subtract, op1=ALU.mult)), reads=[xb[i], mvb, rsb], writes=[xb[i]])


def ln_tmp(C, n=4):
    st, stb = C.sbb([128, n, 2, 6])
    mv, mvb = C.sbb([128, n, 2])
    rs, rsb = C.sbb([128, n])
    return (st, stb, mv, mvb, rs, rsb)


def load_weight_bf16(C, w_dram, rows, cols, stage_rot, conv_eng="pool", name="w"):
    S = C.S
    kt = rows // 128
    wt = C.sb([128, kt, cols], BF16, name=name)
    bufs = [S.buf("%s_%d" % (name, k)) for k in range(kt)]
    CW = 2048
    for k in range(kt):
        for c0 in range(0, cols, CW):
            cw = min(CW, cols - c0)
            S.dma("pool", wt[:, k, c0:c0 + cw], w_dram[k * 128:(k + 1) * 128, c0:c0 + cw], writes=[bufs[k]], key=name)
    return wt, bufs


def make_stage_rot(C, n=2, width=2048, name="wstg"):
    items = []
    for i in range(n):
        t, b = C.sbb([128, width], F32, name="%s%d" % (name, i))
        items.append((t, b, "%s%d" % (name, i)))
    return Rot(items)


def load_cact(C, cT_d):
    S = C.S
    cT, cTb = C.sbb([128, 8], name="c_T")
    cA, cAb = C.sbb([128, 8], name="c_A")
    S.dma("sp", cT[:], cT_d, writes=[cTb], key="c_T")
    S.op("act", lambda e: e.activation(out=cA[:], in_=cT[:], func=AF.Silu), reads=[cTb], writes=[cAb])
    return cA, cAb


def compute_mod_T2(C, cA, cAb, w_d, bT_d, ncols, pm, pmb, stg_rot):
    S = C.S
    nj = ncols // 128
    bT, bTb = C.sbb([128, nj], name="b_T")
    modsb, modb = C.sbb([128, nj], name="mod_sb")
    S.dma("sp", bT[:], bT_d, writes=[bTb], key="b_T")
    for j in range(nj):
        t, b, key = stg_rot.next()
        S.dma("sp", t[:], w_d[:, j * 128:(j + 1) * 128].rearrange("(k p) c -> p k c", p=128), writes=[b], key=key)
        for kc in range(8):
            S.op("pe", (lambda e, t=t, j=j, kc=kc: e.matmul(out=pm[:, j:j + 1], lhsT=t[:, kc, :], rhs=cA[:, kc:kc + 1],
                                                            start=(kc == 0), stop=(kc == 7))), reads=[b, cAb], writes=[pmb])
    S.op("dve", lambda e: e.tensor_tensor(out=modsb[:], in0=pm[:, 0:nj], in1=bT[:], op=ALU.add), reads=[pmb, bTb], writes=[modb])
    return modsb, modb


def compute_mod_bc(C, cA, cAb, w_d, b_bc_d, stg_rot, psrot, name):
    S = C.S
    G, Gb = C.sbb([128, 1024], name=name)
    bb, bbb = C.sbb([128, 1024], name=name + "_b")
    ones, onesb = C.sbb([128, 128], name=name + "_ones")
    CA, CAb = C.sbb([128, 8, 128], name=name + "_CA")
    S.dma("sp", bb[:], b_bc_d, writes=[bbb], key=name + "_b")
    S.op("pool", lambda e: e.memset(ones[:], 1.0), writes=[onesb])
    for kc in range(8):
        S.op("dve", (lambda e, kc=kc: e.tensor_scalar(out=CA[:, kc, :], in0=ones[:], scalar1=cA[:, kc:kc + 1], scalar2=None, op0=ALU.mult)),
             reads=[onesb, cAb], writes=[CAb])
    pss = [psrot.next() for _ in range(2)]
    for j in range(8):
        t, b, key = stg_rot.next()
        S.dma("sp", t[:], w_d[:, j * 128:(j + 1) * 128].rearrange("(k p) c -> p k c", p=128), writes=[b], key=key)
        ps, psb = pss[j // 4]
        for kc in range(8):
            S.op("pe", (lambda e, t=t, j=j, kc=kc, ps=ps: e.matmul(out=ps[:, (j % 4) * 128:(j % 4 + 1) * 128], lhsT=CA[:, kc, :], rhs=t[:, kc, :],
                                                                   start=(kc == 0), stop=(kc == 7))), reads=[b, CAb], writes=[psb])
    for hf in range(2):
        ps, psb = pss[hf]
        S.op("dve", (lambda e, ps=ps, hf=hf: e.scalar_tensor_tensor(out=G[:, hf * 512:(hf + 1) * 512], in0=ps[:], scalar=1.0,
                                                                    in1=bb[:, hf * 512:(hf + 1) * 512], op0=ALU.add, op1=ALU.add)),
             reads=[psb, bbb], writes=[Gb])
    return G, Gb


def mod_stage_rot(C):
    items = []
    for i in range(2):
        t, b = C.sbb([128, 8, 128], F32, name="mstg%d" % i)
        items.append((t, b, "mstg%d" % i))
    return Rot(items)


def ln_affine_store(C, xt, xb, n, lng, lngb, lnb, lnbb, lntmp, out_rows, keyp):
    S = C.S
    layer_norm_group(C, xt, xb, lntmp, n)
    for i in range(n):
        S.op("pool", (lambda e, i=i: e.tensor_tensor(out=xt[:, i, :], in0=xt[:, i, :], in1=lng[:], op=ALU.mult)), reads=[xb[i], lngb], writes=[xb[i]])
        S.op("pool", (lambda e, i=i: e.tensor_tensor(out=xt[:, i, :], in0=xt[:, i, :], in1=lnb[:], op=ALU.add)), reads=[xb[i], lnbb], writes=[xb[i]])
        C.store("sp", out_rows(i), xt[:, i, :], [xb[i]], "%s%d" % (keyp, i))


def consts_att():
    p = np.arange(128)[:, None]
    rr = np.arange(128)[None, :]
    minv = (rr <= 127 - p).astype(np.float32)
    identb = np.eye(128, dtype=np.float32).astype(NPBF16)
    return minv, identb


def consts_ssm():
    I2 = np.concatenate([np.eye(64), np.eye(64)], axis=0).astype(np.float32)
    top = (np.arange(128) < 64).astype(np.float32)
    bot = 1.0 - top
    cst = np.stack([top, bot, -top, -bot, top - bot, bot - top, np.full(128, -np.pi, np.float32), np.zeros(128, np.float32)], axis=1)
    return I2, f32c(cst), np.eye(128, dtype=np.float32)


def run_prog(nc, in_maps):
    res = run_bass_kernel_spmd(nc, in_maps, core_ids=list(range(NCORES)))
    return res.results


def f32c(a):
    return np.ascontiguousarray(a, dtype=np.float32)


def bc128(v):
    return f32c(np.broadcast_to(np.asarray(v)[None, :], (128, v.shape[0])))


PAIRS = [[0, 1], [2, 3], [4, 5], [6, 7]]
I32 = mybir.dt.int32
TWO_PI = 6.283185307179586
NGRP = 16


_SCHED = []


def rank_of(e):
    return _SCHED[-1].rank(e)


def gather(C, S, src, srcb, dsts, dstb, key, rows, block=False):
    S.wait_events("pool", C.outs)
    R = src.shape[0]
    assert R % rows == 0 and len(dsts) == R // rows
    for k in range(R // rows):
        ev = S.coll("AllGather", src[k * rows:(k + 1) * rows, :], dsts[k], PAIRS, reads=[srcb], writes=[dstb], key=key)
        if block:
            C.outs.append(ev)


def phase_A(P, l, x_src, x_srcb, sc, E):
    C = Ctx(P)
    S = C.S
    idt, idb = C.sbb([128, 128], name="idt")
    jt, jb = C.sbb([128, 128], name="jt")
    S.dma("sp", idt[:], E["ident"], writes=[idb], key="idt")
    S.dma("sp", jt[:], E["J"], writes=[jb], key="jt")
    pm, pmb = C.psb([128, 512], name="pm")
    cA, cAb = load_cact(C, E["cT"])
    mrot = mod_stage_rot(C)
    modsb, modb = compute_mod_T2(C, cA, cAb, E["w_ada"][l][:, 0:2048], E["b_adaT"][l], 2048, pm, pmb, mrot)
    S.op("dve", lambda e: e.tensor_scalar(out=modsb[:, 8:16], in0=modsb[:, 8:16], scalar1=1.0, scalar2=None, op0=ALU.add),
         reads=[modb], writes=[modb])
    srot = make_stage_rot(C, 2, 2048)
    wbf, wb = load_weight_bf16(C, E["w_in"][l], D, 4096, srot, "pool", "w_in")

    xrot = Rot([(C.sb([128, 4, D], name="xt%d" % i), [S.buf() for _ in range(4)], "xt%d" % i) for i in range(2)])
    hrot = Rot([(C.sb([128, 8, 512], BF16, name="hT%d" % i), [S.buf() for _ in range(8)]) for i in range(2)])
    hrrot = Rot([(C.sb([128, 8, 512], BF16, name="hR%d" % i), [S.buf() for _ in range(8)]) for i in range(2)])
    lntmps = Rot([ln_tmp(C) for _ in range(2)])
    ptrot = Rot([C.psb([128, 512], name="ptr%d" % i) for i in range(3)])
    pprot = Rot([C.psb([128, 512], name="pp%d" % i) for i in range(4)])
    sbf = Rot([C.sbb([128, 512], BF16, name="sbf%d" % i) + ("sbf%d" % i,) for i in range(4)])
    sf32 = Rot([C.sbb([128, 512], F32, name="sf%d" % i) + ("sf%d" % i,) for i in range(2)])
    qT_s, qT_sb = sc["qT_s"]
    kTr_s, kTr_sb = sc["kTr_s"]
    vr_s, vr_sb = sc["vr_s"]
    ut_s, ut_sb = sc["utok_s"]
    sg_s, sg_sb = sc["sgT_s"]
    if "utok_slabs" not in sc:
        sc["utok_slabs"] = [S.buf() for _ in range(8)]
    ut_slabs = sc["utok_slabs"]

    def group(tg):
        xt, xb, xkey = xrot.next()
        hT, hb = hrot.next()
        hR, hRb = hrrot.next()
        S.dma("sp", xt[:], x_src[tg * 512:(tg + 1) * 512, :].rearrange("(i p) d -> p i d", p=128), reads=[x_srcb], writes=xb, key=xkey)
        layer_norm_group(C, xt, xb, lntmps.next())
        for fc in range(8):
            pt, ptb = ptrot.next()
            for i in range(4):
                S.op("pe", (lambda e, pt=pt, i=i, fc=fc: e.matmul(out=pt[:, i * 128:(i + 1) * 128], lhsT=xt[:, i, fc * 128:(fc + 1) * 128],
                                                               rhs=idt[:], start=True, stop=True)), reads=[xb[i], idb], writes=[ptb])
            S.op("act", (lambda e, pt=pt, fc=fc: e.activation(out=hT[:, fc, :], in_=pt[:], func=AF.Identity,
                                                             scale=modsb[:, 8 + fc:9 + fc], bias=modsb[:, fc:fc + 1])),
                 reads=[ptb, modb], writes=[hb[fc]])
            pt2, pt2b = ptrot.next()
            for i in range(4):
                S.op("pe", (lambda e, pt2=pt2, i=i, fc=fc: e.matmul(out=pt2[:, (3 - i) * 128:(4 - i) * 128], lhsT=xt[:, i, fc * 128:(fc + 1) * 128],
                                                                 rhs=jt[:], start=True, stop=True)), reads=[xb[i], jb], writes=[pt2b])
            S.op("dve", (lambda e, pt2=pt2, fc=fc: e.tensor_scalar(out=hR[:, fc, :], in0=pt2[:], scalar1=modsb[:, 8 + fc:9 + fc],
                                                                  scalar2=modsb[:, fc:fc + 1], op0=ALU.mult, op1=ALU.add)),
                 reads=[pt2b, modb], writes=[hRb[fc]])
        tsl = slice(tg * 512, (tg + 1) * 512)
        rbase = TOK - (tg + 1) * 512
        rsl = slice(rbase, rbase + 512)
        for oc in list(range(0, 8)) + list(range(16, 32)):
            pp, ppb = pprot.next()
            src, srcb = (hR, hRb) if 4 <= oc < 8 else (hT, hb)
            for fc in range(8):
                S.op("pe", (lambda e, pp=pp, fc=fc, oc=oc, src=src: e.matmul(out=pp[:], lhsT=wbf[:, fc, oc * 128:(oc + 1) * 128],
                                                                        rhs=src[:, fc, :], start=(fc == 0), stop=(fc == 7))),
                     reads=[wb[fc], srcb[fc]], writes=[ppb])
            stg, stgb, key = sbf.next()
            if oc < 8:
                S.op("dve", (lambda e, stg=stg, pp=pp: e.tensor_copy(out=stg[:], in_=pp[:])), reads=[ppb], writes=[stgb])
                if oc < 4:
                    C.store("pool", qT_s[oc * 128:(oc + 1) * 128, tsl], stg[:], [stgb], key, writes=[qT_sb])
                else:
                    C.store("pool", kTr_s[(oc - 4) * 128:(oc - 3) * 128, rsl], stg[:], [stgb], key, writes=[kTr_sb])
            else:
                S.op("act", (lambda e, stg=stg, pp=pp: e.activation(out=stg[:], in_=pp[:], func=AF.Sigmoid)), reads=[ppb], writes=[stgb])
                C.store("pool", sg_s[(oc - 16) * 128:(oc - 15) * 128, tsl], stg[:], [stgb], key, writes=[sg_sb])
        for i in range(4):
            pp, ppb = pprot.next()
            for fc in range(8):
                S.op("pe", (lambda e, pp=pp, fc=fc, i=i: e.matmul(out=pp[:], lhsT=hR[:, fc, i * 128:(i + 1) * 128],
                                                               rhs=wbf[:, fc, 1024:1536], start=(fc == 0), stop=(fc == 7))),
                     reads=[wb[fc], hRb[fc]], writes=[ppb])
            stg, stgb, key = sbf.next()
            S.op("dve", (lambda e, stg=stg, pp=pp: e.tensor_copy(out=stg[:], in_=pp[:])), reads=[ppb], writes=[stgb])
            for hg in range(2):
                C.store("pool", vr_s[hg * TOK + rbase + i * 128:hg * TOK + rbase + (i + 1) * 128, :], stg[:, hg * 256:(hg + 1) * 256],
                        [stgb], key, writes=[vr_sb])
        for i in range(4):
            pp, ppb = pprot.next()
            for fc in range(8):
                S.op("pe", (lambda e, pp=pp, fc=fc, i=i: e.matmul(out=pp[:], lhsT=hT[:, fc, i * 128:(i + 1) * 128],
                                                               rhs=wbf[:, fc, 1536:2048], start=(fc == 0), stop=(fc == 7))),
                     reads=[wb[fc], hb[fc]], writes=[ppb])
            stg, stgb, key = sf32.next()
            S.op("dve", (lambda e, stg=stg, pp=pp: e.tensor_copy(out=stg[:], in_=pp[:])), reads=[ppb], writes=[stgb])
            r0 = (tg * 4 + i) * 128
            for hg in range(2):
                C.store("pool", ut_s[hg * TOK + r0:hg * TOK + r0 + 128, :], stg[:, hg * 256:(hg + 1) * 256], [stgb], key,
                        writes=[ut_slabs[hg * 4 + r0 // 1024]])

    ut_gaps, ut_gb = sc["utok_g"]
    for tg in range(TOK // 512):
        group(tg)
        if tg % 2 == 1:
            S.wait_events("pool", C.outs)
            j = tg // 2
            for kslab in (j, 4 + j):
                S.coll("AllGather", ut_s[kslab * 1024:(kslab + 1) * 1024, :], ut_gaps[kslab], PAIRS, reads=[ut_slabs[kslab]], writes=[ut_gb],
                       key="ag_utok")
    for nm, rows in (("qT", 128), ("kTr", 128), ("vr", 2048)):
        s_ap, s_b = sc[nm + "_s"]
        g_aps, g_b = sc[nm + "_g"]
        gather(C, S, s_ap, s_b, g_aps, g_b, "ag_" + nm, rows)
    C.finish()


def phase_Batt(P, sc, E):
    C = Ctx(P)
    S = C.S
    CH = 2048
    qT_g, qT_gb = sc["qT_g"]
    kTr_g, kTr_gb = sc["kTr_g"]
    vr_g, vr_gb = sc["vr_g"]
    oT_s, oT_sb = sc["oT_s"]
    qs = [C.sbb([128, S_LEN], BF16, name="q%d" % i) for i in range(2)]
    ks = [C.sbb([128, S_LEN], BF16, name="k%d" % i) for i in range(2)]
    vs, vsb = C.sbb([128, 64, 256], BF16, name="vs")
    mneg, mnegb = C.sbb([128, 128], BF16, name="mneg")
    idb_t, idbb = C.sbb([128, 128], BF16, name="identb")
    ones, onesb = C.sbb([128, CH], BF16, name="ones")
    zeros, zerosb = C.sbb([128, 3, 128], BF16, name="zeros")
    carry = C.sb([128, 4], F32, name="carry")
    carryb = [S.buf() for _ in range(4)]
    S.dma("sp", mneg[:], E["mneg"], writes=[mnegb], key="mneg")
    S.dma("sp", idb_t[:], E["identb"], writes=[idbb], key="identb")
    S.op("pool", lambda e: e.memset(ones[:], 1.0), writes=[onesb])
    S.op("pool", lambda e: e.memset(zeros[:], 0.0), writes=[zerosb])
    for i in range(2):
        for hf in range(2):
            sl = slice(hf * 4096, (hf + 1) * 4096)
            S.dma("sp", qs[i][0][:, sl], (lambda r, i=i, hf=hf: qT_g[r * 2 + i][hf * 128:(hf + 1) * 128, :]),
                  reads=[qT_gb], writes=[qs[i][1]], key="q%d" % i)
            S.dma("sp", ks[i][0][:, sl], (lambda r, i=i, hf=hf: kTr_g[r * 2 + i][(1 - hf) * 128:(2 - hf) * 128, :]),
                  reads=[kTr_gb], writes=[ks[i][1]], key="k%d" % i)
    for j in range(4):
        src_ = 1 if j < 2 else 0
        S.dma("sp", vs[:, j * 16:(j + 1) * 16, :],
              (lambda r, j=j, src_=src_: vr_g[r * 2 + j % 2][src_ * 2048:(src_ + 1) * 2048, :].rearrange("(blk p) c -> p blk c", p=128)),
              reads=[vr_gb], writes=[vsb], key="vs")

    NB = 3
    gs = [(C.sb([128, CH], F32, name="g%d" % i), [S.buf() for _ in range(4)]) for i in range(NB)]
    cbs = [C.sbb([128, CH + 1], F32, name="cb%d" % i) for i in range(NB)]
    As = [C.sbb([128, CH], BF16, name="A%d" % i) for i in range(2)]
    AT4s = [(C.sb([128, 16, 4, 128], BF16, name="AT4_%d" % i), [S.buf() for _ in range(2)]) for i in range(2)]
    osts = [C.sbb([64, 512], BF16, name="ost%d" % i) + ("ost%d" % i,) for i in range(2)]
    zrot = Rot([C.psb([128, 512], F32, name="z%d" % i) for i in range(4)])
    pTs = [C.psb([128, 1024], BF16, name="pT%d" % i) for i in range(2)]
    pos = [C.psb([64, 512], F32, name="po%d" % i) for i in range(2)]

    units = []
    gcs = []
    for h in range(4):
        for G in range(16):
            N = 512 * (G + 1)
            r0 = S_LEN - N
            offs = list(range(0, N, CH))
            for ci, off in enumerate(offs):
                n = min(CH, N - off)
                gc = dict(h=h, G=G, r0=r0, off=off, n=n, first=(ci == 0), last=(ci == len(offs) - 1), idx=len(gcs))
                gcs.append(gc)
                for k in range(4):
                    lo = 128 * (3 - k) if ci == 0 else 0
                    units.append(dict(gc=gc, k=k, lo=lo, h=h, G=G, r0=r0, off=off, n=n, first=(ci == 0), last=(ci == len(offs) - 1)))

    def s1(i, u):
        h, G, k, r0, off, n, lo = u["h"], u["G"], u["k"], u["r0"], u["off"], u["n"], u["lo"]
        qt = 4 * G + k
        g, gb = gs[i % NB]
        qtile, qb = qs[h // 2]
        ktile, kb = ks[h // 2]
        p0 = (h % 2) * 64
        first_bank = True
        for s in range(lo, n, 512):
            w = min(512, n - s)
            zb, zbb = zrot.next()
            diag = u["first"] and first_bank
            S.op("pe", (lambda e, zb=zb, w=w, s=s, diag=diag: e.matmul(out=zb[:, 0:w], lhsT=qtile[p0:p0 + 64, qt * 128:(qt + 1) * 128],
                                                                   rhs=ktile[p0:p0 + 64, r0 + off + s:r0 + off + s + w],
                                                                   start=True, stop=(not diag))),
                 reads=[qb, kb], writes=[zbb])
            if diag:
                S.op("pe", (lambda e, zb=zb: e.matmul(out=zb[:, 0:128], lhsT=idb_t[:], rhs=mneg[:], start=False, stop=True)),
                     reads=[idbb, mnegb], writes=[zbb])
            S.op("act", (lambda e, zb=zb, w=w, s=s: e.activation(out=g[:, s:s + w], in_=zb[:, 0:w], func=AF.Sigmoid, scale=-0.125)),
                 reads=[zbb], writes=[gb[(s - lo) // 512]])
            first_bank = False

    def s2(i, u):
        n, lo, k = u["n"], u["lo"], u["k"]
        g, gb = gs[i % NB]
        cb, cbb = cbs[i % NB]
        if u["first"]:
            init = 1.0
            rd = []
        else:
            init = carry[:, k:k + 1]
            rd = [carryb[k]]
        ng = (n - lo + 511) // 512
        S.op("dve", lambda e: e.tensor_tensor_scan(out=cb[:, lo + 1:n + 1], data0=g[:, lo:n], data1=ones[:, lo:n], initial=init,
                                                   op0=ALU.mult, op1=ALU.mult),
             reads=gb[0:ng] + [onesb] + rd, writes=[cbb])

    def s3(i, u):
        n, lo, k = u["n"], u["lo"], u["k"]
        cb, cbb = cbs[i % NB]
        A, Ab = As[i % 2]
        if u["first"]:
            S.op("pool", lambda e: e.tensor_scalar(out=A[:, lo:lo + 1], in0=cb[:, lo + 1:lo + 2], scalar1=-1.0, scalar2=1.0,
                                                   op0=ALU.mult, op1=ALU.add), reads=[cbb], writes=[Ab])
        else:
            S.op("pool", lambda e: e.tensor_tensor(out=A[:, 0:1], in0=carry[:, k:k + 1], in1=cb[:, 1:2], op=ALU.subtract),
                 reads=[cbb, carryb[k]], writes=[Ab])
        if not u["last"]:
            S.op("pool", lambda e: e.tensor_copy(out=carry[:, k:k + 1], in_=cb[:, n:n + 1]), reads=[cbb], writes=[carryb[k]])
        S.op("pool", lambda e: e.tensor_tensor(out=A[:, lo + 1:n], in0=cb[:, lo + 1:n], in1=cb[:, lo + 2:n + 1], op=ALU.subtract),
             reads=[cbb], writes=[Ab])

    def s4(i, u):
        n, lo, k = u["n"], u["lo"], u["k"]
        A, Ab = As[i % 2]
        AT4, ATb = AT4s[u["gc"]["idx"] % 2]
        nblk = n // 128
        blo = lo // 128
        if blo > 0:
            S.op("act", lambda e: e.copy(out=AT4[:, 0:blo, k, :], in_=zeros[:, 0:blo, :]), reads=[zerosb], writes=[ATb[0]])
        for b0 in range(0, nblk, 8):
            bs = max(b0, blo)
            be = min(b0 + 8, nblk)
            if bs >= be:
                continue
            pT, pTb = pTs[(b0 // 8) % 2]
            for blk in range(bs, be):
                j = blk - b0
                S.op("pe", (lambda e, pT=pT, j=j, blk=blk: e.transpose(out=pT[:, j * 128:(j + 1) * 128], in_=A[:, blk * 128:(blk + 1) * 128],
                                                                   identity=idb_t[:])),
                     reads=[Ab, idbb], writes=[pTb])
            S.op("act", (lambda e, pT=pT, b0=b0, bs=bs, be=be: e.copy(out=AT4[:, bs:be, k, :],
                                                                   in_=pT[:, (bs - b0) * 128:(be - b0) * 128].rearrange("p (b q) -> p b q", q=128))),
                 reads=[pTb], writes=[ATb[b0 // 8]])

    def s5(gc, part):
        h, G, r0, off, n = gc["h"], gc["G"], gc["r0"], gc["off"], gc["n"]
        AT4, ATb = AT4s[gc["idx"] % 2]
        gidx = h * 16 + G
        po, pob = pos[gidx % 2]
        nblk = n // 128
        kb0 = (r0 + off) // 128
        for blk in range(part * 4, min(part * 4 + 4, nblk)):
            S.op("pe", (lambda e, blk=blk: e.matmul(out=po[:, :], lhsT=vs[:, kb0 + blk, h * 64:(h + 1) * 64],
                                                  rhs=AT4[:, blk, :, :].rearrange("p a b -> p (a b)"),
                                                  start=(gc["first"] and blk == 0), stop=(gc["last"] and blk == nblk - 1))),
                 reads=[vsb, ATb[blk // 8]], writes=[pob])
        if gc["last"] and part * 4 <= nblk - 1 < part * 4 + 4:
            ost, ostb, okey = osts[gidx % 2]
            S.op("act", lambda e: e.copy(out=ost[:], in_=po[:, :]), reads=[pob], writes=[ostb])
            half = G // 8
            c0_ = (G % 8) * 512
            C.store("sp", oT_s[half * 256 + h * 64:half * 256 + (h + 1) * 64, c0_:c0_ + 512], ost[:], [ostb], okey, writes=[oT_sb])

    nun = len(units)
    pending = {}
    for it in range(nun + 9):
        for d, st in enumerate((s1, s2, s3, s4)):
            i = it - d
            if 0 <= i < nun:
                st(i, units[i])
        i5 = it - 4
        if 0 <= i5 < nun and units[i5]["k"] == 3:
            for part in range(4):
                pending.setdefault(it + part, []).append((units[i5]["gc"], part))
        for gc_, part in pending.pop(it, []):
            s5(gc_, part)
    assert not pending
    gather(C, S, oT_s, oT_sb, sc["oT_g"][0], sc["oT_g"][1], "ag_oT", 128)
    C.finish()


def phase_Bssm(P, l, sc, E):
    C = Ctx(P)
    S = C.S
    G = NGRP
    ut_g, ut_gb = sc["utok_g"]
    yt_s, yt_sb = sc["ytok_s"]

    def ld(name, src, shape):
        t, b = C.sbb(shape, name=name)
        S.dma("sp", t[:], src, writes=[b], key=name)
        return t, b

    are, areb = ld("are", E["are"][l], [128, G])
    aim, aimb = ld("aim", E["aim"][l], [128, G])
    ldt, ldtb = ld("ldt", E["ldt"][l], [128, G])
    dTs, dTsb = ld("dTs", E["dTs"][l], [128, G])
    I2, I2b = ld("I2", E["I2"], [128, 64])
    cst, cstb = ld("cst", E["cst"], [128, 8])
    idt, idb = ld("idt", E["ident"], [128, 128])
    bA, bAb = ld("bA", E["bA"][l].rearrange("g p c -> p g c"), [128, G, 16])
    bB, bBb = ld("bB", E["bB"][l].rearrange("g p c -> p g c"), [128, G, 16])
    cTs, cTsb = ld("cTs", E["cTs"][l].rearrange("g p c -> p g c"), [128, G, 16])
    MT, MB, NMT, NMB, SGN, NSGN, NPI = [cst[:, i:i + 1] for i in range(7)]
    psrot = Rot([C.psb([128, 512], F32, name="ps%d" % i) for i in range(8)])

    U_all = C.sb([128, G, 1024], name="U_all")
    U_allb = [S.buf() for _ in range(4)]
    u3rot = Rot([C.sbb([128, 8, 256], F32, name="u3_%d" % i) + ("u3_%d" % i,) for i in range(2)])
    u3grot = Rot([C.sbb([128, 16, 8, 16], F32, name="u3g_%d" % i) for i in range(2)])

    def load_mb(mb):
        u3, u3b, key = u3rot.next()
        src_ = mb // 4
        S.dma("sp", u3[:], (lambda r: ut_g[r * 4 + mb % 4][src_ * 1024:(src_ + 1) * 1024, :].rearrange("(m j) c -> m j c", j=8)),
              reads=[ut_gb], writes=[u3b], key=key)
        u3g, u3gb = u3grot.next()
        S.op("pool", lambda e: e.tensor_copy(out=u3g[:].rearrange("p g j c -> p j g c"), in_=u3[:].rearrange("p j (g c) -> p j g c", c=16)),
             reads=[u3b], writes=[u3gb])

        def quad(gq):
            ps, psb = psrot.next()
            for q in range(4):
                g = gq * 4 + q
                S.op("pe", (lambda e, q=q, g=g: e.matmul(out=ps[:, q * 128:(q + 1) * 128], lhsT=u3g[:, g, :, :].rearrange("p j c -> p (j c)"),
                                                     rhs=idt[:], start=True, stop=True)), reads=[u3gb, idb], writes=[psb])
            dst = U_all[:, gq * 4:(gq + 1) * 4, mb * 128:(mb + 1) * 128]
            src = ps[:].rearrange("p (q m) -> p q m", q=4)
            if gq % 2 == 0:
                S.op("dve", lambda e: e.tensor_copy(out=dst, in_=src), reads=[psb], writes=[U_allb[gq]])
            else:
                S.op("act", lambda e: e.copy(out=dst, in_=src), reads=[psb], writes=[U_allb[gq]])

        for gq in range(4):
            quad(gq)

    for mb in range(8):
        load_mb(mb)

    def small(name):
        return C.sbb([128, G], name=name)

    def dve(fn, reads, writes):
        S.op("dve", fn, reads=reads, writes=writes)

    def tt(out, ob, a, ab, b, bb, op):
        dve(lambda e: e.tensor_tensor(out=out, in0=a, in1=b, op=op), [ab, bb], [ob])

    dt, dtb = small("dt")
    S.op("act", lambda e: e.activation(out=dt[:], in_=ldt[:], func=AF.Exp), reads=[ldtb], writes=[dtb])
    ar, arb = small("ar")
    th, thb = small("th")
    tt(ar[:], arb, are[:], areb, dt[:], dtb, ALU.mult)
    tt(th[:], thb, aim[:], aimb, dt[:], dtb, ALU.mult)
    rho, rhob = small("rho")
    S.op("act", lambda e: e.activation(out=rho[:], in_=ar[:], func=AF.Exp), reads=[arb], writes=[rhob])

    def sin_of(name, shift):
        a, ab = small(name + "_a")
        ki, kib = C.sbb([128, G], I32, name=name + "_ki")
        kf, kfb = small(name + "_kf")
        r, rb = small(name + "_r")
        m, mb_ = small(name + "_m")
        out, outb = small(name)
        dve(lambda e: e.tensor_scalar(out=a[:], in0=th[:], scalar1=shift, scalar2=None, op0=ALU.add), [thb], [ab])
        dve(lambda e: e.tensor_scalar(out=kf[:], in0=a[:], scalar1=1.0 / TWO_PI, scalar2=None, op0=ALU.mult), [ab], [kfb])
        dve(lambda e: e.tensor_copy(out=ki[:], in_=kf[:]), [kfb], [kib])
        dve(lambda e: e.tensor_copy(out=kf[:], in_=ki[:]), [kib], [kfb])
        dve(lambda e: e.scalar_tensor_tensor(out=r[:], in0=kf[:], scalar=-TWO_PI, in1=a[:], op0=ALU.mult, op1=ALU.add), [kfb, ab], [rb])
        dve(lambda e: e.tensor_scalar(out=m[:], in0=r[:], scalar1=-3.141592653589793, scalar2=1e30, op0=ALU.add, op1=ALU.mult), [rb], [mb_])
        dve(lambda e: e.tensor_scalar(out=m[:], in0=m[:], scalar1=0.0, scalar2=1.0, op0=ALU.max, op1=ALU.min), [mb_], [mb_])
        dve(lambda e: e.scalar_tensor_tensor(out=r[:], in0=m[:], scalar=-TWO_PI, in1=r[:], op0=ALU.mult, op1=ALU.add), [mb_, rb], [rb])
        dve(lambda e: e.tensor_scalar(out=m[:], in0=r[:], scalar1=3.141592653589793, scalar2=-1e30, op0=ALU.add, op1=ALU.mult), [rb], [mb_])
        dve(lambda e: e.tensor_scalar(out=m[:], in0=m[:], scalar1=0.0, scalar2=1.0, op0=ALU.max, op1=ALU.min), [mb_], [mb_])
        dve(lambda e: e.scalar_tensor_tensor(out=r[:], in0=m[:], scalar=TWO_PI, in1=r[:], op0=ALU.mult, op1=ALU.add), [mb_, rb], [rb])
        S.op("act", lambda e: e.activation(out=out[:], in_=r[:], func=AF.Sin), reads=[rb], writes=[outb])
        return out, outb

    sn, snb = sin_of("sn", 0.0)
    cs, csb = sin_of("cs", 1.5707963267948966)

    NE = 1 + 4 * 8
    PW, PWb = C.sbb([128, NE, 2, G], name="PW")

    def pidx(k, j):
        return 0 if j == 0 else 1 + k * 8 + (j - 1)

    S.op("pool", lambda e: e.memset(PW[:, 0, 0, :], 1.0), writes=[PWb])
    S.op("pool", lambda e: e.memset(PW[:, 0, 1, :], 0.0), reads=[PWb], writes=[PWb])
    tt(PW[:, 1, 0, :], PWb, rho[:], rhob, cs[:], csb, ALU.mult)
    tt(PW[:, 1, 1, :], PWb, rho[:], rhob, sn[:], snb, ALU.mult)
    t1, t1b = small("cm_t1")
    t2, t2b = small("cm_t2")

    def cmul(io, ia, ib):
        ar_, ai_ = PW[:, ia, 0, :], PW[:, ia, 1, :]
        br_, bi_ = PW[:, ib, 0, :], PW[:, ib, 1, :]
        tt(t1[:], t1b, ai_, PWb, bi_, PWb, ALU.mult)
        tt(t2[:], t2b, ar_, PWb, br_, PWb, ALU.mult)
        tt(PW[:, io, 0, :], PWb, t2[:], t2b, t1[:], t1b, ALU.subtract)
        tt(t1[:], t1b, ar_, PWb, bi_, PWb, ALU.mult)
        tt(t2[:], t2b, ai_, PWb, br_, PWb, ALU.mult)
        tt(PW[:, io, 1, :], PWb, t1[:], t1b, t2[:], t2b, ALU.add)

    for k in range(4):
        if k > 0:
            dve((lambda e, k=k: e.tensor_copy(out=PW[:, pidx(k, 1), :, :], in_=PW[:, pidx(k - 1, 8), :, :])), [PWb], [PWb])
        for j in range(2, 9):
            cmul(pidx(k, j), pidx(k, j - 1), pidx(k, 1))

    nr, nrb = small("nr")
    den, denb = small("den")
    fre, freb = small("fre")
    fim, fimb = small("fim")
    F2, F2b = small("F2")
    lr, li = PW[:, 1, 0, :], PW[:, 1, 1, :]
    dve(lambda e: e.tensor_scalar(out=nr[:], in0=lr, scalar1=-1.0, scalar2=None, op0=ALU.add), [PWb], [nrb])
    tt(den[:], denb, are[:], areb, are[:], areb, ALU.mult)
    tt(t1[:], t1b, aim[:], aimb, aim[:], aimb, ALU.mult)
    tt(den[:], denb, den[:], denb, t1[:], t1b, ALU.add)
    dve(lambda e: e.reciprocal(out=den[:], in_=den[:]), [denb], [denb])
    tt(t1[:], t1b, nr[:], nrb, are[:], areb, ALU.mult)
    tt(t2[:], t2b, li, PWb, aim[:], aimb, ALU.mult)
    tt(fre[:], freb, t1[:], t1b, t2[:], t2b, ALU.add)
    tt(fre[:], freb, fre[:], freb, den[:], denb, ALU.mult)
    tt(t1[:], t1b, li, PWb, are[:], areb, ALU.mult)
    tt(t2[:], t2b, nr[:], nrb, aim[:], aimb, ALU.mult)
    tt(fim[:], fimb, t1[:], t1b, t2[:], t2b, ALU.subtract)
    tt(fim[:], fimb, fim[:], fimb, den[:], denb, ALU.mult)
    dve(lambda e: e.tensor_scalar(out=F2[:], in0=fim[:], scalar1=NSGN, scalar2=None, op0=ALU.mult), [fimb, cstb], [F2b])

    VV, VVb = C.sbb([128, NE, 4, G], name="VV")

    def mkvv(idx):
        pr, pi_ = PW[:, idx, 0, :], PW[:, idx, 1, :]
        for (vi, (s_top, src_top), (s_bot, src_bot)) in ((0, (MT, pr), (NMB, pi_)), (1, (MT, pi_), (MB, pr)),
                                                         (2, (MT, pr), (MB, pi_)), (3, (NMT, pi_), (MB, pr))):
            dve((lambda e, s_bot=s_bot, src_bot=src_bot: e.tensor_scalar(out=t1[:], in0=src_bot, scalar1=s_bot, scalar2=None, op0=ALU.mult)),
                [PWb, cstb], [t1b])
            dve((lambda e, vi=vi, s_top=s_top, src_top=src_top: e.scalar_tensor_tensor(
                out=VV[:, idx, vi, :], in0=src_top, scalar=s_top, in1=t1[:], op0=ALU.mult, op1=ALU.add)),
                [PWb, cstb, t1b], [VVb])

    for idx in range(1, NE):
        mkvv(idx)

    class GSet:
        pass

    def mkset(si):
        gsx = GSet()
        gsx.P0 = [None] + [C.sbb([128, 128], name="P0_%d_%d" % (si, d)) for d in range(1, 9)]
        gsx.PT = {}
        for k in range(4):
            for j in range(1, 8):
                gsx.PT[(k, j)] = C.sbb([128, 128], name="PT_%d_%d_%d" % (si, k, j))
        gsx.tmpB = C.sbb([128, 16], name="tmpB%d" % si)
        gsx.Cm = C.sbb([128, 16], name="Cm%d" % si)
        gsx.Bw = C.sbb([128, 240], name="Bw%d" % si)
        gsx.Rw = C.sbb([128, 256], name="Rw%d" % si)
        gsx.K1T = C.sbb([128, 128], name="K1T%d" % si)
        gsx.T1T = C.sbb([128, 128], name="T1T%d" % si)
        gsx.E = C.sbb([128, 1024], name="E%d" % si)
        gsx.X = {1024: C.sbb([128, 1024], name="X1_%d" % si), 128: C.sbb([128, 128], name="X2_%d" % si),
                 16: C.sbb([128, 16], name="X3_%d" % si), 2: C.sbb([128, 2], name="X4_%d" % si)}
        gsx.F = {128: C.sbb([128, 128], name="F1_%d" % si), 16: C.sbb([128, 16], name="F2_%d" % si), 2: C.sbb([128, 2], name="F3_%d" % si)}
        gsx.Y = C.sbb([128, 1024], name="Y%d" % si)
        gsx.Erm = {128: C.sbb([128, 7, 128], name="Er1_%d" % si), 16: C.sbb([128, 7, 16], name="Er2_%d" % si), 2: C.sbb([128, 7, 2], name="Er3_%d" % si)}
        S.op("pool", lambda e: e.memset(gsx.Bw[0][:], 0.0), writes=[gsx.Bw[1]])
        S.op("pool", lambda e: e.memset(gsx.Rw[0][:], 0.0), writes=[gsx.Rw[1]])
        return gsx

    gsets = [mkset(0), mkset(1)]
    Y3q, Y3qb = C.sbb([128, 8, 8, 64], name="Y3q")

    def build_mat(dst, idx, g, v0, v1):
        t, b = dst
        S.op("dve", lambda e: e.tensor_scalar(out=t[:, 0:64], in0=I2[:], scalar1=VV[:, idx, v0, g:g + 1], scalar2=None, op0=ALU.mult),
             reads=[I2b, VVb], writes=[b])
        S.op("dve", lambda e: e.tensor_scalar(out=t[:, 64:128], in0=I2[:], scalar1=VV[:, idx, v1, g:g + 1], scalar2=None, op0=ALU.mult),
             reads=[I2b, VVb], writes=[b])

    def mm(out, lhsT, rhs, start, stop, reads, writes):
        S.op("pe", lambda e: e.matmul(out=out, lhsT=lhsT, rhs=rhs, start=start, stop=stop), reads=reads, writes=writes)

    def do_group(g):
        gs_ = gsets[g % 2]
        Ub = U_allb[g // 4]
        tB, tBb = gs_.tmpB
        Cm, Cmb = gs_.Cm
        Bw, Bwb = gs_.Bw
        Rw, Rwb = gs_.Rw
        dve(lambda e: e.tensor_scalar(out=tB[:], in0=bB[:, g, :], scalar1=F2[:, g:g + 1], scalar2=None, op0=ALU.mult), [bBb, F2b], [tBb])
        dve(lambda e: e.scalar_tensor_tensor(out=Bw[:, 112:128], in0=bA[:, g, :], scalar=fre[:, g:g + 1], in1=tB[:],
                                             op0=ALU.mult, op1=ALU.add), [bAb, freb, tBb], [Bwb])
        dve(lambda e: e.tensor_scalar(out=Cm[:], in0=cTs[:, g, :], scalar1=SGN, scalar2=None, op0=ALU.mult), [cTsb, cstb], [Cmb])
        for d in range(1, 9):
            build_mat(gs_.P0[d], pidx(0, d), g, 2, 3)
        for k in range(4):
            for j in range(1, 8):
                build_mat(gs_.PT[(k, j)], pidx(k, j), g, 0, 1)

        def PTm(k, j):
            return (idt, idb) if j == 0 else gs_.PT[(k, j)]

        pR, pRb = psrot.next()
        for d in range(9):
            Pd, Pdb = (idt, idb) if d == 0 else gs_.P0[d]
            mm(pR[:, d * 16:(d + 1) * 16], Pd[:], Cm[:], True, True, [Pdb, Cmb], [pRb])
        dve(lambda e: e.tensor_copy(out=Rw[:, 112:256], in_=pR[:, 0:144]), [pRb], [Rwb])
        pK, pKb = psrot.next()
        pT_, pT_b = psrot.next()
        for j in range(8):
            Pj, Pjb = PTm(0, 7 - j)
            mm(pK[:, 0:128], Bw[:, (7 - j) * 16:(7 - j) * 16 + 128], Pj[:], (j == 0), (j == 7), [Bwb, Pjb], [pKb])
        for j in range(8):
            mm(pT_[:, 0:128], Bw[:, (7 - j) * 16:(7 - j) * 16 + 128], Rw[:, (7 - j) * 16:(7 - j) * 16 + 128], (j == 0), (j == 7),
               [Bwb, Rwb], [pT_b])
        K1T, K1Tb = gs_.K1T
        T1T, T1Tb = gs_.T1T
        dve(lambda e: e.tensor_copy(out=K1T[:], in_=pK[:, 0:128]), [pKb], [K1Tb])
        S.op("act", lambda e: e.copy(out=T1T[:], in_=pT_[:, 0:128]), reads=[pT_b], writes=[T1Tb])
        E_, Eb = gs_.E

        def e_half(hf):
            pe_, pe_b = psrot.next()
            mm(pe_[:], K1T[:], U_all[:, g, hf * 512:(hf + 1) * 512], True, True, [K1Tb, Ub], [pe_b])
            if hf == 0:
                dve(lambda e: e.tensor_copy(out=E_[:, hf * 512:(hf + 1) * 512], in_=pe_[:]), [pe_b], [Eb])
            else:
                S.op("act", lambda e: e.copy(out=E_[:, hf * 512:(hf + 1) * 512], in_=pe_[:]), reads=[pe_b], writes=[Eb])

        e_half(0)
        e_half(1)

        def solve(k, Et, Etb, N):
            Xp, Xpb = gs_.X[N]
            if N == 2:
                S.op("pool", lambda e: e.memset(Xp[:, 0:1], 0.0), writes=[Xpb])
                dve(lambda e: e.tensor_copy(out=Xp[:, 1:2], in_=Et[:, 0:1]), [Etb, Xpb], [Xpb])
                return Xp, Xpb
            M = N // 8
            Ev = Et[:, 0:N].rearrange("p (m r) -> p r m", r=8)
            Xv = Xp[:, 0:N].rearrange("p (m r) -> p r m", r=8)
            Ft, Ftb = gs_.F[M]
            pf, pfb = psrot.next()
            for rp in range(8):
                Pj, Pjb = PTm(k, 7 - rp)
                mm(pf[:, 0:M], Pj[:], Ev[:, rp, :], (rp == 0), (rp == 7), [Pjb, Etb], [pfb])
            dve(lambda e: e.tensor_copy(out=Ft[:, 0:M], in_=pf[:, 0:M]), [pfb], [Ftb])
            Zp, Zpb = solve(k + 1, Ft, Ftb, M)
            dve(lambda e: e.tensor_copy(out=Xv[:, 0, :], in_=Zp[:, 0:M]), [Zpb], [Xpb])
            Erm, Ermb = gs_.Erm[M]
            p0, p0b = psrot.next()
            P1, P1b = PTm(k, 1)
            mm(p0[:, 0:M], idt[:], Ev[:, 0, :], True, False, [idb, Etb], [p0b])
            mm(p0[:, 0:M], P1[:], Zp[:, 0:M], False, True, [P1b, Zpb], [p0b])
            S.op("act", lambda e: e.copy(out=Erm[:, 0, :], in_=p0[:, 0:M]), reads=[p0b], writes=[Ermb])
            dve(lambda e: e.tensor_copy(out=Erm[:, 1:7, :], in_=Ev[:, 1:7, :]), [Etb], [Ermb])
            Ef = Erm[:].rearrange("p r m -> p (r m)")
            accA, accAb = psrot.next()
            two = 7 * M > 512
            if two:
                accB, accBb = psrot.next()
            for j in range(7):
                Pj, Pjb = PTm(k, j)
                lo_c, hi_c = j * M, 7 * M
                segs = [(lo_c, hi_c)] if not two else [sg for sg in ((lo_c, min(hi_c, 512)), (max(lo_c, 512), hi_c)) if sg[0] < sg[1]]
                for (a_, b_) in segs:
                    if a_ < 512:
                        mm(accA[:, a_:b_], Pj[:], Ef[:, a_ - lo_c:b_ - lo_c], (j == 0), (j == 6), [Pjb, Ermb], [accAb])
                    else:
                        mm(accB[:, a_ - 512:b_ - 512], Pj[:], Ef[:, a_ - lo_c:b_ - lo_c], (j == 0), (j == 6), [Pjb, Ermb], [accBb])
            nA = min(7, 512 // M)
            dve(lambda e: e.tensor_copy(out=Xv[:, 1:1 + nA, :], in_=accA[:, 0:nA * M].rearrange("p (r m) -> p r m", m=M)), [accAb, Xpb], [Xpb])
            if nA < 7:
                S.op("act", lambda e: e.copy(out=Xv[:, 1 + nA:8, :], in_=accB[:, 0:(7 - nA) * M].rearrange("p (r m) -> p r m", m=M)),
                     reads=[accBb, Xpb], writes=[Xpb])
            return Xp, Xpb

        X1, X1b = solve(1, E_, Eb, 1024)
        Y, Yb = gs_.Y

        def y_half(hf):
            py, pyb = psrot.next()
            sl = slice(hf * 512, (hf + 1) * 512)
            mm(py[:], T1T[:], U_all[:, g, sl], True, False, [T1Tb, Ub], [pyb])
            mm(py[:], Rw[:, 128:256], X1[:, sl], False, True, [Rwb, X1b], [pyb])
            dve(lambda e: e.scalar_tensor_tensor(out=Y[:, sl], in0=U_all[:, g, sl], scalar=dTs[:, g:g + 1], in1=py[:],
                                                 op0=ALU.mult, op1=ALU.add), [pyb, Ub, dTsb], [Yb])

        y_half(0)
        y_half(1)
        S.op("act", lambda e: e.activation(out=Y[:], in_=Y[:], func=AF.Gelu_apprx_tanh), reads=[Yb], writes=[Yb])
        gl = g % 4

        def back(mq):
            ps, psb = psrot.next()
            for q in range(4):
                mb = mq * 4 + q
                mm(ps[:, q * 128:(q + 1) * 128], Y[:, mb * 128:(mb + 1) * 128], idt[:], True, True, [Yb, idb], [psb])
            dst = Y3q[:, mq * 4:(mq + 1) * 4, :, gl * 16:(gl + 1) * 16]
            src = ps[:].rearrange("p (q i c) -> p q i c", q=4, i=8)
            if mq == 0:
                dve(lambda e: e.tensor_copy(out=dst, in_=src), [psb], [Y3qb])
            else:
                S.op("act", lambda e: e.copy(out=dst, in_=src), reads=[psb], writes=[Y3qb])

        back(0)
        back(1)
        if gl == 3:
            gq = g // 4
            dview = yt_s[:, gq * 64:(gq + 1) * 64].rearrange("(mb m i) c -> m mb i c", m=128, i=8)
            for mb_ in range(8):
                C.store("sp", dview[:, mb_, :, :], Y3q[:, mb_, :, :], [Y3qb], "y3q%d" % (mb_ % 2), writes=[yt_sb])

    for g_ in range(G):
        do_group(g_)
    gather(C, S, yt_s, yt_sb, sc["ytok_g"][0], sc["ytok_g"][1], "ag_y", 1024)
    C.finish()


def phase_C1(P, l, x_src, x_srcb, sc, E):
    C = Ctx(P)
    S = C.S
    oT_g, oT_gb = sc["oT_g"]
    yt_g, yt_gb = sc["ytok_g"]
    sg_s, sg_sb = sc["sgT_s"]
    x1_s, x1_sb = sc["x1_s"]
    psrot = Rot([C.psb([128, 512], F32, name="ps%d" % i) for i in range(8)])
    cA, cAb = load_cact(C, E["cT"])
    mrot = mod_stage_rot(C)
    G1, G1b = compute_mod_bc(C, cA, cAb, E["w_ada"][l][:, 2048:3072], E["b_g1_bc"][l], mrot, psrot, "G1")
    srot = make_stage_rot(C, 2, 1024)
    wglu, wglub = load_weight_bf16(C, E["w_glu"][l], 512, 512, srot, "pool", "wglu")
    wssm, wssmb = load_weight_bf16(C, E["w_ssm_up"][l], 512, D, srot, "pool", "wssm")
    wsb, wsbb = load_weight_bf16(C, E["w_sb_up"][l], 512, D, srot, "pool", "wsb")
    wout, woutb = load_weight_bf16(C, E["w_out"][l], D, D, srot, "pool", "wout")

    def ld(name, src, shape, dt=F32):
        t, b = C.sbb(shape, dt, name=name)
        S.dma("sp", t[:], src, writes=[b], key=name)
        return t, b

    bglu, bglub = ld("bglu", E["bgluT"][l], [128, 4])
    lng, lngb = ld("lng", E["ln1g_bc"][l], [128, D])
    lnb, lnbb = ld("lnb", E["ln1b_bc"][l], [128, D])
    idt, idb = ld("idt", E["ident"], [128, 128])

    at_t, at_b = C.sbb([128, 4, 512], BF16, name="at_t")
    yt, ytb = C.sbb([128, 4, 512], F32, name="yt_t")
    sg_t, sg_b = C.sbb([128, 16, 512], BF16, name="sg_t")
    xt = C.sb([128, 4, D], name="x_t")
    xb = [S.buf() for _ in range(4)]
    yg = C.sb([128, 4, 512], F32, name="yg")
    ygB = [S.buf() for _ in range(4)]
    ygb = C.sb([128, 4, 512], BF16, name="ygb")
    ygbB = [S.buf() for _ in range(4)]
    yglu = C.sb([128, 4, 512], BF16, name="yglu")
    ygluB = [S.buf() for _ in range(4)]
    mT = C.sb([128, 8, 512], BF16, name="mT")
    mTB = [S.buf() for _ in range(8)]
    tmps = Rot([C.sbb([128, 512], F32, name="tmp%d" % i) for i in range(4)])
    lntmp = ln_tmp(C)

    def group(tg):
        tsl = slice(tg * 512, (tg + 1) * 512)
        for src in range(2):
            for k2 in range(2):
                S.dma("sp", at_t[:, src * 2 + k2, :],
                      (lambda r, src=src, k2=k2: oT_g[r * 2 + k2][src * 128:(src + 1) * 128, tg * 512:(tg + 1) * 512]),
                      reads=[oT_gb], writes=[at_b], key="at_t")
        for src in range(2):
            S.dma("sp", yt[:, :, src * 256:(src + 1) * 256],
                  (lambda r, src=src: yt_g[r * 4 + tg // 2][src * 1024 + (tg % 2) * 512:src * 1024 + (tg % 2 + 1) * 512, :].rearrange("(i p) c -> p i c", p=128)),
                  reads=[yt_gb], writes=[ytb], key="yt_t")
        S.dma("sp", sg_t[:], sg_s[:, tsl].rearrange("(k p) t -> p k t", p=128), reads=[sg_sb], writes=[sg_b], key="sg_t")
        S.dma("sp", xt[:], x_src[tsl, :].rearrange("(i p) d -> p i d", p=128), reads=[x_srcb], writes=xb, key="x_t")
        for cc in range(4):
            ps, psb = psrot.next()
            for i in range(4):
                S.op("pe", (lambda e, ps=ps, i=i, cc=cc: e.matmul(out=ps[:, i * 128:(i + 1) * 128], lhsT=yt[:, i, cc * 128:(cc + 1) * 128],
                                                               rhs=idt[:], start=True, stop=True)), reads=[ytb, idb], writes=[psb])
            S.op("act", (lambda e, ps=ps, cc=cc: e.copy(out=yg[:, cc, :], in_=ps[:])), reads=[psb], writes=[ygB[cc]])
            S.op("pool", (lambda e, cc=cc: e.tensor_copy(out=ygb[:, cc, :], in_=yg[:, cc, :])), reads=[ygB[cc]], writes=[ygbB[cc]])
        for oc in range(4):
            ps, psb = psrot.next()
            for kc in range(4):
                S.op("pe", (lambda e, ps=ps, oc=oc, kc=kc: e.matmul(out=ps[:], lhsT=wglu[:, kc, oc * 128:(oc + 1) * 128], rhs=ygb[:, kc, :],
                                                                start=(kc == 0), stop=(kc == 3))), reads=[wglub[kc], ygbB[kc]], writes=[psb])
            tmp, tmpb = tmps.next()
            S.op("act", (lambda e, ps=ps, oc=oc, tmp=tmp: e.activation(out=tmp[:], in_=ps[:], func=AF.Sigmoid, bias=bglu[:, oc:oc + 1])),
                 reads=[psb, bglub], writes=[tmpb])
            S.op("dve", (lambda e, oc=oc, tmp=tmp: e.tensor_tensor(out=yglu[:, oc, :], in0=yg[:, oc, :], in1=tmp[:], op=ALU.mult)),
                 reads=[ygB[oc], tmpb], writes=[ygluB[oc]])
        for fo in range(8):
            ps1, ps1b = psrot.next()
            ps2, ps2b = psrot.next()
            for kc in range(4):
                S.op("pe", (lambda e, ps1=ps1, fo=fo, kc=kc: e.matmul(out=ps1[:], lhsT=wsb[:, kc, fo * 128:(fo + 1) * 128], rhs=at_t[:, kc, :],
                                                                  start=(kc == 0), stop=(kc == 3))), reads=[wsbb[kc], at_b], writes=[ps1b])
            for kc in range(4):
                S.op("pe", (lambda e, ps2=ps2, fo=fo, kc=kc: e.matmul(out=ps2[:], lhsT=wssm[:, kc, fo * 128:(fo + 1) * 128], rhs=yglu[:, kc, :],
                                                                  start=(kc == 0), stop=(kc == 3))), reads=[wssmb[kc], ygluB[kc]], writes=[ps2b])
            t1, t1b = tmps.next()
            t2, t2b = tmps.next()
            S.op("dve", (lambda e, ps1=ps1, fo=fo, t1=t1: e.tensor_tensor(out=t1[:], in0=ps1[:], in1=sg_t[:, fo, :], op=ALU.mult)),
                 reads=[ps1b, sg_b], writes=[t1b])
            S.op("dve", (lambda e, ps2=ps2, fo=fo, t2=t2: e.tensor_tensor(out=t2[:], in0=ps2[:], in1=sg_t[:, 8 + fo, :], op=ALU.mult)),
                 reads=[ps2b, sg_b], writes=[t2b])
            S.op("pool", (lambda e, fo=fo, t1=t1, t2=t2: e.tensor_tensor(out=mT[:, fo, :], in0=t1[:], in1=t2[:], op=ALU.add)),
                 reads=[t1b, t2b], writes=[mTB[fo]])
        for i in range(4):
            for hf in range(2):
                ps, psb = psrot.next()
                for fc in range(8):
                    S.op("pe", (lambda e, ps=ps, i=i, hf=hf, fc=fc: e.matmul(out=ps[:], lhsT=mT[:, fc, i * 128:(i + 1) * 128],
                                                                         rhs=wout[:, fc, hf * 512:(hf + 1) * 512],
                                                                         start=(fc == 0), stop=(fc == 7))), reads=[mTB[fc], woutb[fc]], writes=[psb])
                tmp, tmpb = tmps.next()
                S.op("dve", (lambda e, ps=ps, hf=hf, tmp=tmp: e.tensor_tensor(out=tmp[:], in0=ps[:], in1=G1[:, hf * 512:(hf + 1) * 512], op=ALU.mult)),
                     reads=[psb, G1b], writes=[tmpb])
                S.op("dve", (lambda e, i=i, hf=hf, tmp=tmp: e.scalar_tensor_tensor(out=xt[:, i, hf * 512:(hf + 1) * 512],
                                                                                in0=xt[:, i, hf * 512:(hf + 1) * 512], scalar=ALPHA, in1=tmp[:],
                                                                                op0=ALU.mult, op1=ALU.add)), reads=[xb[i], tmpb], writes=[xb[i]])
        layer_norm_group(C, xt, xb, lntmp, 4)
        for i in range(4):
            S.op("pool", (lambda e, i=i: e.tensor_tensor(out=xt[:, i, :], in0=xt[:, i, :], in1=lng[:], op=ALU.mult)), reads=[xb[i], lngb], writes=[xb[i]])
            S.op("pool", (lambda e, i=i: e.tensor_tensor(out=xt[:, i, :], in0=xt[:, i, :], in1=lnb[:], op=ALU.add)), reads=[xb[i], lnbb], writes=[xb[i]])
            r0 = (tg * 4 + i) * 128
            C.store("sp", x1_s[r0:r0 + 128, :], xt[:, i, :], [xb[i]], "x1o%d" % i, writes=[x1_sb])

    for tg in range(TOK // 512):
        group(tg)
    C.finish()


def phase_C2(P, l, sc, E, dst, dstb):
    C = Ctx(P)
    S = C.S
    FH = 2816
    x1_s, x1_sb = sc["x1_s"]
    psrot = Rot([C.psb([128, 512], F32, name="ps%d" % i) for i in range(7)])
    pm, pmb = C.psb([128, 512], F32, name="pm")
    cA, cAb = load_cact(C, E["cT"])
    mrot = mod_stage_rot(C)
    modsb, modb = compute_mod_T2(C, cA, cAb, E["w_ada"][l][:, 3072:5120], E["b_fT"][l], 2048, pm, pmb, mrot)
    S.op("dve", lambda e: e.tensor_scalar(out=modsb[:, 8:16], in0=modsb[:, 8:16], scalar1=1.0, scalar2=None, op0=ALU.add),
         reads=[modb], writes=[modb])
    G2, G2b = compute_mod_bc(C, cA, cAb, E["w_ada"][l][:, 5120:6144], E["b_g2_bc"][l], mrot, psrot, "G2")
    srot = make_stage_rot(C, 2, 1024)
    w1, w1b = load_weight_bf16(C, E["w_ffn_in"][l], D, 2 * FH, srot, "pool", "w1")
    w2, w2b = load_weight_bf16(C, E["w_ffn_out"][l], FH, D, srot, "pool", "w2")
    idt, idb = C.sbb([128, 128], name="idt")
    S.dma("sp", idt[:], E["ident"], writes=[idb], key="idt")
    lng, lngb = C.sbb([128, D], name="lng")
    lnb, lnbb = C.sbb([128, D], name="lnb")
    S.dma("sp", lng[:], E["ln2g_bc"][l], writes=[lngb], key="lng")
    S.dma("sp", lnb[:], E["ln2b_bc"][l], writes=[lnbb], key="lnb")

    NT = 2
    xa = C.sb([128, NT, D], name="xa")
    xab = [S.buf() for _ in range(NT)]
    xr = C.sb([128, NT, D], name="xr")
    xrb = [S.buf() for _ in range(NT)]
    hT = C.sb([128, 8, NT * 128], BF16, name="hT")
    hTB = [S.buf() for _ in range(8)]
    aT = C.sb([128, 22, NT * 128], BF16, name="aT")
    aTB = [S.buf() for _ in range(11)]
    tmps = Rot([C.sbb([128, 512], F32, name="tmp%d" % i) for i in range(2)])
    lntmp_a = ln_tmp(C, NT)
    lntmp_r = ln_tmp(C, NT)
    wr = [dstb] if dstb is not None else []
    W = NT * 128

    def tile(tt_):
        rows = slice(tt_ * W, (tt_ + 1) * W)
        S.dma("sp", xa[:], x1_s[rows, :].rearrange("(i p) d -> p i d", p=128), reads=[x1_sb], writes=xab, key="xa")
        S.dma("sp", xr[:], x1_s[rows, :].rearrange("(i p) d -> p i d", p=128), reads=[x1_sb], writes=xrb, key="xr")
        layer_norm_group(C, xa, xab, lntmp_a, NT)
        for g4 in range(4):
            pt, ptb = psrot.next()
            for q in range(2):
                fc = g4 * 2 + q
                for i in range(NT):
                    S.op("pe", (lambda e, pt=pt, q=q, fc=fc, i=i: e.matmul(out=pt[:, q * W + i * 128:q * W + (i + 1) * 128],
                                                                        lhsT=xa[:, i, fc * 128:(fc + 1) * 128],
                                                                        rhs=idt[:], start=True, stop=True)), reads=[xab[i], idb], writes=[ptb])
            for q in range(2):
                fc = g4 * 2 + q
                S.op("act", (lambda e, pt=pt, q=q, fc=fc: e.activation(out=hT[:, fc, :], in_=pt[:, q * W:(q + 1) * W], func=AF.Identity,
                                                                    scale=modsb[:, 8 + fc:9 + fc], bias=modsb[:, fc:fc + 1])),
                     reads=[ptb, modb], writes=[hTB[fc]])
        for jb in range(11):
            psg, psgb = psrot.next()
            psu, psub = psrot.next()
            for q in range(2):
                j = jb * 2 + q
                for fc in range(8):
                    S.op("pe", (lambda e, psg=psg, q=q, j=j, fc=fc: e.matmul(out=psg[:, q * W:(q + 1) * W], lhsT=w1[:, fc, j * 128:(j + 1) * 128],
                                                                         rhs=hT[:, fc, :], start=(fc == 0), stop=(fc == 7))),
                         reads=[w1b[fc], hTB[fc]], writes=[psgb])
                for fc in range(8):
                    S.op("pe", (lambda e, psu=psu, q=q, j=j, fc=fc: e.matmul(out=psu[:, q * W:(q + 1) * W],
                                                                         lhsT=w1[:, fc, FH + j * 128:FH + (j + 1) * 128],
                                                                         rhs=hT[:, fc, :], start=(fc == 0), stop=(fc == 7))),
                         reads=[w1b[fc], hTB[fc]], writes=[psub])
            tmp, tmpb = tmps.next()
            S.op("act", (lambda e, psg=psg, tmp=tmp: e.activation(out=tmp[:], in_=psg[:], func=AF.Silu)), reads=[psgb], writes=[tmpb])
            S.op("dve", (lambda e, psu=psu, tmp=tmp, jb=jb: e.tensor_tensor(
                out=aT[:, jb * 2:jb * 2 + 2, :].rearrange("p a b -> p (a b)"), in0=tmp[:], in1=psu[:], op=ALU.mult)),
                reads=[tmpb, psub], writes=[aTB[jb]])
        for i in range(NT):
            for hf in range(2):
                ps, psb = psrot.next()
                for kc in range(22):
                    S.op("pe", (lambda e, ps=ps, hf=hf, kc=kc, i=i: e.matmul(out=ps[:], lhsT=aT[:, kc, i * 128:(i + 1) * 128],
                                                                         rhs=w2[:, kc, hf * 512:(hf + 1) * 512],
                                                                         start=(kc == 0), stop=(kc == 21))), reads=[aTB[kc // 2], w2b[kc]], writes=[psb])
                tmp, tmpb = tmps.next()
                S.op("dve", (lambda e, ps=ps, hf=hf, tmp=tmp: e.tensor_tensor(out=tmp[:], in0=ps[:], in1=G2[:, hf * 512:(hf + 1) * 512], op=ALU.mult)),
                     reads=[psb, G2b], writes=[tmpb])
                S.op("dve", (lambda e, hf=hf, tmp=tmp, i=i: e.scalar_tensor_tensor(out=xr[:, i, hf * 512:(hf + 1) * 512],
                                                                                in0=xr[:, i, hf * 512:(hf + 1) * 512],
                                                                                scalar=ALPHA, in1=tmp[:], op0=ALU.mult, op1=ALU.add)),
                     reads=[xrb[i], tmpb], writes=[xrb[i]])
        layer_norm_group(C, xr, xrb, lntmp_r, NT)
        for i in range(NT):
            S.op("pool", (lambda e, i=i: e.tensor_tensor(out=xr[:, i, :], in0=xr[:, i, :], in1=lng[:], op=ALU.mult)), reads=[xrb[i], lngb], writes=[xrb[i]])
            S.op("pool", (lambda e, i=i: e.tensor_tensor(out=xr[:, i, :], in0=xr[:, i, :], in1=lnb[:], op=ALU.add)), reads=[xrb[i], lnbb], writes=[xrb[i]])
            r0 = tt_ * W + i * 128
            C.store("sp", dst[r0:r0 + 128, :], xr[:, i, :], [xrb[i]], "outo%d" % i, writes=wr)

    for tt_ in range(TOK // W):
        tile(tt_)
    C.finish()


EXT_SPECS = None


def build_fused():
    P = Prog()
    _SCHED.append(P.S)
    FH = 2816
    specs = dict(
        x=([TOK, D], F32), cT=([128, 8], F32), w_ada=([2, D, 6144], F32), b_adaT=([2, 128, 16], F32), b_g1_bc=([2, 128, 1024], F32),
        b_fT=([2, 128, 16], F32), b_g2_bc=([2, 128, 1024], F32), w_in=([2, D, 4096], F32),
        ident=([128, 128], F32), J=([128, 128], F32), identb=([128, 128], BF16), mneg=([128, 128], BF16),
        I2=([128, 64], F32), cst=([128, 8], F32),
        are=([2, 128, 16], F32), aim=([2, 128, 16], F32), ldt=([2, 128, 16], F32), dTs=([2, 128, 16], F32),
        bA=([2, 16, 128, 16], F32), bB=([2, 16, 128, 16], F32), cTs=([2, 16, 128, 16], F32),
        bgluT=([2, 128, 4], F32), w_glu=([2, 512, 512], F32), w_ssm_up=([2, 512, D], F32), w_sb_up=([2, 512, D], F32),
        w_out=([2, D, D], F32), ln1g_bc=([2, 128, D], F32), ln1b_bc=([2, 128, D], F32),
        w_ffn_in=([2, D, 2 * FH], F32), w_ffn_out=([2, FH, D], F32), ln2g_bc=([2, 128, D], F32), ln2b_bc=([2, 128, D], F32))
    E = {k: P.din(k, shp, dt) for k, (shp, dt) in specs.items()}
    out_ext = P.dout("out", [TOK, D])
    sc = {}
    for nm, shp, dt in (("qT_s", [512, TOK], BF16), ("kTr_s", [512, TOK], BF16), ("vr_s", [2 * TOK, 256], BF16), ("utok_s", [2 * TOK, 256], F32),
                        ("sgT_s", [2048, TOK], BF16), ("oT_s", [512, TOK], BF16), ("ytok_s", [S_LEN, 256], F32),
                        ("x1_s", [TOK, D], F32), ("xs_s", [TOK, D], F32)):
        sc[nm] = P.scratch(nm, shp, dt)
    for nm, n, shp, dt in (("qT_g", 4, [256, TOK], BF16), ("kTr_g", 4, [256, TOK], BF16), ("vr_g", 4, [4096, 256], BF16),
                           ("utok_g", 8, [2048, 256], F32), ("oT_g", 4, [256, TOK], BF16), ("ytok_g", 8, [2048, 256], F32)):
        aps = [P.scratch("%s_%d" % (nm, k), shp, dt)[0] for k in range(n)]
        sc[nm] = (aps, P.S.buf(nm))
    x_extb = P.S.buf("x_ext")
    for l in range(2):
        xsrc = (E["x"], x_extb) if l == 0 else sc["xs_s"]
        phase_A(P, l, xsrc[0], xsrc[1], sc, E)
        phase_Bssm(P, l, sc, E)
        phase_Batt(P, sc, E)
        phase_C1(P, l, xsrc[0], xsrc[1], sc, E)
        if l == 0:
            phase_C2(P, l, sc, E, sc["xs_s"][0], sc["xs_s"][1])
        else:
            phase_C2(P, l, sc, E, out_ext, None)
    return P.nc


_FUSED = {}


def kernel(x, c, w_ada, b_ada, w_in, w_sb_up, ssm_a_re, ssm_a_im, ssm_log_dt,
           ssm_b_re, ssm_b_im, ssm_c_re, ssm_c_im, ssm_d, w_glu, b_glu,
           w_ssm_up, w_out, ln1_g, ln1_b, w_ffn_in, w_ffn_out, ln2_g, ln2_b):
    A = lambda a: np.asarray(a)
    x = A(x); c = A(c); w_ada = A(w_ada); b_ada = A(b_ada)
    B_, S_, D_ = x.shape
    xf = x.reshape(B_ * S_, D_)
    if "nc" not in _FUSED:
        _FUSED["nc"] = build_fused()
    minv, identb = consts_att()
    I2, cst, ident = consts_ssm()
    J = np.ascontiguousarray(ident[::-1])
    L = w_ada.shape[0]
    common = dict(
        w_ada=f32c(w_ada), w_in=f32c(A(w_in)),
        b_adaT=f32c(np.stack([b_ada[l][0:2048].reshape(16, 128).T for l in range(L)])),
        b_g1_bc=f32c(np.stack([bc128(b_ada[l][2048:3072]) for l in range(L)])),
        b_fT=f32c(np.stack([b_ada[l][3072:5120].reshape(16, 128).T for l in range(L)])),
        b_g2_bc=f32c(np.stack([bc128(b_ada[l][5120:6144]) for l in range(L)])),
        ident=ident, J=J, identb=identb, mneg=np.ascontiguousarray((minv * -30000.0).astype(NPBF16)), I2=I2, cst=cst,
        bgluT=f32c(np.stack([A(b_glu)[l].reshape(4, 128).T for l in range(L)])),
        w_glu=f32c(A(w_glu)), w_ssm_up=f32c(A(w_ssm_up)), w_sb_up=f32c(A(w_sb_up)), w_out=f32c(A(w_out)),
        ln1g_bc=f32c(np.stack([bc128(A(ln1_g)[l]) for l in range(L)])), ln1b_bc=f32c(np.stack([bc128(A(ln1_b)[l]) for l in range(L)])),
        w_ffn_in=f32c(A(w_ffn_in)), w_ffn_out=f32c(A(w_ffn_out)),
        ln2g_bc=f32c(np.stack([bc128(A(ln2_g)[l]) for l in range(L)])), ln2b_bc=f32c(np.stack([bc128(A(ln2_b)[l]) for l in range(L)])))
    are_, aim_, ldt_ = A(ssm_a_re), A(ssm_a_im), A(ssm_log_dt)
    bre_, bim_, cre_, cim_, d_ = A(ssm_b_re), A(ssm_b_im), A(ssm_c_re), A(ssm_c_im), A(ssm_d)
    maps = []
    for core in range(NCORES):
        b, hg = core // 2, core % 2
        gsl = slice(hg * 16, (hg + 1) * 16)
        m = dict(common)
        m["x"] = f32c(xf[core * TOK:(core + 1) * TOK])
        m["cT"] = f32c(c[b].reshape(8, 128).T)
        m["are"] = f32c(np.stack([np.concatenate([are_[l][gsl].T, are_[l][gsl].T], axis=0) for l in range(L)]))
        m["aim"] = f32c(np.stack([np.concatenate([aim_[l][gsl].T, aim_[l][gsl].T], axis=0) for l in range(L)]))
        m["ldt"] = f32c(np.stack([np.broadcast_to(ldt_[l][gsl][None, :], (128, 16)) for l in range(L)]))
        m["dTs"] = f32c(np.stack([np.tile(d_[l][hg * 256:(hg + 1) * 256].reshape(16, 16).T, (8, 1)) for l in range(L)]))
        m["bA"] = f32c(np.stack([np.concatenate([bre_[l][gsl], bim_[l][gsl]], axis=1) for l in range(L)]))
        m["bB"] = f32c(np.stack([np.concatenate([bim_[l][gsl], bre_[l][gsl]], axis=1) for l in range(L)]))
        m["cTs"] = f32c(np.stack([np.concatenate([cre_[l][gsl].transpose(0, 2, 1), cim_[l][gsl].transpose(0, 2, 1)], axis=1) for l in range(L)]))
        maps.append(m)
    res = run_prog(_FUSED["nc"], maps)
    out = np.concatenate([np.asarray(r["out"]) for r in res], axis=0).astype(np.float32)
    return out.reshape(B_, S_, D_)
```

```python
import numpy as np
import concourse.bass as bass
import concourse.mybir as mybir

F32 = mybir.dt.float32
BF16 = mybir.dt.bfloat16
AF = mybir.ActivationFunctionType
ALU = mybir.AluOpType
AX = mybir.AxisListType


class Buf:
    __slots__ = ("name", "w", "r")

    def __init__(self, name):
        self.name = name
        self.w = None
        self.r = {}


class Sched:
    ENG = ("pe", "act", "dve", "pool", "sp")

    def __init__(self, nc, stack):
        self.nc = nc
        self.stack = stack
        self.ops = {e: [] for e in self.ENG}
        self.cnt = {e: 0 for e in self.ENG}
        self.sem = {e: stack.enter_context(nc.semaphore("s_" + e)) for e in self.ENG}
        self.seen = {e: {} for e in self.ENG}
        self.dsems = {}
        self.keymap = {}
        self.nbuf = 0

    def buf(self, name=None):
        self.nbuf += 1
        return Buf(name or ("b%d" % self.nbuf))

    def _waits(self, eng, reads, writes, is_dma=False, dkey=None):
        deps = []
        for b in reads:
            if b.w is not None:
                deps.append((b.w, "raw"))
        for b in writes:
            if b.w is not None:
                deps.append((b.w, "waw"))
            for ev in b.r.values():
                deps.append((ev, "war"))
        waits = {}
        for (ev, kind) in deps:
            sem, val, src = ev
            if not is_dma and src == eng:
                if eng == "pe":
                    continue
                if kind in ("war", "waw"):
                    continue
            if is_dma and kind == "waw" and src == ("dma", dkey):
                continue
            key = id(sem)
            if self.seen[eng].get(key, 0) >= val:
                continue
            if key not in waits or waits[key][1] < val:
                waits[key] = (sem, val)
        for key, (sem, val) in waits.items():
            self.seen[eng][key] = val
        return list(waits.values())

    def _update(self, ev, reads, writes):
        for b in writes:
            b.w = ev
            b.r = {}
        for b in reads:
            if b in writes:
                continue
            k = id(ev[0])
            old = b.r.get(k)
            if old is None or old[1] < ev[1]:
                b.r[k] = ev

    def op(self, eng, emit, reads=(), writes=()):
        waits = self._waits(eng, reads, writes)
        self.cnt[eng] += 1
        ev = (self.sem[eng], self.cnt[eng], eng)
        self.ops[eng].append((waits, emit, (self.sem[eng], 1)))
        self._update(ev, reads, writes)
        return ev

    def new_phase(self):
        self.keymap = {}

    def dsem(self, key):
        if key not in self.dsems:
            self.dsems[key] = [self.stack.enter_context(self.nc.semaphore("d_%d" % len(self.dsems))), 0]
        return self.dsems[key]

    def dma(self, q, out_ap, in_ap, reads=(), writes=(), key=None, **kw):
        assert key is not None
        key = self.keymap.setdefault(key, "k%d" % len(self.keymap))
        waits = self._waits(q, reads, writes, is_dma=True, dkey=key)
        ds = self.dsem(key)
        ds[1] += 16
        ev = (ds[0], ds[1], ("dma", key))
        if callable(in_ap):
            sem_ = ds[0]

            def fn(e):
                rk = self.rank(e)
                with e.If(rk == 0):
                    e.dma_start(out=out_ap, in_=in_ap(0), **kw).then_inc(sem_, 16)
                with e.Else():
                    e.dma_start(out=out_ap, in_=in_ap(1), **kw).then_inc(sem_, 16)
                return None
            self.ops[q].append((waits, fn, (ds[0], 16)))
        else:
            self.ops[q].append((waits, (lambda e: e.dma_start(out=out_ap, in_=in_ap, **kw)), (ds[0], 16)))
        self._update(ev, reads, writes)
        return ev

    def coll(self, kind, in_ap, out_ap, groups, reads=(), writes=(), key=None):
        key = "cc_" + key
        waits = self._waits("pool", reads, writes, is_dma=True, dkey=key)
        ds = self.dsem(key)
        ds[1] += 1
        ev = (ds[0], ds[1], ("dma", key))
        self.ops["pool"].append((waits, (lambda e: e.collective_compute(kind, ALU.bypass, replica_groups=groups,
                                                                         ins=[in_ap.opt()], outs=[out_ap.opt()])), (ds[0], 1)))
        self._update(ev, reads, writes)
        return ev

    def wait_events(self, eng, evs):
        waits = {}
        for ev in evs:
            sem, val, src = ev
            key = id(sem)
            if self.seen[eng].get(key, 0) >= val:
                continue
            if key not in waits or waits[key][1] < val:
                waits[key] = (sem, val)
        for key, (sem, val) in waits.items():
            self.seen[eng][key] = val
        self.ops[eng].append((list(waits.values()), None, None))

    def rank(self, e):
        k = id(e)
        if k not in self._rank_cache:
            self._rank_cache[k] = e.partition_id() % 2
        return self._rank_cache[k]

    def _emit_eng(self, name, e):
        self._rank_cache = {}
        for waits, fn, inc in self.ops[name]:
            for sem, val in waits:
                e.wait_ge(sem, val)
            if fn is None:
                continue
            ins = fn(e)
            if ins is not None:
                ins.then_inc(inc[0], inc[1])

    def emit(self):
        with self.nc.Block() as block:
            @block.tensor
            def _(e):
                self._emit_eng("pe", e)

            @block.scalar
            def _(e):
                self._emit_eng("act", e)

            @block.vector
            def _(e):
                self._emit_eng("dve", e)

            @block.gpsimd
            def _(e):
                self._emit_eng("pool", e)

            @block.sync
            def _(e):
                self._emit_eng("sp", e)
        self.ops = {e: [] for e in self.ENG}


from contextlib import ExitStack
from concourse.bass_utils import run_bass_kernel_spmd
import ml_dtypes

NPBF16 = ml_dtypes.bfloat16
NCORES = 8
S_LEN = 8192
TOK = 4096
D = 1024
LN_EPS = 1e-5
ALPHA = (2 * 2) ** 0.25


class Rot:
    def __init__(self, items):
        self.items = items
        self.i = 0

    def next(self):
        it = self.items[self.i % len(self.items)]
        self.i += 1
        return it


class Prog:
    def __init__(self):
        self.nc = bass.Bass("TRN2", target_bir_lowering=False)
        self.gst = ExitStack()
        self.S = Sched(self.nc, self.gst)
        self.nphase = 0
        self.ext = {}

    def din(self, name, shape, dt=F32):
        if name not in self.ext:
            self.ext[name] = self.nc.dram_tensor(name, list(shape), dt, kind="ExternalInput").ap()
        return self.ext[name]

    def dout(self, name, shape, dt=F32):
        return self.nc.dram_tensor(name, list(shape), dt, kind="ExternalOutput").ap()

    def scratch(self, name, shape, dt=F32):
        t = self.nc.dram_tensor(name, list(shape), dt, kind="Internal").ap()
        return t, self.S.buf(name)


class Ctx:
    def __init__(self, prog=None):
        self.prog = prog
        if prog is None:
            self.nc = bass.Bass("TRN2", target_bir_lowering=False)
            self.st = ExitStack()
            self.S = Sched(self.nc, self.st)
            self.tag = ""
        else:
            self.nc = prog.nc
            self.st = ExitStack()
            self.S = prog.S
            self.S.new_phase()
            prog.nphase += 1
            self.tag = "f%d_" % prog.nphase
        self.outs = []
        self.n = 0

    def din(self, name, shape, dt=F32):
        return self.nc.dram_tensor(name, list(shape), dt, kind="ExternalInput").ap()

    def dout(self, name, shape, dt=F32):
        return self.nc.dram_tensor(name, list(shape), dt, kind="ExternalOutput").ap()

    def sb(self, shape, dt=F32, name=None):
        self.n += 1
        name = "sb_" + self.tag + (name or ("t%d" % self.n))
        return self.st.enter_context(self.nc.sbuf_tensor(name, list(shape), dt))

    def ps(self, shape, dt=F32, name=None):
        self.n += 1
        name = "ps_" + self.tag + (name or ("p%d" % self.n))
        return self.st.enter_context(self.nc.psum_tensor(name, list(shape), dt))

    def sbb(self, shape, dt=F32, name=None):
        t = self.sb(shape, dt, name)
        return t, self.S.buf(name)

    def psb(self, shape, dt=F32, name=None):
        t = self.ps(shape, dt, name)
        return t, self.S.buf(name)

    def store(self, q, out_ap, in_ap, reads, key, writes=()):
        ev = self.S.dma(q, out_ap, in_ap, reads=reads, writes=list(writes), key=key)
        self.outs.append(ev)
        return ev

    def finish(self):
        self.S.wait_events("sp", self.outs)
        self.S.emit()
        self.st.close()
        return self.nc


def layer_norm_group(C, xt, xb, tmp, n=4):
    S = C.S
    st, stb, mv, mvb, rs, rsb = tmp
    for i in range(n):
        S.op("dve", (lambda e, i=i: e.bn_stats(out=st[:, i, 0, :], in_=xt[:, i, 0:512])), reads=[xb[i]], writes=[stb])
        S.op("dve", (lambda e, i=i: e.bn_stats(out=st[:, i, 1, :], in_=xt[:, i, 512:1024])), reads=[xb[i]], writes=[stb])
        S.op("dve", (lambda e, i=i: e.bn_aggr(out=mv[:, i, :], in_=st[:, i, :, :].rearrange("p a b -> p (a b)"))), reads=[stb], writes=[mvb])
    S.op("dve", lambda e: e.tensor_scalar(out=rs[:, 0:n], in0=mv[:, 0:n, 1], scalar1=LN_EPS, scalar2=None, op0=ALU.add),
         reads=[mvb], writes=[rsb])
    S.op("act", lambda e: e.activation(out=rs[:, 0:n], in_=rs[:, 0:n], func=AF.Sqrt), reads=[rsb], writes=[rsb])
    S.op("dve", lambda e: e.reciprocal(out=rs[:, 0:n], in_=rs[:, 0:n]), reads=[rsb], writes=[rsb])
    for i in range(n):
        S.op("dve", (lambda e, i=i: e.tensor_scalar(out=xt[:, i, :], in0=xt[:, i, :], scalar1=mv[:, i, 0:1], scalar2=rs[:, i:i + 1],
                                                    op0=ALU.subtract, op1=ALU.mult)), reads=[xb[i], mvb, rsb], writes=[xb[i]])


def ln_tmp(C, n=4):
    st, stb = C.sbb([128, n, 2, 6])
    mv, mvb = C.sbb([128, n, 2])
    rs, rsb = C.sbb([128, n])
    return (st, stb, mv, mvb, rs, rsb)


def load_weight_bf16(C, w_dram, rows, cols, stage_rot, conv_eng="pool", name="w"):
    S = C.S
    kt = rows // 128
    wt = C.sb([128, kt, cols], BF16, name=name)
    bufs = [S.buf("%s_%d" % (name, k)) for k in range(kt)]
    CW = 2048
    for k in range(kt):
        for c0 in range(0, cols, CW):
            cw = min(CW, cols - c0)
            S.dma("pool", wt[:, k, c0:c0 + cw], w_dram[k * 128:(k + 1) * 128, c0:c0 + cw], writes=[bufs[k]], key=name)
    return wt, bufs


def make_stage_rot(C, n=2, width=2048, name="wstg"):
    items = []
    for i in range(n):
        t, b = C.sbb([128, width], F32, name="%s%d" % (name, i))
        items.append((t, b, "%s%d" % (name, i)))
    return Rot(items)


def load_cact(C, cT_d):
    S = C.S
    cT, cTb = C.sbb([128, 8], name="c_T")
    cA, cAb = C.sbb([128, 8], name="c_A")
    S.dma("sp", cT[:], cT_d, writes=[cTb], key="c_T")
    S.op("act", lambda e: e.activation(out=cA[:], in_=cT[:], func=AF.Silu), reads=[cTb], writes=[cAb])
    return cA, cAb


def compute_mod_T2(C, cA, cAb, w_d, bT_d, ncols, pm, pmb, stg_rot):
    S = C.S
    nj = ncols // 128
    bT, bTb = C.sbb([128, nj], name="b_T")
    modsb, modb = C.sbb([128, nj], name="mod_sb")
    S.dma("sp", bT[:], bT_d, writes=[bTb], key="b_T")
    for j0 in range(0, nj, 2):
        t, b, key = stg_rot.next()
        S.dma("sp", t[:], w_d[:, j0 * 128:(j0 + 2) * 128].rearrange("(k p) c -> p k c", p=128), writes=[b], key=key)
        for jj in range(2):
            j = j0 + jj
            for kc in range(8):
                S.op("pe", (lambda e, t=t, j=j, jj=jj, kc=kc: e.matmul(out=pm[:, j:j + 1], lhsT=t[:, kc, jj * 128:(jj + 1) * 128], rhs=cA[:, kc:kc + 1],
                                                                    start=(kc == 0), stop=(kc == 7))), reads=[b, cAb], writes=[pmb])
    S.op("dve", lambda e: e.tensor_tensor(out=modsb[:], in0=pm[:, 0:nj], in1=bT[:], op=ALU.add), reads=[pmb, bTb], writes=[modb])
    return modsb, modb


def compute_mod_bc(C, cA, cAb, w_d, b_bc_d, stg_rot, psrot, name):
    S = C.S
    G, Gb = C.sbb([128, 1024], name=name)
    ones, onesb = C.sbb([128, 128], name=name + "_ones")
    CA, CAb = C.sbb([128, 8, 128], name=name + "_CA")
    S.dma("sp", G[:], b_bc_d, writes=[Gb], key=name + "_b")
    S.op("pool", lambda e: e.memset(ones[:], 1.0), writes=[onesb])
    for kc in range(8):
        S.op("dve", (lambda e, kc=kc: e.tensor_scalar(out=CA[:, kc, :], in0=ones[:], scalar1=cA[:, kc:kc + 1], scalar2=None, op0=ALU.mult)),
             reads=[onesb, cAb], writes=[CAb])
    pss = [psrot.next() for _ in range(2)]
    for j0 in range(0, 8, 2):
        t, b, key = stg_rot.next()
        S.dma("sp", t[:], w_d[:, j0 * 128:(j0 + 2) * 128].rearrange("(k p) c -> p k c", p=128), writes=[b], key=key)
        ps, psb = pss[j0 // 4]
        for kc in range(8):
            S.op("pe", (lambda e, t=t, j0=j0, kc=kc, ps=ps: e.matmul(out=ps[:, (j0 % 4) * 128:(j0 % 4 + 2) * 128], lhsT=CA[:, kc, :], rhs=t[:, kc, :],
                                                                    start=(kc == 0), stop=(kc == 7))), reads=[b, CAb], writes=[psb])
    for hf in range(2):
        ps, psb = pss[hf]
        S.op("dve", (lambda e, ps=ps, hf=hf: e.scalar_tensor_tensor(out=G[:, hf * 512:(hf + 1) * 512], in0=ps[:], scalar=1.0,
                                                                    in1=G[:, hf * 512:(hf + 1) * 512], op0=ALU.add, op1=ALU.add)),
             reads=[psb, Gb], writes=[Gb])
    return G, Gb


def mod_stage_rot(C):
    items = []
    for i in range(2):
        t, b = C.sbb([128, 8, 256], F32, name="mstg%d" % i)
        items.append((t, b, "mstg%d" % i))
    return Rot(items)


def ln_affine_store(C, xt, xb, n, lng, lngb, lnb, lnbb, lntmp, out_rows, keyp):
    S = C.S
    layer_norm_group(C, xt, xb, lntmp, n)
    for i in range(n):
        S.op("pool", (lambda e, i=i: e.tensor_tensor(out=xt[:, i, :], in0=xt[:, i, :], in1=lng[:], op=ALU.mult)), reads=[xb[i], lngb], writes=[xb[i]])
        S.op("pool", (lambda e, i=i: e.tensor_tensor(out=xt[:, i, :], in0=xt[:, i, :], in1=lnb[:], op=ALU.add)), reads=[xb[i], lnbb], writes=[xb[i]])
        C.store("sp", out_rows(i), xt[:, i, :], [xb[i]], "%s%d" % (keyp, i))


def consts_att():
    p = np.arange(128)[:, None]
    rr = np.arange(128)[None, :]
    minv = (rr <= 127 - p).astype(np.float32)
    identb = np.eye(128, dtype=np.float32).astype(NPBF16)
    return minv, identb


def consts_ssm():
    I2 = np.concatenate([np.eye(64), np.eye(64)], axis=0).astype(np.float32)
    top = (np.arange(128) < 64).astype(np.float32)
    bot = 1.0 - top
    cst = np.stack([top, bot, -top, -bot, top - bot, bot - top, np.full(128, -np.pi, np.float32), np.zeros(128, np.float32)], axis=1)
    return I2, f32c(cst), np.eye(128, dtype=np.float32)


def run_prog(nc, in_maps):
    res = run_bass_kernel_spmd(nc, in_maps, core_ids=list(range(NCORES)))
    return res.results


def f32c(a):
    return np.ascontiguousarray(a, dtype=np.float32)


def bc128(v):
    return f32c(np.broadcast_to(np.asarray(v)[None, :], (128, v.shape[0])))


PAIRS = [[0, 1], [2, 3], [4, 5], [6, 7]]
I32 = mybir.dt.int32
TWO_PI = 6.283185307179586
NGRP = 16


_SCHED = []


def rank_of(e):
    return _SCHED[-1].rank(e)


def gather(C, S, src, srcb, dsts, dstb, key, rows, block=False):
    S.wait_events("pool", C.outs)
    R = src.shape[0]
    assert R % rows == 0 and len(dsts) == R // rows
    for k in range(R // rows):
        ev = S.coll("AllGather", src[k * rows:(k + 1) * rows, :], dsts[k], PAIRS, reads=[srcb], writes=[dstb], key=key)
        if block:
            C.outs.append(ev)


def phase_A(P, l, x_src, x_srcb, sc, E):
    C = Ctx(P)
    S = C.S
    idt, idb = C.sbb([128, 128], name="idt")
    jt, jb = C.sbb([128, 128], name="jt")
    S.dma("sp", idt[:], E["ident"], writes=[idb], key="idt")
    S.dma("sp", jt[:], E["J"], writes=[jb], key="jt")
    pm, pmb = C.psb([128, 512], name="pm")
    cA, cAb = load_cact(C, E["cT"])
    mrot = mod_stage_rot(C)
    modsb, modb = compute_mod_T2(C, cA, cAb, E["w_ada"][l][:, 0:2048], E["b_adaT"][l], 2048, pm, pmb, mrot)
    S.op("dve", lambda e: e.tensor_scalar(out=modsb[:, 8:16], in0=modsb[:, 8:16], scalar1=1.0, scalar2=None, op0=ALU.add),
         reads=[modb], writes=[modb])
    srot = make_stage_rot(C, 2, 2048)
    wbf, wb = load_weight_bf16(C, E["w_in"][l], D, 4096, srot, "pool", "w_in")

    xrot = Rot([(C.sb([128, 4, D], name="xt%d" % i), [S.buf() for _ in range(4)], "xt%d" % i) for i in range(2)])
    hrot = Rot([(C.sb([128, 8, 512], BF16, name="hT%d" % i), [S.buf() for _ in range(8)]) for i in range(2)])
    hrrot = Rot([(C.sb([128, 8, 512], BF16, name="hR%d" % i), [S.buf() for _ in range(8)]) for i in range(2)])
    lntmps = Rot([ln_tmp(C) for _ in range(2)])
    ptrot = Rot([C.psb([128, 512], name="ptr%d" % i) for i in range(3)])
    pprot = Rot([C.psb([128, 512], name="pp%d" % i) for i in range(4)])
    sbf = Rot([C.sbb([128, 512], BF16, name="sbf%d" % i) + ("sbf%d" % i,) for i in range(4)])
    sf32 = Rot([C.sbb([128, 512], F32, name="sf%d" % i) + ("sf%d" % i,) for i in range(2)])
    qT_s, qT_sb = sc["qT_s"]
    kTr_s, kTr_sb = sc["kTr_s"]
    vr_s, vr_sb = sc["vr_s"]
    ut_s, ut_sb = sc["utok_s"]
    sg_s, sg_sb = sc["sgT_s"]
    if "utok_slabs" not in sc:
        sc["utok_slabs"] = [S.buf() for _ in range(8)]
    ut_slabs = sc["utok_slabs"]

    def group(tg):
        xt, xb, xkey = xrot.next()
        hT, hb = hrot.next()
        hR, hRb = hrrot.next()
        S.dma("sp", xt[:], x_src[tg * 512:(tg + 1) * 512, :].rearrange("(i p) d -> p i d", p=128), reads=[x_srcb], writes=xb, key=xkey)
        layer_norm_group(C, xt, xb, lntmps.next())
        for fc in range(8):
            pt, ptb = ptrot.next()
            for i in range(4):
                S.op("pe", (lambda e, pt=pt, i=i, fc=fc: e.matmul(out=pt[:, i * 128:(i + 1) * 128], lhsT=xt[:, i, fc * 128:(fc + 1) * 128],
                                                               rhs=idt[:], start=True, stop=True)), reads=[xb[i], idb], writes=[ptb])
            S.op("act", (lambda e, pt=pt, fc=fc: e.activation(out=hT[:, fc, :], in_=pt[:], func=AF.Identity,
                                                             scale=modsb[:, 8 + fc:9 + fc], bias=modsb[:, fc:fc + 1])),
                 reads=[ptb, modb], writes=[hb[fc]])
            pt2, pt2b = ptrot.next()
            for i in range(4):
                S.op("pe", (lambda e, pt2=pt2, i=i, fc=fc: e.matmul(out=pt2[:, (3 - i) * 128:(4 - i) * 128], lhsT=xt[:, i, fc * 128:(fc + 1) * 128],
                                                                 rhs=jt[:], start=True, stop=True)), reads=[xb[i], jb], writes=[pt2b])
            S.op("dve", (lambda e, pt2=pt2, fc=fc: e.tensor_scalar(out=hR[:, fc, :], in0=pt2[:], scalar1=modsb[:, 8 + fc:9 + fc],
                                                                  scalar2=modsb[:, fc:fc + 1], op0=ALU.mult, op1=ALU.add)),
                 reads=[pt2b, modb], writes=[hRb[fc]])
        tsl = slice(tg * 512, (tg + 1) * 512)
        rbase = TOK - (tg + 1) * 512
        rsl = slice(rbase, rbase + 512)
        for oc in list(range(0, 8)) + list(range(16, 32)):
            pp, ppb = pprot.next()
            src, srcb = (hR, hRb) if 4 <= oc < 8 else (hT, hb)
            for fc in range(8):
                S.op("pe", (lambda e, pp=pp, fc=fc, oc=oc, src=src: e.matmul(out=pp[:], lhsT=wbf[:, fc, oc * 128:(oc + 1) * 128],
                                                                        rhs=src[:, fc, :], start=(fc == 0), stop=(fc == 7))),
                     reads=[wb[fc], srcb[fc]], writes=[ppb])
            stg, stgb, key = sbf.next()
            if oc < 8:
                S.op("dve", (lambda e, stg=stg, pp=pp: e.tensor_copy(out=stg[:], in_=pp[:])), reads=[ppb], writes=[stgb])
                if oc < 4:
                    C.store("pool", qT_s[oc * 128:(oc + 1) * 128, tsl], stg[:], [stgb], key, writes=[qT_sb])
                else:
                    C.store("pool", kTr_s[(oc - 4) * 128:(oc - 3) * 128, rsl], stg[:], [stgb], key, writes=[kTr_sb])
            else:
                S.op("act", (lambda e, stg=stg, pp=pp: e.activation(out=stg[:], in_=pp[:], func=AF.Sigmoid)), reads=[ppb], writes=[stgb])
                C.store("pool", sg_s[(oc - 16) * 128:(oc - 15) * 128, tsl], stg[:], [stgb], key, writes=[sg_sb])
        for i in range(4):
            pp, ppb = pprot.next()
            for fc in range(8):
                S.op("pe", (lambda e, pp=pp, fc=fc, i=i: e.matmul(out=pp[:], lhsT=hR[:, fc, i * 128:(i + 1) * 128],
                                                               rhs=wbf[:, fc, 1024:1536], start=(fc == 0), stop=(fc == 7))),
                     reads=[wb[fc], hRb[fc]], writes=[ppb])
            stg, stgb, key = sbf.next()
            S.op("dve", (lambda e, stg=stg, pp=pp: e.tensor_copy(out=stg[:], in_=pp[:])), reads=[ppb], writes=[stgb])
            for hg in range(2):
                C.store("pool", vr_s[hg * TOK + rbase + i * 128:hg * TOK + rbase + (i + 1) * 128, :], stg[:, hg * 256:(hg + 1) * 256],
                        [stgb], key, writes=[vr_sb])
        for i in range(4):
            pp, ppb = pprot.next()
            for fc in range(8):
                S.op("pe", (lambda e, pp=pp, fc=fc, i=i: e.matmul(out=pp[:], lhsT=hT[:, fc, i * 128:(i + 1) * 128],
                                                               rhs=wbf[:, fc, 1536:2048], start=(fc == 0), stop=(fc == 7))),
                     reads=[wb[fc], hb[fc]], writes=[ppb])
            stg, stgb, key = sf32.next()
            S.op("dve", (lambda e, stg=stg, pp=pp: e.tensor_copy(out=stg[:], in_=pp[:])), reads=[ppb], writes=[stgb])
            r0 = (tg * 4 + i) * 128
            for hg in range(2):
                C.store("pool", ut_s[hg * TOK + r0:hg * TOK + r0 + 128, :], stg[:, hg * 256:(hg + 1) * 256], [stgb], key,
                        writes=[ut_slabs[hg * 4 + r0 // 1024]])

    ut_gaps, ut_gb = sc["utok_g"]
    for tg in range(TOK // 512):
        group(tg)
        if tg % 2 == 1:
            S.wait_events("pool", C.outs)
            j = tg // 2
            for kslab in (j, 4 + j):
                S.coll("AllGather", ut_s[kslab * 1024:(kslab + 1) * 1024, :], ut_gaps[kslab], PAIRS, reads=[ut_slabs[kslab]], writes=[ut_gb],
                       key="ag_utok")
    for nm, rows in (("qT", 128), ("kTr", 128), ("vr", 2048)):
        s_ap, s_b = sc[nm + "_s"]
        g_aps, g_b = sc[nm + "_g"]
        gather(C, S, s_ap, s_b, g_aps, g_b, "ag_" + nm, rows)
    C.finish()


def phase_Batt(P, sc, E):
    C = Ctx(P)
    S = C.S
    CH = 2048
    qT_g, qT_gb = sc["qT_g"]
    kTr_g, kTr_gb = sc["kTr_g"]
    vr_g, vr_gb = sc["vr_g"]
    oT_s, oT_sb = sc["oT_s"]
    qs = [C.sbb([128, S_LEN], BF16, name="q%d" % i) for i in range(2)]
    ks = [C.sbb([128, S_LEN], BF16, name="k%d" % i) for i in range(2)]
    vs, vsb = C.sbb([128, 64, 256], BF16, name="vs")
    mneg, mnegb = C.sbb([128, 128], BF16, name="mneg")
    idb_t, idbb = C.sbb([128, 128], BF16, name="identb")
    ones, onesb = C.sbb([128, CH], BF16, name="ones")
    zeros, zerosb = C.sbb([128, 3, 128], BF16, name="zeros")
    carry = C.sb([128, 4], F32, name="carry")
    carryb = [S.buf() for _ in range(4)]
    S.dma("sp", mneg[:], E["mneg"], writes=[mnegb], key="mneg")
    S.dma("sp", idb_t[:], E["identb"], writes=[idbb], key="identb")
    S.op("pool", lambda e: e.memset(ones[:], 1.0), writes=[onesb])
    S.op("pool", lambda e: e.memset(zeros[:], 0.0), writes=[zerosb])
    for i in range(2):
        for hf in range(2):
            sl = slice(hf * 4096, (hf + 1) * 4096)
            S.dma("sp", qs[i][0][:, sl], (lambda r, i=i, hf=hf: qT_g[r * 2 + i][hf * 128:(hf + 1) * 128, :]),
                  reads=[qT_gb], writes=[qs[i][1]], key="q%d" % i)
            S.dma("sp", ks[i][0][:, sl], (lambda r, i=i, hf=hf: kTr_g[r * 2 + i][(1 - hf) * 128:(2 - hf) * 128, :]),
                  reads=[kTr_gb], writes=[ks[i][1]], key="k%d" % i)
    for j in range(4):
        src_ = 1 if j < 2 else 0
        S.dma("sp", vs[:, j * 16:(j + 1) * 16, :],
              (lambda r, j=j, src_=src_: vr_g[r * 2 + j % 2][src_ * 2048:(src_ + 1) * 2048, :].rearrange("(blk p) c -> p blk c", p=128)),
              reads=[vr_gb], writes=[vsb], key="vs")

    NB = 3
    gs = [(C.sb([128, CH], F32, name="g%d" % i), [S.buf() for _ in range(4)]) for i in range(NB)]
    cbs = [C.sbb([128, CH + 1], F32, name="cb%d" % i) for i in range(NB)]
    As = [C.sbb([128, CH], BF16, name="A%d" % i) for i in range(2)]
    AT4s = [(C.sb([128, 16, 4, 128], BF16, name="AT4_%d" % i), [S.buf() for _ in range(2)]) for i in range(2)]
    osts = [C.sbb([64, 512], BF16, name="ost%d" % i) + ("ost%d" % i,) for i in range(2)]
    zrot = Rot([C.psb([128, 512], F32, name="z%d" % i) for i in range(4)])
    pTs = [C.psb([128, 1024], BF16, name="pT%d" % i) for i in range(2)]
    pos = [C.psb([64, 512], F32, name="po%d" % i) for i in range(2)]

    units = []
    gcs = []
    for h in range(4):
        for G in range(16):
            N = 512 * (G + 1)
            r0 = S_LEN - N
            offs = list(range(0, N, CH))
            for ci, off in enumerate(offs):
                n = min(CH, N - off)
                gc = dict(h=h, G=G, r0=r0, off=off, n=n, first=(ci == 0), last=(ci == len(offs) - 1), idx=len(gcs))
                gcs.append(gc)
                for k in range(4):
                    lo = 128 * (3 - k) if ci == 0 else 0
                    units.append(dict(gc=gc, k=k, lo=lo, h=h, G=G, r0=r0, off=off, n=n, first=(ci == 0), last=(ci == len(offs) - 1)))

    def s1(i, u):
        h, G, k, r0, off, n, lo = u["h"], u["G"], u["k"], u["r0"], u["off"], u["n"], u["lo"]
        qt = 4 * G + k
        g, gb = gs[i % NB]
        qtile, qb = qs[h // 2]
        ktile, kb = ks[h // 2]
        p0 = (h % 2) * 64
        first_bank = True
        for s in range(lo, n, 512):
            w = min(512, n - s)
            zb, zbb = zrot.next()
            diag = u["first"] and first_bank
            S.op("pe", (lambda e, zb=zb, w=w, s=s, diag=diag: e.matmul(out=zb[:, 0:w], lhsT=qtile[p0:p0 + 64, qt * 128:(qt + 1) * 128],
                                                                   rhs=ktile[p0:p0 + 64, r0 + off + s:r0 + off + s + w],
                                                                   start=True, stop=(not diag))),
                 reads=[qb, kb], writes=[zbb])
            if diag:
                S.op("pe", (lambda e, zb=zb: e.matmul(out=zb[:, 0:128], lhsT=idb_t[:], rhs=mneg[:], start=False, stop=True)),
                     reads=[idbb, mnegb], writes=[zbb])
            S.op("act", (lambda e, zb=zb, w=w, s=s: e.activation(out=g[:, s:s + w], in_=zb[:, 0:w], func=AF.Sigmoid, scale=-0.125)),
                 reads=[zbb], writes=[gb[(s - lo) // 512]])
            first_bank = False

    def s2(i, u):
        n, lo, k = u["n"], u["lo"], u["k"]
        g, gb = gs[i % NB]
        cb, cbb = cbs[i % NB]
        if u["first"]:
            S.op("dve", lambda e: e.memset(cb[:, lo:lo + 1], 1.0), writes=[cbb])
            init = 1.0
            rd = []
        else:
            S.op("dve", lambda e: e.tensor_copy(out=cb[:, 0:1], in_=carry[:, k:k + 1]), reads=[carryb[k]], writes=[cbb])
            init = carry[:, k:k + 1]
            rd = [carryb[k]]
        ng = (n - lo + 511) // 512
        S.op("dve", lambda e: e.tensor_tensor_scan(out=cb[:, lo + 1:n + 1], data0=g[:, lo:n], data1=ones[:, lo:n], initial=init,
                                                   op0=ALU.mult, op1=ALU.mult),
             reads=gb[0:ng] + [onesb, cbb] + rd, writes=[cbb])
        if not u["last"]:
            S.op("dve", lambda e: e.tensor_copy(out=carry[:, k:k + 1], in_=cb[:, n:n + 1]), reads=[cbb], writes=[carryb[k]])

    def s3(i, u):
        n, lo = u["n"], u["lo"]
        cb, cbb = cbs[i % NB]
        A, Ab = As[i % 2]
        S.op("pool", lambda e: e.tensor_tensor(out=A[:, lo:n], in0=cb[:, lo:n], in1=cb[:, lo + 1:n + 1], op=ALU.subtract),
             reads=[cbb], writes=[Ab])

    def s4(i, u):
        n, lo, k = u["n"], u["lo"], u["k"]
        A, Ab = As[i % 2]
        AT4, ATb = AT4s[u["gc"]["idx"] % 2]
        nblk = n // 128
        blo = lo // 128
        if blo > 0:
            S.op("act", lambda e: e.copy(out=AT4[:, 0:blo, k, :], in_=zeros[:, 0:blo, :]), reads=[zerosb], writes=[ATb[0]])
        for b0 in range(0, nblk, 8):
            bs = max(b0, blo)
            be = min(b0 + 8, nblk)
            if bs >= be:
                continue
            pT, pTb = pTs[(b0 // 8) % 2]
            for blk in range(bs, be):
                j = blk - b0
                S.op("pe", (lambda e, pT=pT, j=j, blk=blk: e.transpose(out=pT[:, j * 128:(j + 1) * 128], in_=A[:, blk * 128:(blk + 1) * 128],
                                                                   identity=idb_t[:])),
                     reads=[Ab, idbb], writes=[pTb])
            S.op("act", (lambda e, pT=pT, b0=b0, bs=bs, be=be: e.copy(out=AT4[:, bs:be, k, :],
                                                                   in_=pT[:, (bs - b0) * 128:(be - b0) * 128].rearrange("p (b q) -> p b q", q=128))),
                 reads=[pTb], writes=[ATb[b0 // 8]])

    def s5(gc, part):
        h, G, r0, off, n = gc["h"], gc["G"], gc["r0"], gc["off"], gc["n"]
        AT4, ATb = AT4s[gc["idx"] % 2]
        gidx = h * 16 + G
        po, pob = pos[gidx % 2]
        nblk = n // 128
        kb0 = (r0 + off) // 128
        for blk in range(part * 4, min(part * 4 + 4, nblk)):
            S.op("pe", (lambda e, blk=blk: e.matmul(out=po[:, :], lhsT=vs[:, kb0 + blk, h * 64:(h + 1) * 64],
                                                  rhs=AT4[:, blk, :, :].rearrange("p a b -> p (a b)"),
                                                  start=(gc["first"] and blk == 0), stop=(gc["last"] and blk == nblk - 1))),
                 reads=[vsb, ATb[blk // 8]], writes=[pob])
        if gc["last"] and part * 4 <= nblk - 1 < part * 4 + 4:
            ost, ostb, okey = osts[gidx % 2]
            S.op("act", lambda e: e.copy(out=ost[:], in_=po[:, :]), reads=[pob], writes=[ostb])
            half = G // 8
            c0_ = (G % 8) * 512
            C.store("sp", oT_s[half * 256 + h * 64:half * 256 + (h + 1) * 64, c0_:c0_ + 512], ost[:], [ostb], okey, writes=[oT_sb])

    nun = len(units)
    pending = {}
    for it in range(nun + 9):
        for d, st in enumerate((s1, s2, s3, s4)):
            i = it - d
            if 0 <= i < nun:
                st(i, units[i])
        i5 = it - 4
        if 0 <= i5 < nun and units[i5]["k"] == 3:
            for part in range(4):
                pending.setdefault(it + part, []).append((units[i5]["gc"], part))
        for gc_, part in pending.pop(it, []):
            s5(gc_, part)
    assert not pending
    gather(C, S, oT_s, oT_sb, sc["oT_g"][0], sc["oT_g"][1], "ag_oT", 128)
    C.finish()


def phase_Bssm(P, l, sc, E):
    C = Ctx(P)
    S = C.S
    G = NGRP
    ut_g, ut_gb = sc["utok_g"]
    yt_s, yt_sb = sc["ytok_s"]

    def ld(name, src, shape):
        t, b = C.sbb(shape, name=name)
        S.dma("sp", t[:], src, writes=[b], key=name)
        return t, b

    are, areb = ld("are", E["are"][l], [128, G])
    aim, aimb = ld("aim", E["aim"][l], [128, G])
    ldt, ldtb = ld("ldt", E["ldt"][l], [128, G])
    dTs, dTsb = ld("dTs", E["dTs"][l], [128, G])
    I2, I2b = ld("I2", E["I2"], [128, 64])
    cst, cstb = ld("cst", E["cst"], [128, 8])
    idt, idb = ld("idt", E["ident"], [128, 128])
    bA, bAb = ld("bA", E["bA"][l].rearrange("g p c -> p g c"), [128, G, 16])
    bB, bBb = ld("bB", E["bB"][l].rearrange("g p c -> p g c"), [128, G, 16])
    cTs, cTsb = ld("cTs", E["cTs"][l].rearrange("g p c -> p g c"), [128, G, 16])
    MT, MB, NMT, NMB, SGN, NSGN, NPI = [cst[:, i:i + 1] for i in range(7)]
    psrot = Rot([C.psb([128, 512], F32, name="ps%d" % i) for i in range(8)])

    U_all = C.sb([128, G, 1024], name="U_all")
    U_allb = [S.buf() for _ in range(4)]
    u3rot = Rot([C.sbb([128, 8, 256], F32, name="u3_%d" % i) + ("u3_%d" % i,) for i in range(2)])
    u3grot = Rot([C.sbb([128, 16, 8, 16], F32, name="u3g_%d" % i) for i in range(2)])

    def load_mb(mb):
        u3, u3b, key = u3rot.next()
        src_ = mb // 4
        S.dma("sp", u3[:], (lambda r: ut_g[r * 4 + mb % 4][src_ * 1024:(src_ + 1) * 1024, :].rearrange("(m j) c -> m j c", j=8)),
              reads=[ut_gb], writes=[u3b], key=key)
        u3g, u3gb = u3grot.next()
        S.op("pool", lambda e: e.tensor_copy(out=u3g[:].rearrange("p g j c -> p j g c"), in_=u3[:].rearrange("p j (g c) -> p j g c", c=16)),
             reads=[u3b], writes=[u3gb])

        def quad(gq):
            ps, psb = psrot.next()
            for q in range(4):
                g = gq * 4 + q
                S.op("pe", (lambda e, q=q, g=g: e.matmul(out=ps[:, q * 128:(q + 1) * 128], lhsT=u3g[:, g, :, :].rearrange("p j c -> p (j c)"),
                                                     rhs=idt[:], start=True, stop=True)), reads=[u3gb, idb], writes=[psb])
            dst = U_all[:, gq * 4:(gq + 1) * 4, mb * 128:(mb + 1) * 128]
            src = ps[:].rearrange("p (q m) -> p q m", q=4)
            if gq % 2 == 0:
                S.op("dve", lambda e: e.tensor_copy(out=dst, in_=src), reads=[psb], writes=[U_allb[gq]])
            else:
                S.op("act", lambda e: e.copy(out=dst, in_=src), reads=[psb], writes=[U_allb[gq]])

        for gq in range(4):
            quad(gq)

    for mb in range(8):
        load_mb(mb)

    def small(name):
        return C.sbb([128, G], name=name)

    def dve(fn, reads, writes):
        S.op("dve", fn, reads=reads, writes=writes)

    def tt(out, ob, a, ab, b, bb, op):
        dve(lambda e: e.tensor_tensor(out=out, in0=a, in1=b, op=op), [ab, bb], [ob])

    dt, dtb = small("dt")
    S.op("act", lambda e: e.activation(out=dt[:], in_=ldt[:], func=AF.Exp), reads=[ldtb], writes=[dtb])
    ar, arb = small("ar")
    th, thb = small("th")
    tt(ar[:], arb, are[:], areb, dt[:], dtb, ALU.mult)
    tt(th[:], thb, aim[:], aimb, dt[:], dtb, ALU.mult)
    rho, rhob = small("rho")
    S.op("act", lambda e: e.activation(out=rho[:], in_=ar[:], func=AF.Exp), reads=[arb], writes=[rhob])

    def sin_of(name, shift):
        a, ab = small(name + "_a")
        ki, kib = C.sbb([128, G], I32, name=name + "_ki")
        kf, kfb = small(name + "_kf")
        r, rb = small(name + "_r")
        m, mb_ = small(name + "_m")
        out, outb = small(name)
        dve(lambda e: e.tensor_scalar(out=a[:], in0=th[:], scalar1=shift, scalar2=None, op0=ALU.add), [thb], [ab])
        dve(lambda e: e.tensor_scalar(out=kf[:], in0=a[:], scalar1=1.0 / TWO_PI, scalar2=None, op0=ALU.mult), [ab], [kfb])
        dve(lambda e: e.tensor_copy(out=ki[:], in_=kf[:]), [kfb], [kib])
        dve(lambda e: e.tensor_copy(out=kf[:], in_=ki[:]), [kib], [kfb])
        dve(lambda e: e.scalar_tensor_tensor(out=r[:], in0=kf[:], scalar=-TWO_PI, in1=a[:], op0=ALU.mult, op1=ALU.add), [kfb, ab], [rb])
        dve(lambda e: e.tensor_scalar(out=m[:], in0=r[:], scalar1=-3.141592653589793, scalar2=1e30, op0=ALU.add, op1=ALU.mult), [rb], [mb_])
        dve(lambda e: e.tensor_scalar(out=m[:], in0=m[:], scalar1=0.0, scalar2=1.0, op0=ALU.max, op1=ALU.min), [mb_], [mb_])
        dve(lambda e: e.scalar_tensor_tensor(out=r[:], in0=m[:], scalar=-TWO_PI, in1=r[:], op0=ALU.mult, op1=ALU.add), [mb_, rb], [rb])
        dve(lambda e: e.tensor_scalar(out=m[:], in0=r[:], scalar1=3.141592653589793, scalar2=-1e30, op0=ALU.add, op1=ALU.mult), [rb], [mb_])
        dve(lambda e: e.tensor_scalar(out=m[:], in0=m[:], scalar1=0.0, scalar2=1.0, op0=ALU.max, op1=ALU.min), [mb_], [mb_])
        dve(lambda e: e.scalar_tensor_tensor(out=r[:], in0=m[:], scalar=TWO_PI, in1=r[:], op0=ALU.mult, op1=ALU.add), [mb_, rb], [rb])
        S.op("act", lambda e: e.activation(out=out[:], in_=r[:], func=AF.Sin), reads=[rb], writes=[outb])
        return out, outb

    sn, snb = sin_of("sn", 0.0)
    cs, csb = sin_of("cs", 1.5707963267948966)

    NE = 1 + 4 * 8
    PW, PWb = C.sbb([128, NE, 2, G], name="PW")

    def pidx(k, j):
        return 0 if j == 0 else 1 + k * 8 + (j - 1)

    S.op("pool", lambda e: e.memset(PW[:, 0, 0, :], 1.0), writes=[PWb])
    S.op("pool", lambda e: e.memset(PW[:, 0, 1, :], 0.0), reads=[PWb], writes=[PWb])
    tt(PW[:, 1, 0, :], PWb, rho[:], rhob, cs[:], csb, ALU.mult)
    tt(PW[:, 1, 1, :], PWb, rho[:], rhob, sn[:], snb, ALU.mult)
    t1, t1b = small("cm_t1")
    t2, t2b = small("cm_t2")

    def cmul(io, ia, ib):
        ar_, ai_ = PW[:, ia, 0, :], PW[:, ia, 1, :]
        br_, bi_ = PW[:, ib, 0, :], PW[:, ib, 1, :]
        tt(t1[:], t1b, ai_, PWb, bi_, PWb, ALU.mult)
        tt(t2[:], t2b, ar_, PWb, br_, PWb, ALU.mult)
        tt(PW[:, io, 0, :], PWb, t2[:], t2b, t1[:], t1b, ALU.subtract)
        tt(t1[:], t1b, ar_, PWb, bi_, PWb, ALU.mult)
        tt(t2[:], t2b, ai_, PWb, br_, PWb, ALU.mult)
        tt(PW[:, io, 1, :], PWb, t1[:], t1b, t2[:], t2b, ALU.add)

    for k in range(4):
        if k > 0:
            dve((lambda e, k=k: e.tensor_copy(out=PW[:, pidx(k, 1), :, :], in_=PW[:, pidx(k - 1, 8), :, :])), [PWb], [PWb])
        for j in range(2, 9):
            cmul(pidx(k, j), pidx(k, j - 1), pidx(k, 1))

    nr, nrb = small("nr")
    den, denb = small("den")
    fre, freb = small("fre")
    fim, fimb = small("fim")
    F2, F2b = small("F2")
    lr, li = PW[:, 1, 0, :], PW[:, 1, 1, :]
    dve(lambda e: e.tensor_scalar(out=nr[:], in0=lr, scalar1=-1.0, scalar2=None, op0=ALU.add), [PWb], [nrb])
    tt(den[:], denb, are[:], areb, are[:], areb, ALU.mult)
    tt(t1[:], t1b, aim[:], aimb, aim[:], aimb, ALU.mult)
    tt(den[:], denb, den[:], denb, t1[:], t1b, ALU.add)
    dve(lambda e: e.reciprocal(out=den[:], in_=den[:]), [denb], [denb])
    tt(t1[:], t1b, nr[:], nrb, are[:], areb, ALU.mult)
    tt(t2[:], t2b, li, PWb, aim[:], aimb, ALU.mult)
    tt(fre[:], freb, t1[:], t1b, t2[:], t2b, ALU.add)
    tt(fre[:], freb, fre[:], freb, den[:], denb, ALU.mult)
    tt(t1[:], t1b, li, PWb, are[:], areb, ALU.mult)
    tt(t2[:], t2b, nr[:], nrb, aim[:], aimb, ALU.mult)
    tt(fim[:], fimb, t1[:], t1b, t2[:], t2b, ALU.subtract)
    tt(fim[:], fimb, fim[:], fimb, den[:], denb, ALU.mult)
    dve(lambda e: e.tensor_scalar(out=F2[:], in0=fim[:], scalar1=NSGN, scalar2=None, op0=ALU.mult), [fimb, cstb], [F2b])

    VV, VVb = C.sbb([128, NE, 4, G], name="VV")

    def mkvv(idx):
        pr, pi_ = PW[:, idx, 0, :], PW[:, idx, 1, :]
        for (vi, (s_top, src_top), (s_bot, src_bot)) in ((0, (MT, pr), (NMB, pi_)), (1, (MT, pi_), (MB, pr)),
                                                         (2, (MT, pr), (MB, pi_)), (3, (NMT, pi_), (MB, pr))):
            dve((lambda e, s_bot=s_bot, src_bot=src_bot: e.tensor_scalar(out=t1[:], in0=src_bot, scalar1=s_bot, scalar2=None, op0=ALU.mult)),
                [PWb, cstb], [t1b])
            dve((lambda e, vi=vi, s_top=s_top, src_top=src_top: e.scalar_tensor_tensor(
                out=VV[:, idx, vi, :], in0=src_top, scalar=s_top, in1=t1[:], op0=ALU.mult, op1=ALU.add)),
                [PWb, cstb, t1b], [VVb])

    for idx in range(1, NE):
        mkvv(idx)

    class GSet:
        pass

    def mkset(si):
        gsx = GSet()
        gsx.P0 = [None] + [C.sbb([128, 128], name="P0_%d_%d" % (si, d)) for d in range(1, 9)]
        gsx.PT = {}
        for k in range(4):
            for j in range(1, 8):
                gsx.PT[(k, j)] = C.sbb([128, 128], name="PT_%d_%d_%d" % (si, k, j))
        gsx.tmpB = C.sbb([128, 16], name="tmpB%d" % si)
        gsx.Cm = C.sbb([128, 16], name="Cm%d" % si)
        gsx.Bw = C.sbb([128, 240], name="Bw%d" % si)
        gsx.Rw = C.sbb([128, 256], name="Rw%d" % si)
        gsx.K1T = C.sbb([128, 128], name="K1T%d" % si)
        gsx.T1T = C.sbb([128, 128], name="T1T%d" % si)
        gsx.E = C.sbb([128, 1024], name="E%d" % si)
        gsx.X = {1024: C.sbb([128, 1024], name="X1_%d" % si), 128: C.sbb([128, 128], name="X2_%d" % si),
                 16: C.sbb([128, 16], name="X3_%d" % si), 2: C.sbb([128, 2], name="X4_%d" % si)}
        gsx.F = {128: C.sbb([128, 128], name="F1_%d" % si), 16: C.sbb([128, 16], name="F2_%d" % si), 2: C.sbb([128, 2], name="F3_%d" % si)}
        gsx.Y = C.sbb([128, 1024], name="Y%d" % si)
        gsx.Erm = {128: C.sbb([128, 7, 128], name="Er1_%d" % si), 16: C.sbb([128, 7, 16], name="Er2_%d" % si), 2: C.sbb([128, 7, 2], name="Er3_%d" % si)}
        S.op("pool", lambda e: e.memset(gsx.Bw[0][:], 0.0), writes=[gsx.Bw[1]])
        S.op("pool", lambda e: e.memset(gsx.Rw[0][:], 0.0), writes=[gsx.Rw[1]])
        return gsx

    gsets = [mkset(0), mkset(1)]
    Y3q, Y3qb = C.sbb([128, 8, 8, 64], name="Y3q")

    def build_mat(dst, idx, g, v0, v1):
        t, b = dst
        S.op("dve", lambda e: e.tensor_scalar(out=t[:, 0:64], in0=I2[:], scalar1=VV[:, idx, v0, g:g + 1], scalar2=None, op0=ALU.mult),
             reads=[I2b, VVb], writes=[b])
        S.op("dve", lambda e: e.tensor_scalar(out=t[:, 64:128], in0=I2[:], scalar1=VV[:, idx, v1, g:g + 1], scalar2=None, op0=ALU.mult),
             reads=[I2b, VVb], writes=[b])

    def mm(out, lhsT, rhs, start, stop, reads, writes):
        S.op("pe", lambda e: e.matmul(out=out, lhsT=lhsT, rhs=rhs, start=start, stop=stop), reads=reads, writes=writes)

    def do_group(g):
        gs_ = gsets[g % 2]
        Ub = U_allb[g // 4]
        tB, tBb = gs_.tmpB
        Cm, Cmb = gs_.Cm
        Bw, Bwb = gs_.Bw
        Rw, Rwb = gs_.Rw
        dve(lambda e: e.tensor_scalar(out=tB[:], in0=bB[:, g, :], scalar1=F2[:, g:g + 1], scalar2=None, op0=ALU.mult), [bBb, F2b], [tBb])
        dve(lambda e: e.scalar_tensor_tensor(out=Bw[:, 112:128], in0=bA[:, g, :], scalar=fre[:, g:g + 1], in1=tB[:],
                                             op0=ALU.mult, op1=ALU.add), [bAb, freb, tBb], [Bwb])
        dve(lambda e: e.tensor_scalar(out=Cm[:], in0=cTs[:, g, :], scalar1=SGN, scalar2=None, op0=ALU.mult), [cTsb, cstb], [Cmb])
        for d in range(1, 9):
            build_mat(gs_.P0[d], pidx(0, d), g, 2, 3)
        for k in range(4):
            for j in range(1, 8):
                build_mat(gs_.PT[(k, j)], pidx(k, j), g, 0, 1)

        def PTm(k, j):
            return (idt, idb) if j == 0 else gs_.PT[(k, j)]

        pR, pRb = psrot.next()
        for d in range(9):
            Pd, Pdb = (idt, idb) if d == 0 else gs_.P0[d]
            mm(pR[:, d * 16:(d + 1) * 16], Pd[:], Cm[:], True, True, [Pdb, Cmb], [pRb])
        dve(lambda e: e.tensor_copy(out=Rw[:, 112:256], in_=pR[:, 0:144]), [pRb], [Rwb])
        pK, pKb = psrot.next()
        pT_, pT_b = psrot.next()
        for j in range(8):
            Pj, Pjb = PTm(0, 7 - j)
            mm(pK[:, 0:128], Bw[:, (7 - j) * 16:(7 - j) * 16 + 128], Pj[:], (j == 0), (j == 7), [Bwb, Pjb], [pKb])
        for j in range(8):
            mm(pT_[:, 0:128], Bw[:, (7 - j) * 16:(7 - j) * 16 + 128], Rw[:, (7 - j) * 16:(7 - j) * 16 + 128], (j == 0), (j == 7),
               [Bwb, Rwb], [pT_b])
        K1T, K1Tb = gs_.K1T
        T1T, T1Tb = gs_.T1T
        dve(lambda e: e.tensor_copy(out=K1T[:], in_=pK[:, 0:128]), [pKb], [K1Tb])
        S.op("act", lambda e: e.copy(out=T1T[:], in_=pT_[:, 0:128]), reads=[pT_b], writes=[T1Tb])
        E_, Eb = gs_.E

        def e_half(hf):
            pe_, pe_b = psrot.next()
            mm(pe_[:], K1T[:], U_all[:, g, hf * 512:(hf + 1) * 512], True, True, [K1Tb, Ub], [pe_b])
            if hf == 0:
                dve(lambda e: e.tensor_copy(out=E_[:, hf * 512:(hf + 1) * 512], in_=pe_[:]), [pe_b], [Eb])
            else:
                S.op("act", lambda e: e.copy(out=E_[:, hf * 512:(hf + 1) * 512], in_=pe_[:]), reads=[pe_b], writes=[Eb])

        e_half(0)
        e_half(1)

        def solve(k, Et, Etb, N):
            Xp, Xpb = gs_.X[N]
            if N == 2:
                S.op("pool", lambda e: e.memset(Xp[:, 0:1], 0.0), writes=[Xpb])
                dve(lambda e: e.tensor_copy(out=Xp[:, 1:2], in_=Et[:, 0:1]), [Etb, Xpb], [Xpb])
                return Xp, Xpb
            M = N // 8
            Ev = Et[:, 0:N].rearrange("p (m r) -> p r m", r=8)
            Xv = Xp[:, 0:N].rearrange("p (m r) -> p r m", r=8)
            Ft, Ftb = gs_.F[M]
            pf, pfb = psrot.next()
            for rp in range(8):
                Pj, Pjb = PTm(k, 7 - rp)
                mm(pf[:, 0:M], Pj[:], Ev[:, rp, :], (rp == 0), (rp == 7), [Pjb, Etb], [pfb])
            dve(lambda e: e.tensor_copy(out=Ft[:, 0:M], in_=pf[:, 0:M]), [pfb], [Ftb])
            Zp, Zpb = solve(k + 1, Ft, Ftb, M)
            dve(lambda e: e.tensor_copy(out=Xv[:, 0, :], in_=Zp[:, 0:M]), [Zpb], [Xpb])
            Erm, Ermb = gs_.Erm[M]
            p0, p0b = psrot.next()
            P1, P1b = PTm(k, 1)
            mm(p0[:, 0:M], idt[:], Ev[:, 0, :], True, False, [idb, Etb], [p0b])
            mm(p0[:, 0:M], P1[:], Zp[:, 0:M], False, True, [P1b, Zpb], [p0b])
            S.op("act", lambda e: e.copy(out=Erm[:, 0, :], in_=p0[:, 0:M]), reads=[p0b], writes=[Ermb])
            dve(lambda e: e.tensor_copy(out=Erm[:, 1:7, :], in_=Ev[:, 1:7, :]), [Etb], [Ermb])
            Ef = Erm[:].rearrange("p r m -> p (r m)")
            accA, accAb = psrot.next()
            two = 7 * M > 512
            if two:
                accB, accBb = psrot.next()
            for j in range(7):
                Pj, Pjb = PTm(k, j)
                lo_c, hi_c = j * M, 7 * M
                segs = [(lo_c, hi_c)] if not two else [sg for sg in ((lo_c, min(hi_c, 512)), (max(lo_c, 512), hi_c)) if sg[0] < sg[1]]
                for (a_, b_) in segs:
                    if a_ < 512:
                        mm(accA[:, a_:b_], Pj[:], Ef[:, a_ - lo_c:b_ - lo_c], (j == 0), (j == 6), [Pjb, Ermb], [accAb])
                    else:
                        mm(accB[:, a_ - 512:b_ - 512], Pj[:], Ef[:, a_ - lo_c:b_ - lo_c], (j == 0), (j == 6), [Pjb, Ermb], [accBb])
            nA = min(7, 512 // M)
            dve(lambda e: e.tensor_copy(out=Xv[:, 1:1 + nA, :], in_=accA[:, 0:nA * M].rearrange("p (r m) -> p r m", m=M)), [accAb, Xpb], [Xpb])
            if nA < 7:
                S.op("act", lambda e: e.copy(out=Xv[:, 1 + nA:8, :], in_=accB[:, 0:(7 - nA) * M].rearrange("p (r m) -> p r m", m=M)),
                     reads=[accBb, Xpb], writes=[Xpb])
            return Xp, Xpb

        X1, X1b = solve(1, E_, Eb, 1024)
        Y, Yb = gs_.Y

        def y_half(hf):
            py, pyb = psrot.next()
            sl = slice(hf * 512, (hf + 1) * 512)
            mm(py[:], T1T[:], U_all[:, g, sl], True, False, [T1Tb, Ub], [pyb])
            mm(py[:], Rw[:, 128:256], X1[:, sl], False, True, [Rwb, X1b], [pyb])
            dve(lambda e: e.scalar_tensor_tensor(out=Y[:, sl], in0=U_all[:, g, sl], scalar=dTs[:, g:g + 1], in1=py[:],
                                                 op0=ALU.mult, op1=ALU.add), [pyb, Ub, dTsb], [Yb])

        y_half(0)
        y_half(1)
        S.op("act", lambda e: e.activation(out=Y[:], in_=Y[:], func=AF.Gelu_apprx_tanh), reads=[Yb], writes=[Yb])
        gl = g % 4

        def back(mq):
            ps, psb = psrot.next()
            for q in range(4):
                mb = mq * 4 + q
                mm(ps[:, q * 128:(q + 1) * 128], Y[:, mb * 128:(mb + 1) * 128], idt[:], True, True, [Yb, idb], [psb])
            dst = Y3q[:, mq * 4:(mq + 1) * 4, :, gl * 16:(gl + 1) * 16]
            src = ps[:].rearrange("p (q i c) -> p q i c", q=4, i=8)
            if mq == 0:
                dve(lambda e: e.tensor_copy(out=dst, in_=src), [psb], [Y3qb])
            else:
                S.op("act", lambda e: e.copy(out=dst, in_=src), reads=[psb], writes=[Y3qb])

        back(0)
        back(1)
        if gl == 3:
            gq = g // 4
            dview = yt_s[:, gq * 64:(gq + 1) * 64].rearrange("(mb m i) c -> m mb i c", m=128, i=8)
            for mb_ in range(8):
                C.store("sp", dview[:, mb_, :, :], Y3q[:, mb_, :, :], [Y3qb], "y3q%d" % (mb_ % 2), writes=[yt_sb])

    for g_ in range(G):
        do_group(g_)
    gather(C, S, yt_s, yt_sb, sc["ytok_g"][0], sc["ytok_g"][1], "ag_y", 1024)
    C.finish()


def phase_C1(P, l, x_src, x_srcb, sc, E):
    C = Ctx(P)
    S = C.S
    oT_g, oT_gb = sc["oT_g"]
    yt_g, yt_gb = sc["ytok_g"]
    sg_s, sg_sb = sc["sgT_s"]
    x1_s, x1_sb = sc["x1_s"]
    psrot = Rot([C.psb([128, 512], F32, name="ps%d" % i) for i in range(8)])
    cA, cAb = load_cact(C, E["cT"])
    mrot = mod_stage_rot(C)
    G1, G1b = compute_mod_bc(C, cA, cAb, E["w_ada"][l][:, 2048:3072], E["b_g1_bc"][l], mrot, psrot, "G1")
    srot = make_stage_rot(C, 2, 1024)
    wglu, wglub = load_weight_bf16(C, E["w_glu"][l], 512, 512, srot, "pool", "wglu")
    wssm, wssmb = load_weight_bf16(C, E["w_ssm_up"][l], 512, D, srot, "pool", "wssm")
    wsb, wsbb = load_weight_bf16(C, E["w_sb_up"][l], 512, D, srot, "pool", "wsb")
    wout, woutb = load_weight_bf16(C, E["w_out"][l], D, D, srot, "pool", "wout")

    def ld(name, src, shape, dt=F32):
        t, b = C.sbb(shape, dt, name=name)
        S.dma("sp", t[:], src, writes=[b], key=name)
        return t, b

    bglu, bglub = ld("bglu", E["bgluT"][l], [128, 4])
    lng, lngb = ld("lng", E["ln1g_bc"][l], [128, D])
    lnb, lnbb = ld("lnb", E["ln1b_bc"][l], [128, D])
    idt, idb = ld("idt", E["ident"], [128, 128])

    at_t, at_b = C.sbb([128, 4, 512], BF16, name="at_t")
    yt, ytb = C.sbb([128, 4, 512], F32, name="yt_t")
    sg_t, sg_b = C.sbb([128, 16, 512], BF16, name="sg_t")
    xt = C.sb([128, 4, D], name="x_t")
    xb = [S.buf() for _ in range(4)]
    yg = C.sb([128, 4, 512], F32, name="yg")
    ygB = [S.buf() for _ in range(4)]
    ygb = C.sb([128, 4, 512], BF16, name="ygb")
    ygbB = [S.buf() for _ in range(4)]
    yglu = C.sb([128, 4, 512], BF16, name="yglu")
    ygluB = [S.buf() for _ in range(4)]
    mT = C.sb([128, 8, 512], BF16, name="mT")
    mTB = [S.buf() for _ in range(8)]
    tmps = Rot([C.sbb([128, 512], F32, name="tmp%d" % i) for i in range(4)])
    lntmp = ln_tmp(C)

    def group(tg):
        tsl = slice(tg * 512, (tg + 1) * 512)
        for src in range(2):
            for k2 in range(2):
                S.dma("sp", at_t[:, src * 2 + k2, :],
                      (lambda r, src=src, k2=k2: oT_g[r * 2 + k2][src * 128:(src + 1) * 128, tg * 512:(tg + 1) * 512]),
                      reads=[oT_gb], writes=[at_b], key="at_t")
        for src in range(2):
            S.dma("sp", yt[:, :, src * 256:(src + 1) * 256],
                  (lambda r, src=src: yt_g[r * 4 + tg // 2][src * 1024 + (tg % 2) * 512:src * 1024 + (tg % 2 + 1) * 512, :].rearrange("(i p) c -> p i c", p=128)),
                  reads=[yt_gb], writes=[ytb], key="yt_t")
        S.dma("sp", sg_t[:], sg_s[:, tsl].rearrange("(k p) t -> p k t", p=128), reads=[sg_sb], writes=[sg_b], key="sg_t")
        S.dma("sp", xt[:], x_src[tsl, :].rearrange("(i p) d -> p i d", p=128), reads=[x_srcb], writes=xb, key="x_t")
        for cc in range(4):
            ps, psb = psrot.next()
            for i in range(4):
                S.op("pe", (lambda e, ps=ps, i=i, cc=cc: e.matmul(out=ps[:, i * 128:(i + 1) * 128], lhsT=yt[:, i, cc * 128:(cc + 1) * 128],
                                                               rhs=idt[:], start=True, stop=True)), reads=[ytb, idb], writes=[psb])
            S.op("act", (lambda e, ps=ps, cc=cc: e.copy(out=yg[:, cc, :], in_=ps[:])), reads=[psb], writes=[ygB[cc]])
            S.op("pool", (lambda e, cc=cc: e.tensor_copy(out=ygb[:, cc, :], in_=yg[:, cc, :])), reads=[ygB[cc]], writes=[ygbB[cc]])
        for oc in range(4):
            ps, psb = psrot.next()
            for kc in range(4):
                S.op("pe", (lambda e, ps=ps, oc=oc, kc=kc: e.matmul(out=ps[:], lhsT=wglu[:, kc, oc * 128:(oc + 1) * 128], rhs=ygb[:, kc, :],
                                                                start=(kc == 0), stop=(kc == 3))), reads=[wglub[kc], ygbB[kc]], writes=[psb])
            tmp, tmpb = tmps.next()
            S.op("act", (lambda e, ps=ps, oc=oc, tmp=tmp: e.activation(out=tmp[:], in_=ps[:], func=AF.Sigmoid, bias=bglu[:, oc:oc + 1])),
                 reads=[psb, bglub], writes=[tmpb])
            S.op("dve", (lambda e, oc=oc, tmp=tmp: e.tensor_tensor(out=yglu[:, oc, :], in0=yg[:, oc, :], in1=tmp[:], op=ALU.mult)),
                 reads=[ygB[oc], tmpb], writes=[ygluB[oc]])
        for fo in range(8):
            ps1, ps1b = psrot.next()
            ps2, ps2b = psrot.next()
            for kc in range(4):
                S.op("pe", (lambda e, ps1=ps1, fo=fo, kc=kc: e.matmul(out=ps1[:], lhsT=wsb[:, kc, fo * 128:(fo + 1) * 128], rhs=at_t[:, kc, :],
                                                                  start=(kc == 0), stop=(kc == 3))), reads=[wsbb[kc], at_b], writes=[ps1b])
            for kc in range(4):
                S.op("pe", (lambda e, ps2=ps2, fo=fo, kc=kc: e.matmul(out=ps2[:], lhsT=wssm[:, kc, fo * 128:(fo + 1) * 128], rhs=yglu[:, kc, :],
                                                                  start=(kc == 0), stop=(kc == 3))), reads=[wssmb[kc], ygluB[kc]], writes=[ps2b])
            t1, t1b = tmps.next()
            t2, t2b = tmps.next()
            S.op("dve", (lambda e, ps1=ps1, fo=fo, t1=t1: e.tensor_tensor(out=t1[:], in0=ps1[:], in1=sg_t[:, fo, :], op=ALU.mult)),
                 reads=[ps1b, sg_b], writes=[t1b])
            S.op("dve", (lambda e, ps2=ps2, fo=fo, t2=t2: e.tensor_tensor(out=t2[:], in0=ps2[:], in1=sg_t[:, 8 + fo, :], op=ALU.mult)),
                 reads=[ps2b, sg_b], writes=[t2b])
            S.op("pool", (lambda e, fo=fo, t1=t1, t2=t2: e.tensor_tensor(out=mT[:, fo, :], in0=t1[:], in1=t2[:], op=ALU.add)),
                 reads=[t1b, t2b], writes=[mTB[fo]])
        for i in range(4):
            for hf in range(2):
                ps, psb = psrot.next()
                for fc in range(8):
                    S.op("pe", (lambda e, ps=ps, i=i, hf=hf, fc=fc: e.matmul(out=ps[:], lhsT=mT[:, fc, i * 128:(i + 1) * 128],
                                                                         rhs=wout[:, fc, hf * 512:(hf + 1) * 512],
                                                                         start=(fc == 0), stop=(fc == 7))), reads=[mTB[fc], woutb[fc]], writes=[psb])
                tmp, tmpb = tmps.next()
                S.op("dve", (lambda e, ps=ps, hf=hf, tmp=tmp: e.tensor_tensor(out=tmp[:], in0=ps[:], in1=G1[:, hf * 512:(hf + 1) * 512], op=ALU.mult)),
                     reads=[psb, G1b], writes=[tmpb])
                S.op("dve", (lambda e, i=i, hf=hf, tmp=tmp: e.scalar_tensor_tensor(out=xt[:, i, hf * 512:(hf + 1) * 512],
                                                                                in0=xt[:, i, hf * 512:(hf + 1) * 512], scalar=ALPHA, in1=tmp[:],
                                                                                op0=ALU.mult, op1=ALU.add)), reads=[xb[i], tmpb], writes=[xb[i]])
        layer_norm_group(C, xt, xb, lntmp, 4)
        for i in range(4):
            S.op("pool", (lambda e, i=i: e.tensor_tensor(out=xt[:, i, :], in0=xt[:, i, :], in1=lng[:], op=ALU.mult)), reads=[xb[i], lngb], writes=[xb[i]])
            S.op("pool", (lambda e, i=i: e.tensor_tensor(out=xt[:, i, :], in0=xt[:, i, :], in1=lnb[:], op=ALU.add)), reads=[xb[i], lnbb], writes=[xb[i]])
            r0 = (tg * 4 + i) * 128
            C.store("sp", x1_s[r0:r0 + 128, :], xt[:, i, :], [xb[i]], "x1o%d" % i, writes=[x1_sb])

    for tg in range(TOK // 512):
        group(tg)
    C.finish()


def phase_C2(P, l, sc, E, dst, dstb):
    C = Ctx(P)
    S = C.S
    FH = 2816
    x1_s, x1_sb = sc["x1_s"]
    psrot = Rot([C.psb([128, 512], F32, name="ps%d" % i) for i in range(7)])
    pm, pmb = C.psb([128, 512], F32, name="pm")
    cA, cAb = load_cact(C, E["cT"])
    mrot = mod_stage_rot(C)
    modsb, modb = compute_mod_T2(C, cA, cAb, E["w_ada"][l][:, 3072:5120], E["b_fT"][l], 2048, pm, pmb, mrot)
    S.op("dve", lambda e: e.tensor_scalar(out=modsb[:, 8:16], in0=modsb[:, 8:16], scalar1=1.0, scalar2=None, op0=ALU.add),
         reads=[modb], writes=[modb])
    G2, G2b = compute_mod_bc(C, cA, cAb, E["w_ada"][l][:, 5120:6144], E["b_g2_bc"][l], mrot, psrot, "G2")
    srot = make_stage_rot(C, 2, 1024)
    w1, w1b = load_weight_bf16(C, E["w_ffn_in"][l], D, 2 * FH, srot, "pool", "w1")
    w2, w2b = load_weight_bf16(C, E["w_ffn_out"][l], FH, D, srot, "pool", "w2")
    idt, idb = C.sbb([128, 128], name="idt")
    S.dma("sp", idt[:], E["ident"], writes=[idb], key="idt")
    lng, lngb = C.sbb([128, D], name="lng")
    lnb, lnbb = C.sbb([128, D], name="lnb")
    S.dma("sp", lng[:], E["ln2g_bc"][l], writes=[lngb], key="lng")
    S.dma("sp", lnb[:], E["ln2b_bc"][l], writes=[lnbb], key="lnb")

    NT = 2
    xa = C.sb([128, NT, D], name="xa")
    xab = [S.buf() for _ in range(NT)]
    xr = C.sb([128, NT, D], name="xr")
    xrb = [S.buf() for _ in range(NT)]
    hT = C.sb([128, 8, NT * 128], BF16, name="hT")
    hTB = [S.buf() for _ in range(8)]
    aT = C.sb([128, 22, NT * 128], BF16, name="aT")
    aTB = [S.buf() for _ in range(11)]
    tmps = Rot([C.sbb([128, 512], F32, name="tmp%d" % i) for i in range(1)])
    lntmp_a = ln_tmp(C, NT)
    lntmp_r = ln_tmp(C, NT)
    wr = [dstb] if dstb is not None else []
    W = NT * 128

    def tile(tt_):
        rows = slice(tt_ * W, (tt_ + 1) * W)
        S.dma("sp", xa[:], x1_s[rows, :].rearrange("(i p) d -> p i d", p=128), reads=[x1_sb], writes=xab, key="xa")
        S.dma("sp", xr[:], x1_s[rows, :].rearrange("(i p) d -> p i d", p=128), reads=[x1_sb], writes=xrb, key="xr")
        layer_norm_group(C, xa, xab, lntmp_a, NT)
        for g4 in range(4):
            pt, ptb = psrot.next()
            for q in range(2):
                fc = g4 * 2 + q
                for i in range(NT):
                    S.op("pe", (lambda e, pt=pt, q=q, fc=fc, i=i: e.matmul(out=pt[:, q * W + i * 128:q * W + (i + 1) * 128],
                                                                        lhsT=xa[:, i, fc * 128:(fc + 1) * 128],
                                                                        rhs=idt[:], start=True, stop=True)), reads=[xab[i], idb], writes=[ptb])
            for q in range(2):
                fc = g4 * 2 + q
                S.op("act", (lambda e, pt=pt, q=q, fc=fc: e.activation(out=hT[:, fc, :], in_=pt[:, q * W:(q + 1) * W], func=AF.Identity,
                                                                    scale=modsb[:, 8 + fc:9 + fc], bias=modsb[:, fc:fc + 1])),
                     reads=[ptb, modb], writes=[hTB[fc]])
        for jb in range(11):
            psg, psgb = psrot.next()
            psu, psub = psrot.next()
            for q in range(2):
                j = jb * 2 + q
                for fc in range(8):
                    S.op("pe", (lambda e, psg=psg, q=q, j=j, fc=fc: e.matmul(out=psg[:, q * W:(q + 1) * W], lhsT=w1[:, fc, j * 128:(j + 1) * 128],
                                                                         rhs=hT[:, fc, :], start=(fc == 0), stop=(fc == 7))),
                         reads=[w1b[fc], hTB[fc]], writes=[psgb])
                for fc in range(8):
                    S.op("pe", (lambda e, psu=psu, q=q, j=j, fc=fc: e.matmul(out=psu[:, q * W:(q + 1) * W],
                                                                         lhsT=w1[:, fc, FH + j * 128:FH + (j + 1) * 128],
                                                                         rhs=hT[:, fc, :], start=(fc == 0), stop=(fc == 7))),
                         reads=[w1b[fc], hTB[fc]], writes=[psub])
            tmp, tmpb = tmps.next()
            S.op("act", (lambda e, psg=psg, tmp=tmp: e.activation(out=tmp[:], in_=psg[:], func=AF.Silu)), reads=[psgb], writes=[tmpb])
            S.op("dve", (lambda e, psu=psu, tmp=tmp, jb=jb: e.tensor_tensor(
                out=aT[:, jb * 2:jb * 2 + 2, :].rearrange("p a b -> p (a b)"), in0=tmp[:], in1=psu[:], op=ALU.mult)),
                reads=[tmpb, psub], writes=[aTB[jb]])
        for i in range(NT):
            for hf in range(2):
                ps, psb = psrot.next()
                for kc in range(22):
                    S.op("pe", (lambda e, ps=ps, hf=hf, kc=kc, i=i: e.matmul(out=ps[:], lhsT=aT[:, kc, i * 128:(i + 1) * 128],
                                                                         rhs=w2[:, kc, hf * 512:(hf + 1) * 512],
                                                                         start=(kc == 0), stop=(kc == 21))), reads=[aTB[kc // 2], w2b[kc]], writes=[psb])
                tmp, tmpb = tmps.next()
                S.op("dve", (lambda e, ps=ps, hf=hf, tmp=tmp: e.tensor_tensor(out=tmp[:], in0=ps[:], in1=G2[:, hf * 512:(hf + 1) * 512], op=ALU.mult)),
                     reads=[psb, G2b], writes=[tmpb])
                S.op("dve", (lambda e, hf=hf, tmp=tmp, i=i: e.scalar_tensor_tensor(out=xr[:, i, hf * 512:(hf + 1) * 512],
                                                                                in0=xr[:, i, hf * 512:(hf + 1) * 512],
                                                                                scalar=ALPHA, in1=tmp[:], op0=ALU.mult, op1=ALU.add)),
                     reads=[xrb[i], tmpb], writes=[xrb[i]])
        layer_norm_group(C, xr, xrb, lntmp_r, NT)
        for i in range(NT):
            S.op("pool", (lambda e, i=i: e.tensor_tensor(out=xr[:, i, :], in0=xr[:, i, :], in1=lng[:], op=ALU.mult)), reads=[xrb[i], lngb], writes=[xrb[i]])
            S.op("pool", (lambda e, i=i: e.tensor_tensor(out=xr[:, i, :], in0=xr[:, i, :], in1=lnb[:], op=ALU.add)), reads=[xrb[i], lnbb], writes=[xrb[i]])
            r0 = tt_ * W + i * 128
            C.store("sp", dst[r0:r0 + 128, :], xr[:, i, :], [xrb[i]], "outo%d" % i, writes=wr)

    for tt_ in range(TOK // W):
        tile(tt_)
    C.finish()


EXT_SPECS = None


def build_fused():
    P = Prog()
    _SCHED.append(P.S)
    FH = 2816
    specs = dict(
        x=([TOK, D], F32), cT=([128, 8], F32), w_ada=([2, D, 6144], F32), b_adaT=([2, 128, 16], F32), b_g1_bc=([2, 128, 1024], F32),
        b_fT=([2, 128, 16], F32), b_g2_bc=([2, 128, 1024], F32), w_in=([2, D, 4096], F32),
        ident=([128, 128], F32), J=([128, 128], F32), identb=([128, 128], BF16), mneg=([128, 128], BF16),
        I2=([128, 64], F32), cst=([128, 8], F32),
        are=([2, 128, 16], F32), aim=([2, 128, 16], F32), ldt=([2, 128, 16], F32), dTs=([2, 128, 16], F32),
        bA=([2, 16, 128, 16], F32), bB=([2, 16, 128, 16], F32), cTs=([2, 16, 128, 16], F32),
        bgluT=([2, 128, 4], F32), w_glu=([2, 512, 512], F32), w_ssm_up=([2, 512, D], F32), w_sb_up=([2, 512, D], F32),
        w_out=([2, D, D], F32), ln1g_bc=([2, 128, D], F32), ln1b_bc=([2, 128, D], F32),
        w_ffn_in=([2, D, 2 * FH], F32), w_ffn_out=([2, FH, D], F32), ln2g_bc=([2, 128, D], F32), ln2b_bc=([2, 128, D], F32))
    E = {k: P.din(k, shp, dt) for k, (shp, dt) in specs.items()}
    out_ext = P.dout("out", [TOK, D])
    sc = {}
    for nm, shp, dt in (("qT_s", [512, TOK], BF16), ("kTr_s", [512, TOK], BF16), ("vr_s", [2 * TOK, 256], BF16), ("utok_s", [2 * TOK, 256], F32),
                        ("sgT_s", [2048, TOK], BF16), ("oT_s", [512, TOK], BF16), ("ytok_s", [S_LEN, 256], F32),
                        ("x1_s", [TOK, D], F32), ("xs_s", [TOK, D], F32)):
        sc[nm] = P.scratch(nm, shp, dt)
    for nm, n, shp, dt in (("qT_g", 4, [256, TOK], BF16), ("kTr_g", 4, [256, TOK], BF16), ("vr_g", 4, [4096, 256], BF16),
                           ("utok_g", 8, [2048, 256], F32), ("oT_g", 4, [256, TOK], BF16), ("ytok_g", 8, [2048, 256], F32)):
        aps = [P.scratch("%s_%d" % (nm, k), shp, dt)[0] for k in range(n)]
        sc[nm] = (aps, P.S.buf(nm))
    x_extb = P.S.buf("x_ext")
    for l in range(2):
        xsrc = (E["x"], x_extb) if l == 0 else sc["xs_s"]
        phase_A(P, l, xsrc[0], xsrc[1], sc, E)
        phase_Bssm(P, l, sc, E)
        phase_Batt(P, sc, E)
        phase_C1(P, l, xsrc[0], xsrc[1], sc, E)
        if l == 0:
            phase_C2(P, l, sc, E, sc["xs_s"][0], sc["xs_s"][1])
        else:
            phase_C2(P, l, sc, E, out_ext, None)
    return P.nc


_FUSED = {}


def kernel(x, c, w_ada, b_ada, w_in, w_sb_up, ssm_a_re, ssm_a_im, ssm_log_dt,
           ssm_b_re, ssm_b_im, ssm_c_re, ssm_c_im, ssm_d, w_glu, b_glu,
           w_ssm_up, w_out, ln1_g, ln1_b, w_ffn_in, w_ffn_out, ln2_g, ln2_b):
    A = lambda a: np.asarray(a)
    x = A(x); c = A(c); w_ada = A(w_ada); b_ada = A(b_ada)
    B_, S_, D_ = x.shape
    xf = x.reshape(B_ * S_, D_)
    if "nc" not in _FUSED:
        _FUSED["nc"] = build_fused()
    minv, identb = consts_att()
    I2, cst, ident = consts_ssm()
    J = np.ascontiguousarray(ident[::-1])
    L = w_ada.shape[0]
    common = dict(
        w_ada=f32c(w_ada), w_in=f32c(A(w_in)),
        b_adaT=f32c(np.stack([b_ada[l][0:2048].reshape(16, 128).T for l in range(L)])),
        b_g1_bc=f32c(np.stack([bc128(b_ada[l][2048:3072]) for l in range(L)])),
        b_fT=f32c(np.stack([b_ada[l][3072:5120].reshape(16, 128).T for l in range(L)])),
        b_g2_bc=f32c(np.stack([bc128(b_ada[l][5120:6144]) for l in range(L)])),
        ident=ident, J=J, identb=identb, mneg=np.ascontiguousarray((minv * -30000.0).astype(NPBF16)), I2=I2, cst=cst,
        bgluT=f32c(np.stack([A(b_glu)[l].reshape(4, 128).T for l in range(L)])),
        w_glu=f32c(A(w_glu)), w_ssm_up=f32c(A(w_ssm_up)), w_sb_up=f32c(A(w_sb_up)), w_out=f32c(A(w_out)),
        ln1g_bc=f32c(np.stack([bc128(A(ln1_g)[l]) for l in range(L)])), ln1b_bc=f32c(np.stack([bc128(A(ln1_b)[l]) for l in range(L)])),
        w_ffn_in=f32c(A(w_ffn_in)), w_ffn_out=f32c(A(w_ffn_out)),
        ln2g_bc=f32c(np.stack([bc128(A(ln2_g)[l]) for l in range(L)])), ln2b_bc=f32c(np.stack([bc128(A(ln2_b)[l]) for l in range(L)])))
    are_, aim_, ldt_ = A(ssm_a_re), A(ssm_a_im), A(ssm_log_dt)
    bre_, bim_, cre_, cim_, d_ = A(ssm_b_re), A(ssm_b_im), A(ssm_c_re), A(ssm_c_im), A(ssm_d)
    maps = []
    for core in range(NCORES):
        b, hg = core // 2, core % 2
        gsl = slice(hg * 16, (hg + 1) * 16)
        m = dict(common)
        m["x"] = f32c(xf[core * TOK:(core + 1) * TOK])
        m["cT"] = f32c(c[b].reshape(8, 128).T)
        m["are"] = f32c(np.stack([np.concatenate([are_[l][gsl].T, are_[l][gsl].T], axis=0) for l in range(L)]))
        m["aim"] = f32c(np.stack([np.concatenate([aim_[l][gsl].T, aim_[l][gsl].T], axis=0) for l in range(L)]))
        m["ldt"] = f32c(np.stack([np.broadcast_to(ldt_[l][gsl][None, :], (128, 16)) for l in range(L)]))
        m["dTs"] = f32c(np.stack([np.tile(d_[l][hg * 256:(hg + 1) * 256].reshape(16, 16).T, (8, 1)) for l in range(L)]))
        m["bA"] = f32c(np.stack([np.concatenate([bre_[l][gsl], bim_[l][gsl]], axis=1) for l in range(L)]))
        m["bB"] = f32c(np.stack([np.concatenate([bim_[l][gsl], bre_[l][gsl]], axis=1) for l in range(L)]))
        m["cTs"] = f32c(np.stack([np.concatenate([cre_[l][gsl].transpose(0, 2, 1), cim_[l][gsl].transpose(0, 2, 1)], axis=1) for l in range(L)]))
        maps.append(m)
    res = run_prog(_FUSED["nc"], maps)
    out = np.concatenate([np.asarray(r["out"]) for r in res], axis=0).astype(np.float32)
    return out.reshape(B_, S_, D_)
```
